# Optimizing a Trainium2 kernel written in Bass

```python
import jax, jax.numpy as jnp
from jax import lax
import numpy as np

D_MODEL = 2048
BATCH = 16
SEQ = 2048
DEPTH = 1

GRID_W = 64
R_HEADS = 16
R_HEAD_DIM = 64
R_WIDTH = R_HEADS * R_HEAD_DIM
DECAY_LORA = 64
ICLR_LORA = 64
A_Q_HEADS = 16
A_KV_HEADS = 4
A_HEAD_DIM = 64
A_GROUP = A_Q_HEADS // A_KV_HEADS
A_WIDTH = A_Q_HEADS * A_HEAD_DIM
A_KV_WIDTH = A_KV_HEADS * A_HEAD_DIM
AXIS_DIM = A_HEAD_DIM // 2
Q_BLOCK = 128
ROPE_THETA = 10000.0
N_BRANCHES = 2
NORM_EPS = 1e-6
GN_EPS = 64e-5
SHIFT_SIZES = (R_WIDTH, R_WIDTH, R_WIDTH, 2 * DECAY_LORA, 2 * ICLR_LORA)
SHIFT_WIDTH = 3 * R_WIDTH + 2 * DECAY_LORA + 2 * ICLR_LORA
REST_SIZES = (R_WIDTH, A_WIDTH, A_KV_WIDTH, A_KV_WIDTH, A_WIDTH, N_BRANCHES * D_MODEL)
D_IN = SHIFT_WIDTH + 2 * R_WIDTH // 2 * 1 + 2 * A_WIDTH + 2 * A_KV_WIDTH + N_BRANCHES * D_MODEL - R_WIDTH + R_WIDTH // 1 * 0

kernel_name = "hybrid_rwkv7_axial_gqa_gated_block"


def _split(t, sizes):
    out, off = [], 0
    for s in sizes:
        out.append(t[..., off:off + s])
        off += s
    return out


def rms_norm(x, g, eps=NORM_EPS):
    xf = x.astype(jnp.float32)
    y = xf * lax.rsqrt(jnp.mean(xf * xf, axis=-1, keepdims=True) + eps)
    return (y * g.astype(jnp.float32)).astype(x.dtype)


def centred_shift(y, mu):
    prev = jnp.pad(y[:, :-1], ((0, 0), (1, 0), (0, 0)))
    nxt = jnp.pad(y[:, 1:], ((0, 0), (0, 1), (0, 0)))
    return y + mu[0] * (prev - y) + mu[1] * (nxt - y)


def _heads(t):
    return t.reshape(*t.shape[:-1], R_HEADS, R_HEAD_DIM)


def _dir_stack(t):
    s = jnp.stack([t[0], jnp.flip(t[1], axis=1)], axis=0)
    return jnp.moveaxis(s, 2, 0)


def _wkv_step(S, inp):
    w, kk, b, k, v, r = inp
    sa = jnp.einsum('dbhvk,dbhk->dbhv', S, -kk)
    S = S * w[..., None, :] + sa[..., None] * b[..., None, :] + v[..., :, None] * k[..., None, :]
    o = jnp.einsum('dbhvk,dbhk->dbhv', S, r)
    return S, o


def rwkv7_bidir(xr, xk, xv, wdown, adown, w0, w_up, a0, a_up, k_k, k_a, r_k, gn_w, gn_b):
    f32 = jnp.float32
    B, T, C = xr.shape
    wd = jnp.tanh(wdown.astype(f32).reshape(B, T, 2, DECAY_LORA))
    w_raw = w0.astype(f32)[:, None, None, :] + jnp.einsum('btdr,drc->dbtc', wd, w_up.astype(f32))
    decay = jnp.exp(-jnp.exp(-jax.nn.softplus(-w_raw) - 0.5))
    ad = adown.astype(f32).reshape(B, T, 2, ICLR_LORA)
    a = jax.nn.sigmoid(a0.astype(f32)[:, None, None, :] + jnp.einsum('btdr,drc->dbtc', ad, a_up.astype(f32)))
    r = xr.astype(f32)
    k = xk.astype(f32)
    v = xv.astype(f32)
    kkh = _heads(k * k_k.astype(f32))
    kkh = kkh * lax.rsqrt(jnp.maximum(jnp.sum(kkh * kkh, axis=-1, keepdims=True), 1e-24))
    ah = _heads(a)
    k_eff = _heads(k[None] * (1.0 + (a - 1.0) * k_a.astype(f32)))
    rh, vh = _heads(r), _heads(v)
    both = lambda t: jnp.stack([t, t], axis=0)
    seq_in = (_dir_stack(_heads(decay)), _dir_stack(both(kkh)), _dir_stack(kkh[None] * ah),
              _dir_stack(k_eff), _dir_stack(both(vh)), _dir_stack(both(rh)))
    S0 = jnp.zeros((2, B, R_HEADS, R_HEAD_DIM, R_HEAD_DIM), f32)
    _, outs = lax.scan(_wkv_step, S0, seq_in)
    outs = jnp.moveaxis(outs, 0, 2)
    wkv = outs[0] + jnp.flip(outs[1], axis=1)
    mu = jnp.mean(wkv, axis=-1, keepdims=True)
    var = jnp.mean(jnp.square(wkv - mu), axis=-1, keepdims=True)
    gn = (wkv - mu) * lax.rsqrt(var + GN_EPS)
    gn = gn * _heads(gn_w.astype(f32)) + _heads(gn_b.astype(f32))
    coef = jnp.sum(rh[None] * k_eff * _heads(r_k.astype(f32)), axis=-1, keepdims=True).sum(0)
    out = gn + coef * vh
    return out.reshape(B, T, C).astype(xr.dtype)


def axial_rope_tables(T):
    rows = T // GRID_W
    row = jnp.repeat(jnp.arange(rows, dtype=jnp.float32), GRID_W)
    col = jnp.tile(jnp.arange(GRID_W, dtype=jnp.float32), rows)
    freqs = ROPE_THETA ** (-jnp.arange(0, AXIS_DIM, 2, dtype=jnp.float32) / AXIS_DIM)
    ang = jnp.concatenate([row[:, None] * freqs, col[:, None] * freqs], axis=-1)
    return jnp.cos(ang), jnp.sin(ang)


def apply_rope(x, cos, sin):
    xf = x.astype(jnp.float32).reshape(*x.shape[:-1], A_HEAD_DIM // 2, 2)
    x0, x1 = xf[..., 0], xf[..., 1]
    c, s = cos[None, :, None, :], sin[None, :, None, :]
    out = jnp.stack([x0 * c - x1 * s, x0 * s + x1 * c], axis=-1)
    return out.reshape(x.shape).astype(x.dtype)


def axial_gqa(q, k, v, q_norm_g, k_norm_g):
    B, T, _ = q.shape
    q = rms_norm(q.reshape(B, T, A_Q_HEADS, A_HEAD_DIM), q_norm_g)
    k = rms_norm(k.reshape(B, T, A_KV_HEADS, A_HEAD_DIM), k_norm_g)
    v = v.reshape(B, T, A_KV_HEADS, A_HEAD_DIM)
    cos, sin = axial_rope_tables(T)
    q = apply_rope(q, cos, sin)
    k = apply_rope(k, cos, sin)
    scale = A_HEAD_DIM ** -0.5
    nb = T // Q_BLOCK
    qb = q.reshape(B, nb, Q_BLOCK, A_KV_HEADS, A_GROUP, A_HEAD_DIM).transpose(1, 0, 2, 3, 4, 5)

    def one_block(qblk):
        s = jnp.einsum('bqhgd,bkhd->bhgqk', qblk, k).astype(jnp.float32) * scale
        p = jax.nn.softmax(s, axis=-1)
        return jnp.einsum('bhgqk,bkhd->bqhgd', p.astype(v.dtype), v)

    o = lax.map(one_block, qb)
    return o.transpose(1, 0, 2, 3, 4, 5).reshape(B, T, A_WIDTH)


def hybrid_layer(x, norm_g, w_in, shift_mu, w0, w_up, a0, a_up, k_k, k_a, r_k, gn_w, gn_b,
                 q_norm_g, k_norm_g, w_branch_rwkv, w_branch_attn, w_out):
    h = rms_norm(x, norm_g)
    proj = jnp.einsum('btd,de->bte', h, w_in)
    shifted = centred_shift(proj[..., :SHIFT_WIDTH], shift_mu)
    xr, xk, xv, wdown, adown = _split(shifted, SHIFT_SIZES)
    z_r, q, k, v, z_a, gates = _split(proj[..., SHIFT_WIDTH:], REST_SIZES)
    o_r = rwkv7_bidir(xr, xk, xv, wdown, adown, w0, w_up, a0, a_up, k_k, k_a, r_k, gn_w, gn_b)
    o_r = o_r * jax.nn.silu(z_r)
    o_a = axial_gqa(q, k, v, q_norm_g, k_norm_g) * jax.nn.silu(z_a)
    p_r = jnp.einsum('btc,cd->btd', o_r, w_branch_rwkv)
    p_a = jnp.einsum('btc,cd->btd', o_a, w_branch_attn)
    g_r, g_a = _split(gates, (D_MODEL, D_MODEL))
    merged = jax.nn.sigmoid(g_r) * p_r + jax.nn.sigmoid(g_a) * p_a
    return x + jnp.einsum('btd,de->bte', merged, w_out)


def setup_inputs(seed: int = 0) -> dict:
    key = jax.random.key(seed)
    ks = jax.random.split(key, 20)
    f32 = jnp.float32
    nrm = lambda k, s, sc: jax.random.normal(k, s, f32) * sc
    d_in = SHIFT_WIDTH + sum(REST_SIZES)
    return {
        "x": nrm(ks[0], (BATCH, SEQ, D_MODEL), 1.0),
        "norm_g": 1.0 + nrm(ks[1], (DEPTH, D_MODEL), 0.02),
        "w_in": nrm(ks[2], (DEPTH, D_MODEL, d_in), D_MODEL ** -0.5),
        "shift_mu": jax.random.uniform(ks[3], (DEPTH, 2, SHIFT_WIDTH), f32, 0.0, 0.5),
        "w0": jax.random.uniform(ks[4], (DEPTH, 2, R_WIDTH), f32, -5.0, 1.0),
        "w_up": nrm(ks[5], (DEPTH, 2, DECAY_LORA, R_WIDTH), 0.1 * DECAY_LORA ** -0.5),
        "a0": nrm(ks[6], (DEPTH, 2, R_WIDTH), 0.3),
        "a_up": nrm(ks[7], (DEPTH, 2, ICLR_LORA, R_WIDTH), 0.1 * ICLR_LORA ** -0.5),
        "k_k": 0.85 + nrm(ks[8], (DEPTH, R_WIDTH), 0.02),
        "k_a": 1.0 + nrm(ks[9], (DEPTH, R_WIDTH), 0.02),
        "r_k": nrm(ks[10], (DEPTH, R_WIDTH), 0.1),
        "gn_w": 1.0 + nrm(ks[11], (DEPTH, R_WIDTH), 0.02),
        "gn_b": nrm(ks[12], (DEPTH, R_WIDTH), 0.02),
        "q_norm_g": 1.0 + nrm(ks[13], (DEPTH, A_HEAD_DIM), 0.02),
        "k_norm_g": 1.0 + nrm(ks[14], (DEPTH, A_HEAD_DIM), 0.02),
        "w_branch_rwkv": nrm(ks[15], (DEPTH, R_WIDTH, D_MODEL), R_WIDTH ** -0.5),
        "w_branch_attn": nrm(ks[16], (DEPTH, A_WIDTH, D_MODEL), A_WIDTH ** -0.5),
        "w_out": nrm(ks[17], (DEPTH, D_MODEL, D_MODEL), D_MODEL ** -0.5),
        "final_norm_g": 1.0 + nrm(ks[18], (D_MODEL,), 0.02),
    }


def reference(x, norm_g, w_in, shift_mu, w0, w_up, a0, a_up, k_k, k_a, r_k, gn_w, gn_b,
              q_norm_g, k_norm_g, w_branch_rwkv, w_branch_attn, w_out, final_norm_g):
    for l in range(DEPTH):
        x = hybrid_layer(x, norm_g[l], w_in[l], shift_mu[l], w0[l], w_up[l], a0[l], a_up[l],
                         k_k[l], k_a[l], r_k[l], gn_w[l], gn_b[l], q_norm_g[l], k_norm_g[l],
                         w_branch_rwkv[l], w_branch_attn[l], w_out[l])
    return rms_norm(x, final_norm_g)
```

```python
import math
from contextlib import ExitStack
import numpy as np
import ml_dtypes
import concourse.bass as bass
import concourse.mybir as mybir
from concourse.bass_utils import run_bass_kernel_spmd

F32 = mybir.dt.float32
BF16 = mybir.dt.bfloat16
ALU = mybir.AluOpType
AF = mybir.ActivationFunctionType
AX = mybir.AxisListType

NORM_EPS = 1e-6
GN_EPS = 64e-5
HD = 64
LORA = 64
CL = 64


class Cfg:
    def __init__(self, T=2048, D=2048, RH=16, QH=16, KVH=4, BPC=2, GRID_W=64, debug=False):
        self.T, self.D, self.RH, self.QH, self.KVH, self.BPC, self.GRID_W = T, D, RH, QH, KVH, BPC, GRID_W
        self.debug = debug
        self.RW = RH * HD
        self.AW = QH * HD
        self.KVW = KVH * HD
        self.GROUP = QH // KVH
        self.SHIFT_W = 3 * self.RW + 4 * LORA
        self.O_R, self.O_K, self.O_V = 0, self.RW, 2 * self.RW
        self.O_WD, self.O_AD = 3 * self.RW, 3 * self.RW + 2 * LORA
        self.O_ZR = self.SHIFT_W
        self.O_Q = self.O_ZR + self.RW
        self.O_AK = self.O_Q + self.AW
        self.O_AV = self.O_AK + self.KVW
        self.O_ZA = self.O_AV + self.KVW
        self.O_GR = self.O_ZA + self.AW
        self.O_GA = self.O_GR + D
        self.D_IN = self.O_GA + D
        self.KC = D // 128
        self.NCT = self.D_IN // 128
        self.TC = min(512, T)
        self.NTC = T // self.TC
        self.NT = T // 128
        self.NCH = T // CL


class Buf:
    __slots__ = ("name", "w", "r", "chan")

    def __init__(self, name):
        self.name, self.w, self.r, self.chan = name, None, {}, None


class Chan:
    __slots__ = ("sem", "cnt", "key")


class _Rec:
    def __getattr__(self, name):
        def f(*a, **k):
            self.call = (name, a, k)
            return self
        return f


class Prog:
    ENG = ("pe", "dve", "act", "pool", "sp")

    def __init__(self, nc, stack):
        self.nc, self.stack = nc, stack
        self.ops = {e: [] for e in self.ENG}
        self.sem = {e: stack.enter_context(nc.semaphore("s_" + e)) for e in self.ENG}
        self.cnt = {e: 0 for e in self.ENG}
        self.pending = {e: False for e in self.ENG}
        self.known = {e: {} for e in self.ENG}
        self.chans = {}
        self.chan_of = {}
        self.NCHAN = 24
        self.nbuf = 0

    def buf(self, name=None):
        self.nbuf += 1
        return Buf(name or "b%d" % self.nbuf)

    def _chan(self, b, dedicated=False):
        if b.chan is None and dedicated:
            c = Chan()
            c.key = "cd_" + b.name
            c.sem = self.stack.enter_context(self.nc.semaphore(c.key))
            c.cnt = 0
            self.chans[c.key] = c
            b.chan = c
        if b.chan is None:
            if b.name not in self.chan_of:
                idx = len(self.chan_of) % self.NCHAN
                key = "c_%d" % idx
                if key not in self.chans:
                    c = Chan()
                    c.key = key
                    c.sem = self.stack.enter_context(self.nc.semaphore(key))
                    c.cnt = 0
                    self.chans[key] = c
                self.chan_of[b.name] = key
            b.chan = self.chans[self.chan_of[b.name]]
        return b.chan

    def _waits(self, eng, reads, writes):
        need = {}

        def add(ev, raw):
            if ev is None:
                return
            key, val = ev
            if key == eng and eng == "pe":
                return
            if key in self.chans:
                val = self.chans[key].cnt * 16
            if need.get(key, 0) < val:
                need[key] = val

        for b in reads:
            add(b.w, True)
        for b in writes:
            add(b.w, False)
            for k, v in b.r.items():
                add((k, v), False)
        out = []
        kn = self.known[eng]
        for key, val in need.items():
            if kn.get(key, 0) >= val:
                continue
            kn[key] = val
            sem = self.chans[key].sem if key in self.chans else self.sem[key]
            out.append((sem, val))
        return out

    def _mark(self, ev, reads, writes):
        k, v = ev
        for b in reads:
            if b.r.get(k, 0) < v:
                b.r[k] = v
        for b in writes:
            b.w = ev
            b.r = {}

    def op(self, eng, fn, reads=(), writes=(), inc=True):
        rec = _Rec()
        fn(rec)
        name_, a_, k_ = rec.call
        fn = lambda e, name_=name_, a_=a_, k_=k_: getattr(e, name_)(*a_, **k_)
        waits = self._waits(eng, reads, writes)
        if inc:
            self.cnt[eng] += 1
            ev = (eng, self.cnt[eng])
            self.pending[eng] = False
            self.ops[eng].append((waits, fn, (self.sem[eng], 1)))
        else:
            ev = (eng, self.cnt[eng] + 1)
            self.pending[eng] = True
            self.ops[eng].append((waits, fn, None))
        self._mark(ev, reads, writes)

    def dma(self, q, out_ap, in_ap, reads, writes, chanbuf):
        waits = self._waits(q, reads, writes)
        c = self._chan(chanbuf, dedicated=(q == "pool"))
        c.cnt += 1
        ev = (c.key, c.cnt * 16)
        self.ops[q].append((waits, lambda e: e.dma_start(out=out_ap, in_=in_ap), (c.sem, 16)))
        self._mark(ev, reads, writes)

    def finish(self):
        for e in self.ENG:
            if self.pending[e]:
                raise RuntimeError("engine %s ends with a non-incrementing op" % e)
        waits = []
        for e in self.ENG:
            if e != "sp" and self.cnt[e] > 0:
                waits.append((self.sem[e], self.cnt[e]))
        for c in self.chans.values():
            if c.cnt:
                waits.append((c.sem, c.cnt * 16))
        self.ops["sp"].append((waits, None, None))

    def emit(self):
        nc = self.nc
        engmap = {"pe": "tensor", "dve": "vector", "act": "scalar", "pool": "gpsimd", "sp": "sync"}
        with nc.Block() as block:
            for e in self.ENG:
                ops = self.ops[e]

                def body(eng, ops=ops):
                    for waits, fn, inc in ops:
                        for sem, val in waits:
                            eng.wait_ge(sem, val)
                        if fn is None:
                            continue
                        try:
                            ins = fn(eng)
                        except Exception:
                            print("FAILED OP:", getattr(fn, "__defaults__", None))
                            raise
                        if inc is not None:
                            ins.then_inc(inc[0], inc[1])

                getattr(block, engmap[e])(body)


def rev_ap(ap):
    pat = [list(p) for p in ap.ap]
    step, n = pat[-1]
    pat[-1] = [-step, n]
    return bass.AP(tensor=ap.tensor, offset=ap.offset + step * (n - 1), ap=pat)


def build(cfg):
    c = cfg
    T, D, KC, TC, NTC, NT, RW = c.T, c.D, c.KC, c.TC, c.NTC, c.NT, c.RW
    nc = bass.Bass("TRN2", target_bir_lowering=False)
    st = ExitStack()
    P = Prog(nc, st)

    def din(name, shape, dt=F32):
        return nc.dram_tensor(name, list(shape), dt, kind="ExternalInput").ap()

    def dscr(name, shape, dt, dbg=False):
        kind = "ExternalOutput" if (dbg and c.debug) else "Internal"
        return nc.dram_tensor(name, list(shape), dt, kind=kind).ap()

    x_d = din("x", [c.BPC, T, D])
    w_in_d = din("w_in", [D, c.D_IN])
    w_br_d = din("w_br", [RW, D])
    w_ba_d = din("w_ba", [c.AW, D])
    w_out_d = din("w_out", [D, D])
    normg_d = din("norm_g", [D])
    fing_d = din("final_norm_g", [D])
    NST = c.SHIFT_W // 128
    NHP = RW // 128
    NPC = 2 * NST + 9 * NHP + 2
    pc_d = din("pc", [128, NPC])
    wup_d = din("w_up", [2 * LORA, RW])
    aup_d = din("a_up", [2 * LORA, RW])
    cmask_d = din("c_masks", [64, 3, 512])
    cident_d = din("c_ident", [128, 128])
    cj_d = din("c_j64", [64, 64])
    crot_d = din("c_rot", [128, 128])
    ccos_d = din("c_cos", [128, T])
    csin_d = din("c_sin", [128, T])
    cbd_d = din("c_bdones", [128, 128])
    cm0_d = din("c_m0", [128, T])
    y_d = nc.dram_tensor("y", [c.BPC, T, D], F32, kind="ExternalOutput").ap()

    wq_in = dscr("wq_in", [c.NCT, 128, KC, 128], BF16)
    wq_br = dscr("wq_br", [RW, D], BF16)
    wq_ba = dscr("wq_ba", [c.AW, D], BF16)
    wq_out = dscr("wq_out", [D, D], BF16)
    NPT = c.O_GR // 128
    proj_s = dscr("proj_s", [NPT, 128, T], F32)
    hT_s = dscr("hT_s", [c.BPC, D, T], BF16, dbg=True)
    orT_s = dscr("orT_s", [c.BPC, RW, T], BF16, dbg=True)
    oaT_s = dscr("oaT_s", [c.BPC, c.AW, T], BF16, dbg=True)

    def sb(name, shape, dt=F32):
        return st.enter_context(nc.sbuf_tensor(name, list(shape), dt))

    def ps(name, shape, dt=F32):
        return st.enter_context(nc.psum_tensor(name, list(shape), dt))

    B_wq = P.buf("wq")
    B_hTs = [P.buf("hTs%d" % b) for b in range(c.BPC)]
    B_orTs = [P.buf("orTs%d" % b) for b in range(c.BPC)]
    B_oaTs = [P.buf("oaTs%d" % b) for b in range(c.BPC)]
    B_proj = [P.buf("proj%d" % i) for i in range(NPT)]
    B_y = P.buf("y")

    ident_f = sb("ident_f", [128, 128]); B_identf = P.buf("identf")
    ident_b = sb("ident_b", [128, 128], BF16); B_identb = P.buf("identb")
    pc = sb("pc_sb", [128, NPC]); B_pc = P.buf("pc")
    ccT = sb("ccT", [128, NST]); B_cc = P.buf("cc")
    omka = sb("omka", [128, NHP]); B_omka = P.buf("omka")
    bdones = sb("bdones", [128, 128]); B_bd = P.buf("bd")
    rotm = sb("rotm", [128, 128]); B_rot = P.buf("rot")
    masks = sb("masks", [64, 3, 512]); B_mask = P.buf("mask")
    I8 = sb("I8", [64, 512]); B_I8 = P.buf("I8")
    j64 = sb("j64", [64, 64]); B_j64 = P.buf("j64")
    P.dma("sp", ident_f[:, :], cident_d[:, :], [], [B_identf], B_identf)
    P.dma("sp", pc[:, :], pc_d[:, :], [], [B_pc], B_pc)
    P.dma("sp", bdones[:, :], cbd_d[:, :], [], [B_bd], B_bd)
    P.dma("sp", rotm[:, :], crot_d[:, :], [], [B_rot], B_rot)
    P.dma("sp", masks[:, :, :], cmask_d[:, :, :], [], [B_mask], B_mask)
    P.dma("sp", j64[:, :], cj_d[:, :], [], [B_j64], B_j64)
    P.op("dve", lambda e: e.tensor_copy(out=ident_b[:, :], in_=ident_f[:, :]), [B_identf], [B_identb])
    for g8 in range(8):
        P.op("dve", lambda e, g8=g8: e.tensor_copy(out=I8[:, g8 * 64:(g8 + 1) * 64], in_=ident_f[0:64, 0:64]),
             [B_identf], [B_I8])
    pcv = pc[:, 0:2 * NST].rearrange("p (n j) -> p n j", j=2)
    P.op("dve", lambda e: e.tensor_tensor(out=ccT[:, :], in0=pcv[:, :, 0], in1=pcv[:, :, 1], op=ALU.add),
         [B_pc], [B_cc])
    P.op("dve", lambda e: e.tensor_scalar(out=ccT[:, :], in0=ccT[:, :], scalar1=-1.0, scalar2=1.0,
                                          op0=ALU.mult, op1=ALU.add), [B_cc], [B_cc])
    PCB = 2 * NST

    def pcc(hp, j):
        return pc[:, PCB + 9 * hp + j:PCB + 9 * hp + j + 1]

    for hp in range(NHP):
        P.op("dve", lambda e, hp=hp: e.tensor_scalar(out=omka[:, hp:hp + 1], in0=pcc(hp, 5), scalar1=-1.0,
                                                     scalar2=1.0, op0=ALU.mult, op1=ALU.add), [B_pc], [B_omka])

    KG = 4
    for ct in range(c.NCT):
        for k0 in range(0, KC, KG):
            k1 = min(KC, k0 + KG)
            P.dma("pool", wq_in[ct][:, k0:k1, :],
                  w_in_d[k0 * 128:k1 * 128, ct * 128:(ct + 1) * 128].rearrange("(kc kp) c -> kp kc c", kp=128),
                  [], [B_wq], B_wq)
    for dst_, src_, rows in ((wq_br, w_br_d, RW), (wq_ba, w_ba_d, c.AW), (wq_out, w_out_d, D)):
        for r0 in range(0, rows, 256):
            r1 = min(rows, r0 + 256)
            P.dma("pool", dst_[r0:r1, :], src_[r0:r1, :], [], [B_wq], B_wq)

    ARENA_BYTES = 196 * 1024
    arena_t = sb("arena", [128, ARENA_BYTES // 2], BF16)

    class Arena:
        def __init__(self):
            self.off, self.prev, self.live = 0, {}, []

        def reset(self):
            for b in self.live:
                evs = list(b.r.items())
                if b.w is not None:
                    evs.append(b.w)
                for k, v in evs:
                    if self.prev.get(k, 0) < v:
                        self.prev[k] = v
            self.live, self.off = [], 0

        def alloc(self, name, fshape, dt=F32, parts=128):
            n = 1
            for s in fshape:
                n *= s
            nbytes = n * (4 if dt == F32 else 2)
            nbytes = (nbytes + 63) // 64 * 64
            assert self.off + nbytes <= ARENA_BYTES, ("arena overflow", name, self.off, nbytes)
            v = arena_t[0:parts, self.off // 2:(self.off + n * (4 if dt == F32 else 2)) // 2]
            if dt == F32:
                v = v.bitcast(F32)
            if len(fshape) == 2:
                v = v.rearrange("p (a b) -> p a b", a=fshape[0])
            elif len(fshape) == 3:
                v = v.rearrange("p (a b c) -> p a b c", a=fshape[0], b=fshape[1])
            self.off += nbytes
            b = P.buf(name)
            b.r = dict(self.prev)
            self.live.append(b)
            return v, b

    AR = Arena()
    dbg_cnt = [0]

    def dbg(name, ap, bb, shape):
        if not c.debug:
            return
        t = nc.dram_tensor("dbg_" + name, list(shape), F32, kind="ExternalOutput").ap()
        P.dma("sp", t, ap, [bb], [P.buf("dbgd_" + name)], bb)

    psb = [ps("psb%d" % i, [128, 512]) for i in range(8)]
    B_ps = [P.buf("ps%d" % i) for i in range(8)]

    def phase_a_proj(b):
        AR.reset()
        hT, B_hT = AR.alloc("hT", [KC, T], BF16)
        gbc, B_gbc = AR.alloc("gbc", [D])
        P.dma("sp", gbc, normg_d.partition_broadcast(128), [], [B_gbc], B_gbc)
        xt, B_xt, xn, B_xn, st1, B_st1 = [], [], [], [], [], []
        for i in range(2):
            v, bb = AR.alloc("xt%d" % i, [D]); xt.append(v); B_xt.append(bb)
            v, bb = AR.alloc("xn%d" % i, [D], BF16); xn.append(v); B_xn.append(bb)
            v, bb = AR.alloc("st1_%d" % i, [4]); st1.append(v); B_st1.append(bb)
        junk, B_junk = AR.alloc("junk", [D], BF16)
        for it in range(NT):
            i = it % 2
            P.dma("sp", xt[i], x_d[b, it * 128:(it + 1) * 128, :], [], [B_xt[i]], B_xt[i])
            P.op("act", lambda e, i=i: e.activation(out=junk, in_=xt[i], func=AF.Square, accum_out=st1[i][:, 0:1]),
                 [B_xt[i]], [B_junk, B_st1[i]])
            P.op("act", lambda e, i=i: e.activation(out=st1[i][:, 1:2], in_=st1[i][:, 0:1], func=AF.Sqrt,
                                                     scale=1.0 / D, bias=NORM_EPS), [B_st1[i]], [B_st1[i]])
            P.op("dve", lambda e, i=i: e.reciprocal(out=st1[i][:, 2:3], in_=st1[i][:, 1:2]), [B_st1[i]], [B_st1[i]])
            P.op("dve", lambda e, i=i: e.scalar_tensor_tensor(out=xn[i], in0=xt[i], scalar=st1[i][:, 2:3], in1=gbc,
                                                               op0=ALU.mult, op1=ALU.mult),
                 [B_xt[i], B_st1[i], B_gbc], [B_xn[i]])
            for k0 in range(0, KC, 4):
                nk = min(4, KC - k0)
                pi = (k0 // 4) % 2
                pt = psb[pi][:, :].bitcast(BF16)
                for kk in range(nk):
                    P.op("pe", lambda e, i=i, kk=kk, k0=k0, pt=pt: e.transpose(
                        out=pt[:, kk * 128:(kk + 1) * 128], in_=xn[i][:, (k0 + kk) * 128:(k0 + kk + 1) * 128],
                        identity=ident_b[:, :]), [B_xn[i], B_identb], [B_ps[pi]], inc=(kk == nk - 1))
                src = pt[:, 0:nk * 128].rearrange("p (k t) -> p k t", k=nk)
                dst = hT[:, k0:k0 + nk, it * 128:(it + 1) * 128]
                if (k0 // 4) % 2 == 0:
                    P.op("act", lambda e, src=src, dst=dst: e.activation(out=dst, in_=src, func=AF.Copy),
                         [B_ps[pi]], [B_hT])
                else:
                    P.op("dve", lambda e, src=src, dst=dst: e.tensor_copy(out=dst, in_=src), [B_ps[pi]], [B_hT])
        for k0 in range(0, KC, KG):
            k1 = min(KC, k0 + KG)
            P.dma("sp", hT_s[b][k0 * 128:k1 * 128, :].rearrange("(kc kp) t -> kp kc t", kp=128), hT[:, k0:k1, :],
                  [B_hT], [B_hTs[b]], B_hT)

        wt, B_wt, pdst, B_pdst = [], [], [], []
        for i in range(2):
            v, bb = AR.alloc("wt%d" % i, [KC, 128], BF16); wt.append(v); B_wt.append(bb)
            v, bb = AR.alloc("pdst%d" % i, [T]); pdst.append(v); B_pdst.append(bb)
        ypad, B_ypad = AR.alloc("ypad", [T + 2])
        P.op("dve", lambda e: e.memset(ypad[:, 0:1], 0.0), [], [B_ypad])
        P.op("dve", lambda e: e.memset(ypad[:, T + 1:T + 2], 0.0), [], [B_ypad])
        ct_wd = c.O_WD // 128
        silu_tiles = set(range(c.O_ZR // 128, c.O_Q // 128)) | set(range(c.O_ZA // 128, c.O_GR // 128))
        for ct in range(NPT):
            i = ct % 2
            P.dma("sp", wt[i], wq_in[ct], [B_wq], [B_wt[i]], B_wt[i])
            shifted = ct < NST
            dst = pdst[i]
            for tc in range(NTC):
                pb = (ct * NTC + tc) % 2
                for kc in range(KC):
                    P.op("pe", lambda e, i=i, kc=kc, tc=tc, pb=pb: e.matmul(
                        psb[pb][:, 0:TC], lhsT=wt[i][:, kc, :], rhs=hT[:, kc, tc * TC:(tc + 1) * TC],
                        start=(kc == 0), stop=(kc == KC - 1)), [B_wt[i], B_hT], [B_ps[pb]], inc=(kc == KC - 1))
                if shifted:
                    P.op("act", lambda e, tc=tc, pb=pb: e.activation(out=ypad[:, 1 + tc * TC:1 + (tc + 1) * TC],
                                                                     in_=psb[pb][:, 0:TC], func=AF.Copy),
                         [B_ps[pb]], [B_ypad])
                else:
                    fn = AF.Silu if ct in silu_tiles else AF.Copy
                    P.op("act", lambda e, tc=tc, pb=pb, dst=dst, fn=fn: e.activation(
                        out=dst[:, tc * TC:(tc + 1) * TC], in_=psb[pb][:, 0:TC], func=fn), [B_ps[pb]], [B_pdst[i]])
            if shifted:
                P.op("dve", lambda e, ct=ct, dst=dst: e.tensor_scalar(out=dst, in0=ypad[:, 1:T + 1],
                                                                      scalar1=ccT[:, ct:ct + 1], scalar2=None,
                                                                      op0=ALU.mult), [B_ypad, B_cc], [B_pdst[i]])
                P.op("dve", lambda e, ct=ct, dst=dst: e.scalar_tensor_tensor(
                    out=dst, in0=ypad[:, 0:T], scalar=pc[:, 2 * ct:2 * ct + 1], in1=dst, op0=ALU.mult, op1=ALU.add),
                    [B_ypad, B_pc, B_pdst[i]], [B_pdst[i]])
                P.op("dve", lambda e, ct=ct, dst=dst: e.scalar_tensor_tensor(
                    out=dst, in0=ypad[:, 2:T + 2], scalar=pc[:, 2 * ct + 1:2 * ct + 2], in1=dst, op0=ALU.mult,
                    op1=ALU.add), [B_ypad, B_pc, B_pdst[i]], [B_pdst[i]])
                if ct == ct_wd:
                    P.op("act", lambda e, dst=dst: e.activation(out=dst, in_=dst, func=AF.Tanh),
                         [B_pdst[i]], [B_pdst[i]])
            P.dma("sp", proj_s[ct], dst, [B_pdst[i]], [B_proj[ct]], B_pdst[i])

    stop = getattr(c, "stop", None)
    NCH = c.NCH
    G = 4
    NG = NCH // G

    def rwkv_phase(b):
        AR.reset()
        A = AR.alloc
        wdT, B_wd = A("wdT", [T]); adT, B_ad = A("adT", [T])
        rT, B_r = A("rT", [T]); kT, B_k = A("kT", [T]); vT, B_v = A("vT", [T]); kkT, B_kk = A("kkT", [T])
        vrT, B_vr = A("vrT", [T])
        W_, B_W = A("W_", [T]); A_, B_A = A("A_", [T]); KE_, B_KE = A("KE_", [T]); G0_, B_G0 = A("G0_", [T])
        GM_, B_GM = A("GM_", [T + 1]); CS_, B_CS = A("CS_", [T])
        oT = []; B_oT = []
        for d in range(2):
            v, bb = A("oT%d" % d, [T]); oT.append(v); B_oT.append(bb)
        m0, B_m0 = A("m0", [T]); m1, B_m1 = A("m1", [T])
        wup, B_wup = A("wup", [RW]); aup, B_aup = A("aup", [RW])
        gL, B_gL = A("gL", [NCH])
        S_, B_S = A("S_", [64]); Sg_, B_Sg = A("Sg_", [64]); Sbd, B_Sbd = A("Sbd", [128])
        BTt, B_BTt, KTt, B_KTt, Vt, B_Vt, CTm, B_CTm, DTm, B_DTm, Nm, B_Nm, BVs, B_BVs = ([] for _ in range(14))
        for i in range(2):
            for lst, bl, nm, shp in ((BTt, B_BTt, "BTt", [G, 128]), (KTt, B_KTt, "KTt", [G, 128]),
                                     (Vt, B_Vt, "Vt", [G, 128]), (CTm, B_CTm, "CTm", [512]),
                                     (DTm, B_DTm, "DTm", [512]), (Nm, B_Nm, "Nm", [512]), (BVs, B_BVs, "BVs", [512])):
                v, bb = A("%s%d" % (nm, i), shp, F32, 64); lst.append(v); bl.append(bb)
        ATm, B_ATm = A("ATm", [512], F32, 64); Am, B_Am = A("Am", [512], F32, 64)
        BTm, B_BTm = A("BTm", [512], F32, 64)
        Pq, B_Pq, PTq, B_PTq = [], [], [], []
        for i in range(2):
            v, bb = A("Pq%d" % i, [512], F32, 64); Pq.append(v); B_Pq.append(bb)
            v, bb = A("PTq%d" % i, [512], F32, 64); PTq.append(v); B_PTq.append(bb)
        RHSs, B_RHS = A("RHSs", [128], F32, 64); Us, B_Us = A("Us", [128], F32, 64)
        ob, B_ob = A("ob", [T], BF16)

        P.dma("sp", m0, cm0_d[:, :], [], [B_m0], B_m0)
        P.op("dve", lambda e: e.tensor_scalar(out=m1, in0=m0, scalar1=-1.0, scalar2=1.0, op0=ALU.mult, op1=ALU.add),
             [B_m0], [B_m1])
        P.dma("sp", wup, wup_d[:, :], [], [B_wup], B_wup)
        P.dma("sp", aup, aup_d[:, :], [], [B_aup], B_aup)
        P.dma("sp", wdT, proj_s[c.O_WD // 128], [B_proj[c.O_WD // 128]], [B_wd], B_wd)
        P.dma("sp", adT, proj_s[c.O_AD // 128], [B_proj[c.O_AD // 128]], [B_ad], B_ad)
        P.op("dve", lambda e: e.memset(GM_[:, 0:1], 1.0), [], [B_GM])

        def dv(fn, reads, writes):
            P.op("dve", fn, reads, writes)

        for hp in range(NHP):
            hs = [slice(0, 64), slice(64, 128)]
            for buf, bb, off in ((rT, B_r, c.O_R), (kT, B_k, c.O_K), (vT, B_v, c.O_V)):
                ctl = off // 128 + hp
                P.dma("sp", buf, proj_s[ctl], [B_proj[ctl]], [bb], bb)
            dv(lambda e, hp=hp: e.tensor_scalar(out=kkT, in0=kT, scalar1=pcc(hp, 4), scalar2=None, op0=ALU.mult),
               [B_k, B_pc], [B_kk])
            dv(lambda e: e.tensor_tensor(out=W_, in0=kkT, in1=kkT, op=ALU.mult), [B_kk], [B_W])
            for tc in range(NTC):
                sl = slice(tc * TC, (tc + 1) * TC)
                P.op("pe", lambda e, sl=sl: e.matmul(psb[0][:, 0:TC], lhsT=bdones[:, :], rhs=W_[:, sl], start=True,
                                                     stop=True), [B_bd, B_W], [B_ps[0]])
                dv(lambda e, sl=sl: e.tensor_scalar_max(out=A_[:, sl], in0=psb[0][:, 0:TC], scalar1=1e-24),
                   [B_ps[0]], [B_A])
            P.op("act", lambda e: e.activation(out=A_, in_=A_, func=AF.Sqrt), [B_A], [B_A])
            dv(lambda e: e.reciprocal(out=A_, in_=A_), [B_A], [B_A])
            dv(lambda e: e.tensor_tensor(out=kkT, in0=kkT, in1=A_, op=ALU.mult), [B_kk, B_A], [B_kk])
            dv(lambda e: e.tensor_copy(out=vrT, in_=rev_ap(vT)), [B_v], [B_vr])

            for d in range(2):
                dsl = slice(d * 64, (d + 1) * 64)
                rsrc = (lambda ap: ap) if d == 0 else rev_ap
                vsrc = vT if d == 0 else vrT
                B_vsrc = B_v if d == 0 else B_vr
                for tc in range(NTC):
                    sl = slice(tc * TC, (tc + 1) * TC)
                    osl = sl if d == 0 else slice(T - (tc + 1) * TC, T - tc * TC)
                    P.op("pe", lambda e, sl=sl, hp=hp, dsl=dsl: e.matmul(
                        psb[0][:, 0:TC], lhsT=wup[dsl, hp * 128:(hp + 1) * 128], rhs=wdT[dsl, sl], start=True,
                        stop=True), [B_wup, B_wd], [B_ps[0]])
                    P.op("act", lambda e, osl=osl, hp=hp, d=d: e.activation(
                        out=W_[:, osl], in_=rsrc(psb[0][:, 0:TC]), func=AF.Sigmoid, bias=pcc(hp, 0 + d), scale=1.0),
                        [B_ps[0], B_pc], [B_W])
                    P.op("pe", lambda e, sl=sl, hp=hp, dsl=dsl: e.matmul(
                        psb[1][:, 0:TC], lhsT=aup[dsl, hp * 128:(hp + 1) * 128], rhs=adT[dsl, sl], start=True,
                        stop=True), [B_aup, B_ad], [B_ps[1]])
                    P.op("act", lambda e, osl=osl, hp=hp, d=d: e.activation(
                        out=A_[:, osl], in_=rsrc(psb[1][:, 0:TC]), func=AF.Sigmoid, bias=pcc(hp, 2 + d), scale=1.0),
                        [B_ps[1], B_pc], [B_A])
                P.op("act", lambda e: e.activation(out=W_, in_=W_, func=AF.Exp, scale=-math.exp(-0.5)), [B_W], [B_W])
                if hp == 0 and b == 0:
                    dbg("w%d" % d, W_, B_W, [128, T]); dbg("a%d" % d, A_, B_A, [128, T])
                dv(lambda e, hp=hp: e.tensor_scalar(out=KE_, in0=A_, scalar1=pcc(hp, 5), scalar2=omka[:, hp:hp + 1],
                                                    op0=ALU.mult, op1=ALU.add), [B_A, B_pc, B_omka], [B_KE])
                dv(lambda e: e.tensor_tensor(out=KE_, in0=KE_, in1=rsrc(kT), op=ALU.mult), [B_KE, B_k], [B_KE])
                dv(lambda e, hp=hp: e.scalar_tensor_tensor(out=G0_, in0=KE_, scalar=pcc(hp, 6), in1=rsrc(rT),
                                                           op0=ALU.mult, op1=ALU.mult), [B_KE, B_pc, B_r], [B_G0])
                if d == 0:
                    dv(lambda e: e.tensor_copy(out=CS_, in_=G0_), [B_G0], [B_CS])
                else:
                    dv(lambda e: e.tensor_tensor(out=CS_, in0=CS_, in1=rev_ap(G0_), op=ALU.add), [B_CS, B_G0], [B_CS])
                dv(lambda e: e.tensor_tensor(out=A_, in0=A_, in1=rsrc(kkT), op=ALU.mult), [B_A, B_kk], [B_A])
                dv(lambda e: e.tensor_tensor(out=G0_, in0=W_, in1=m0, op=ALU.mult), [B_W, B_m0], [B_G0])
                dv(lambda e: e.tensor_tensor(out=W_, in0=W_, in1=G0_, op=ALU.subtract), [B_W, B_G0], [B_W])
                dv(lambda e: e.tensor_tensor_scan(out=GM_[:, 1:T + 1], data0=G0_, data1=W_, initial=0.0,
                                                  op0=ALU.mult, op1=ALU.add), [B_G0, B_W], [B_GM])
                dv(lambda e: e.tensor_tensor(out=G0_, in0=GM_[:, 0:T], in1=m0, op=ALU.mult), [B_GM, B_m0], [B_G0])
                dv(lambda e: e.tensor_tensor(out=G0_, in0=G0_, in1=m1, op=ALU.add), [B_G0, B_m1], [B_G0])
                dv(lambda e: e.scalar_tensor_tensor(out=G0_, in0=rsrc(kkT), scalar=-1.0, in1=G0_, op0=ALU.mult,
                                                    op1=ALU.mult), [B_kk, B_G0], [B_G0])
                dv(lambda e: e.tensor_tensor(out=W_, in0=rsrc(rT), in1=GM_[:, 1:T + 1], op=ALU.mult),
                   [B_r, B_GM], [B_W])
                dv(lambda e: e.tensor_copy(out=gL, in_=GM_[:, 1:T + 1].rearrange("p (n l) -> p n l", l=CL)[:, :, CL - 1]),
                   [B_GM], [B_gL])
                dv(lambda e: e.reciprocal(out=GM_[:, 1:T + 1], in_=GM_[:, 1:T + 1]), [B_GM], [B_GM])
                dv(lambda e: e.tensor_tensor(out=A_, in0=A_, in1=GM_[:, 1:T + 1], op=ALU.mult), [B_A, B_GM], [B_A])
                dv(lambda e: e.tensor_tensor(out=KE_, in0=KE_, in1=GM_[:, 1:T + 1], op=ALU.mult), [B_KE, B_GM], [B_KE])
                at, bt, kt, rt = G0_, A_, KE_, W_
                B_at, B_bt, B_kt, B_rt = B_G0, B_A, B_KE, B_W
                dv(lambda e: e.memset(S_, 0.0), [], [B_S])
                dv(lambda e: e.memset(Sbd, 0.0), [], [B_Sbd])

                def precompute(g, s):
                    t0 = g * G * CL
                    for src, B_src, dstl, B_dstl, pb in ((bt, B_bt, BTt, B_BTt, 3), (kt, B_kt, KTt, B_KTt, 4),
                                                         (vsrc, B_vsrc, Vt, B_Vt, 3)):
                        for j in range(G):
                            P.op("pe", lambda e, src=src, j=j, pb=pb: e.matmul(
                                psb[pb][0:64, j * 128:(j + 1) * 128], lhsT=src[:, t0 + j * CL:t0 + (j + 1) * CL],
                                rhs=ident_f[:, :], start=True, stop=True), [B_src, B_identf], [B_ps[pb]],
                                inc=(j == G - 1))
                        P.op("act", lambda e, dstl=dstl, pb=pb: e.activation(
                            out=dstl[s], in_=psb[pb][0:64, :].rearrange("p (g c) -> p g c", g=G), func=AF.Copy),
                            [B_ps[pb]], [B_dstl[s]])
                        yield
                    if stop == "P1":
                        return
                    specs = ((0, bt, B_bt, at, B_at), (1, at, B_at, bt, B_bt), (2, kt, B_kt, at, B_at),
                             (3, bt, B_bt, rt, B_rt), (4, kt, B_kt, rt, B_rt))
                    for h in range(2):
                        for pb, L, B_L, R, B_R in specs:
                            for j in range(G):
                                tsl = slice(t0 + j * CL, t0 + (j + 1) * CL)
                                col = (j * 2 + h) * 64
                                P.op("pe", lambda e, pb=pb, L=L, R=R, tsl=tsl, h=h, col=col: e.matmul(
                                    psb[pb][0:64, col:col + 64], lhsT=L[hs[h], tsl], rhs=R[hs[h], tsl], start=True,
                                    stop=True), [B_L, B_R], [B_ps[pb]], inc=(j == G - 1 and h == 1))
                    yield
                    dv(lambda e: e.tensor_tensor(out=ATm, in0=psb[0][0:64, :], in1=masks[:, 0, :], op=ALU.mult),
                       [B_ps[0], B_mask], [B_ATm])
                    dv(lambda e: e.tensor_tensor(out=Am, in0=psb[1][0:64, :], in1=masks[:, 1, :], op=ALU.mult),
                       [B_ps[1], B_mask], [B_Am])
                    dv(lambda e: e.tensor_tensor(out=BTm, in0=psb[2][0:64, :], in1=masks[:, 0, :], op=ALU.mult),
                       [B_ps[2], B_mask], [B_BTm])
                    dv(lambda e: e.tensor_tensor(out=CTm[s], in0=psb[3][0:64, :], in1=masks[:, 2, :], op=ALU.mult),
                       [B_ps[3], B_mask], [B_CTm[s]])
                    dv(lambda e: e.tensor_tensor(out=DTm[s], in0=psb[4][0:64, :], in1=masks[:, 2, :], op=ALU.mult),
                       [B_ps[4], B_mask], [B_DTm[s]])
                    yield
                    dv(lambda e: e.tensor_tensor(out=Nm[s], in0=ATm, in1=I8[:, :], op=ALU.add), [B_ATm, B_I8], [B_Nm[s]])
                    if stop == "P2":
                        return
                    for j in range(G):
                        for h in range(2):
                            col = (j * 2 + h) * 64
                            P.op("pe", lambda e, j=j, h=h, col=col: e.matmul(
                                psb[2][0:64, col:col + 64], lhsT=BTm[:, col:col + 64],
                                rhs=Vt[s][:, j, h * 64:(h + 1) * 64], start=True, stop=True),
                                [B_BTm, B_Vt[s]], [B_ps[2]], inc=(j == G - 1 and h == 1))
                    P.op("act", lambda e: e.activation(out=BVs[s], in_=psb[2][0:64, :], func=AF.Copy),
                         [B_ps[2]], [B_BVs[s]])
                    yield
                    if stop == "P3":
                        return
                    Pc, B_Pc, PTc, B_PTc = Am, B_Am, ATm, B_ATm
                    for lev in range(5):
                        q = lev % 2
                        last = lev == 4
                        for p8 in range(2 * G):
                            col = p8 * 64
                            P.op("pe", lambda e, Pc=Pc, PTc=PTc, col=col: e.matmul(
                                psb[0][0:64, col:col + 64], lhsT=PTc[:, col:col + 64], rhs=Pc[:, col:col + 64],
                                start=True, stop=True), [B_Pc, B_PTc], [B_ps[0]], inc=(p8 == 2 * G - 1))
                        if not last:
                            for p8 in range(2 * G):
                                col = p8 * 64
                                P.op("pe", lambda e, Pc=Pc, PTc=PTc, col=col: e.matmul(
                                    psb[1][0:64, col:col + 64], lhsT=Pc[:, col:col + 64], rhs=PTc[:, col:col + 64],
                                    start=True, stop=True), [B_Pc, B_PTc], [B_ps[1]], inc=(p8 == 2 * G - 1))
                        yield
                        dv(lambda e, q=q: e.tensor_copy(out=Pq[q], in_=psb[0][0:64, :]), [B_ps[0]], [B_Pq[q]])
                        if not last:
                            P.op("act", lambda e, q=q: e.activation(out=PTq[q], in_=psb[1][0:64, :], func=AF.Copy),
                                 [B_ps[1]], [B_PTq[q]])
                        for p8 in range(2 * G):
                            col = p8 * 64
                            P.op("pe", lambda e, q=q, col=col: e.matmul(
                                psb[4][0:64, col:col + 64], lhsT=Pq[q][:, col:col + 64], rhs=Nm[s][:, col:col + 64],
                                start=True, stop=True), [B_Pq[q], B_Nm[s]], [B_ps[4]], inc=(p8 == 2 * G - 1))
                        yield
                        dv(lambda e: e.tensor_tensor(out=Nm[s], in0=Nm[s], in1=psb[4][0:64, :], op=ALU.add),
                           [B_Nm[s], B_ps[4]], [B_Nm[s]])
                        Pc, B_Pc, PTc, B_PTc = Pq[q], B_Pq[q], PTq[q], B_PTq[q]

                def chain(g, s):
                    for j in range(G):
                        ch = g * G + j
                        tsl = slice(ch * CL, (ch + 1) * CL)
                        P.op("pe", lambda e, tsl=tsl: e.matmul(
                            psb[5][0:64, 0:128], lhsT=at[:, tsl], rhs=Sbd, start=True, stop=True),
                            [B_at, B_Sbd], [B_ps[5]])
                        yield
                        dv(lambda e, j=j: e.tensor_tensor(out=RHSs, in0=psb[5][0:64, 0:128],
                                                          in1=BVs[s][:, j * 128:(j + 1) * 128], op=ALU.add),
                           [B_ps[5], B_BVs[s]], [B_RHS])
                        for h in range(2):
                            col = (j * 2 + h) * 64
                            P.op("pe", lambda e, h=h, col=col: e.matmul(
                                psb[5][0:64, 128 + h * 64:128 + (h + 1) * 64], lhsT=Nm[s][:, col:col + 64],
                                rhs=RHSs[:, h * 64:(h + 1) * 64], start=True, stop=True),
                                [B_Nm[s], B_RHS], [B_ps[5]], inc=(h == 1))
                        yield
                        P.op("act", lambda e: e.activation(out=Us, in_=psb[5][0:64, 128:256], func=AF.Copy),
                             [B_ps[5]], [B_Us])
                        dv(lambda e, ch=ch: e.tensor_scalar(out=Sg_, in0=S_, scalar1=gL[:, ch:ch + 1], scalar2=None,
                                                            op0=ALU.mult), [B_S, B_gL], [B_Sg])
                        for h in range(2):
                            hc = slice(h * 64, (h + 1) * 64)
                            P.op("pe", lambda e, h=h, hc=hc, j=j: e.matmul(
                                psb[6][hs[h], 0:64], lhsT=KTt[s][:, j, hc], rhs=Vt[s][:, j, hc], start=True,
                                stop=False, tile_position=(0, h * 64)), [B_KTt[s], B_Vt[s]], [B_ps[6]], inc=False)
                            P.op("pe", lambda e, h=h, hc=hc, j=j: e.matmul(
                                psb[6][hs[h], 0:64], lhsT=BTt[s][:, j, hc], rhs=Us[:, hc], start=False, stop=True,
                                tile_position=(0, h * 64)), [B_BTt[s], B_Us], [B_ps[6]], inc=(h == 1))
                        oc = (ch % 8) * 64
                        for h in range(2):
                            hc = slice(h * 64, (h + 1) * 64)
                            col = (j * 2 + h) * 64
                            P.op("pe", lambda e, h=h, hc=hc, tsl=tsl, oc=oc: e.matmul(
                                psb[7][hs[h], oc:oc + 64], lhsT=Sbd[:, hc], rhs=rt[:, tsl], start=True,
                                stop=False, tile_position=(0, h * 64)), [B_Sbd, B_rt], [B_ps[7]], inc=False)
                            P.op("pe", lambda e, h=h, hc=hc, col=col, oc=oc: e.matmul(
                                psb[7][hs[h], oc:oc + 64], lhsT=Us[:, hc], rhs=CTm[s][:, col:col + 64], start=False,
                                stop=False, tile_position=(0, h * 64)), [B_Us, B_CTm[s]], [B_ps[7]], inc=False)
                            P.op("pe", lambda e, h=h, hc=hc, col=col, oc=oc, j=j: e.matmul(
                                psb[7][hs[h], oc:oc + 64], lhsT=Vt[s][:, j, hc], rhs=DTm[s][:, col:col + 64],
                                start=False, stop=True, tile_position=(0, h * 64)), [B_Vt[s], B_DTm[s]], [B_ps[7]],
                                inc=(h == 1))
                        yield
                        dv(lambda e, ch=ch: e.scalar_tensor_tensor(out=S_, in0=psb[6][:, 0:64], scalar=gL[:, ch:ch + 1],
                                                                   in1=Sg_, op0=ALU.mult, op1=ALU.add),
                           [B_ps[6], B_gL, B_Sg], [B_S])
                        for h in range(2):
                            P.op("act", lambda e, h=h: e.activation(out=Sbd[hs[h], h * 64:(h + 1) * 64], in_=S_[hs[h], :],
                                                                    func=AF.Copy), [B_S], [B_Sbd])
                        if ch % 8 == 7 or ch == NCH - 1:
                            nch8 = ch % 8 + 1
                            c0 = (ch - nch8 + 1) * CL
                            dst = oT[d][:, c0:c0 + nch8 * CL]
                            if d == 1:
                                dst = rev_ap(oT[d][:, T - c0 - nch8 * CL:T - c0])
                            P.op("act", lambda e, dst=dst, nch8=nch8: e.activation(
                                out=dst, in_=psb[7][:, 0:nch8 * CL], func=AF.Copy), [B_ps[7]], [B_oT[d]])

                if hp == 0 and b == 0 and d == 0:
                    dbg("at", at, B_at, [128, T]); dbg("bt", bt, B_bt, [128, T]); dbg("kt", kt, B_kt, [128, T])
                    dbg("rt", rt, B_rt, [128, T]); dbg("v", vT, B_v, [128, T]); dbg("gL", gL, B_gL, [128, NCH])
                if stop == "R1":
                    break
                for _ in precompute(0, 0):
                    pass
                for g in range(NG):
                    if stop in ("R2", "P1", "P2", "P3"):
                        break
                    ch_it = chain(g, g % 2)
                    pre_it = precompute(g + 1, (g + 1) % 2) if g + 1 < NG else iter(())
                    ch_done = pre_done = False
                    while not (ch_done and pre_done):
                        if not ch_done:
                            try:
                                next(ch_it)
                            except StopIteration:
                                ch_done = True
                        for _ in range(2):
                            if not pre_done:
                                try:
                                    next(pre_it)
                                except StopIteration:
                                    pre_done = True

            if stop in ("R1", "R2", "P1", "P2", "P3"):
                break
            if hp == 0 and b == 0:
                dbg("o0", oT[0], B_oT[0], [128, T]); dbg("o1", oT[1], B_oT[1], [128, T])
                dbg("kk", kkT, B_kk, [128, T]); dbg("r", rT, B_r, [128, T]); dbg("cs", CS_, B_CS, [128, T])
            zr, B_zr = W_, B_W
            ctl = c.O_ZR // 128 + hp
            P.dma("sp", zr, proj_s[ctl], [B_proj[ctl]], [B_zr], B_zr)
            wkv, B_wkv = oT[0], B_oT[0]
            dv(lambda e: e.tensor_tensor(out=wkv, in0=oT[0], in1=oT[1], op=ALU.add), [B_oT[0], B_oT[1]], [B_wkv])
            cen, B_cen = A_, B_A
            for tc in range(NTC):
                sl = slice(tc * TC, (tc + 1) * TC)
                P.op("pe", lambda e, sl=sl: e.matmul(psb[0][:, 0:TC], lhsT=bdones[:, :], rhs=wkv[:, sl], start=True,
                                                     stop=True), [B_bd, B_wkv], [B_ps[0]])
                dv(lambda e, sl=sl: e.scalar_tensor_tensor(out=cen[:, sl], in0=psb[0][:, 0:TC], scalar=-1.0 / HD,
                                                           in1=wkv[:, sl], op0=ALU.mult, op1=ALU.add),
                   [B_ps[0], B_wkv], [B_cen])
            sq, B_sq = KE_, B_KE
            dv(lambda e: e.tensor_tensor(out=sq, in0=cen, in1=cen, op=ALU.mult), [B_cen], [B_sq])
            rs, B_rs = G0_, B_G0
            for tc in range(NTC):
                sl = slice(tc * TC, (tc + 1) * TC)
                P.op("pe", lambda e, sl=sl: e.matmul(psb[1][:, 0:TC], lhsT=bdones[:, :], rhs=sq[:, sl], start=True,
                                                     stop=True), [B_bd, B_sq], [B_ps[1]])
                P.op("act", lambda e, sl=sl: e.activation(out=rs[:, sl], in_=psb[1][:, 0:TC], func=AF.Sqrt,
                                                           scale=1.0 / HD, bias=GN_EPS), [B_ps[1]], [B_rs])
            dv(lambda e: e.reciprocal(out=rs, in_=rs), [B_rs], [B_rs])
            dv(lambda e: e.tensor_tensor(out=cen, in0=cen, in1=rs, op=ALU.mult), [B_cen, B_rs], [B_cen])
            dv(lambda e, hp=hp: e.tensor_scalar(out=cen, in0=cen, scalar1=pcc(hp, 7), scalar2=pcc(hp, 8),
                                                op0=ALU.mult, op1=ALU.add), [B_cen, B_pc], [B_cen])
            for tc in range(NTC):
                sl = slice(tc * TC, (tc + 1) * TC)
                P.op("pe", lambda e, sl=sl: e.matmul(psb[0][:, 0:TC], lhsT=bdones[:, :], rhs=CS_[:, sl], start=True,
                                                     stop=True), [B_bd, B_CS], [B_ps[0]])
                dv(lambda e, sl=sl: e.tensor_tensor(out=sq[:, sl], in0=psb[0][:, 0:TC], in1=vT[:, sl], op=ALU.mult),
                   [B_ps[0], B_v], [B_sq])
            dv(lambda e: e.tensor_tensor(out=cen, in0=cen, in1=sq, op=ALU.add), [B_cen, B_sq], [B_cen])
            dv(lambda e: e.tensor_tensor(out=ob, in0=cen, in1=zr, op=ALU.mult), [B_cen, B_zr], [B_ob])
            P.dma("sp", orT_s[b][hp * 128:(hp + 1) * 128, :], ob, [B_ob], [B_orTs[b]], B_ob)

    NKP = c.KVW // 128
    NQP = c.AW // 128
    SCALE = float(HD) ** -0.5

    def attn_phase(b):
        AR.reset()
        A = AR.alloc
        cosT, B_cos = A("cosT", [T]); sinT, B_sin = A("sinT", [T])
        P.dma("sp", cosT, ccos_d[:, :], [], [B_cos], B_cos)
        P.dma("sp", sinT, csin_d[:, :], [], [B_sin], B_sin)
        src, B_src = A("asrc", [T]); t1, B_t1 = A("at1", [T]); t2, B_t2 = A("at2", [T])
        za, B_za = A("za", [T])
        nb, B_nb = A("anb", [T], BF16)
        og, B_og = A("og", [T], BF16)
        qz, B_qz = [], []
        for p in range(2):
            v, bb = A("qz%d" % p, [T], BF16); qz.append(v); B_qz.append(bb)
        kd, B_kd = [], []
        for kvh in range(c.KVH):
            v, bb = A("kd%d" % kvh, [T], BF16); kd.append(v); B_kd.append(bb)
        Va, B_Va = [], []
        for kvh in range(c.KVH):
            row, brow = [], []
            for p in range(2):
                v, bb = A("Va%d_%d" % (kvh, p), [NT, 128], BF16); row.append(v); brow.append(bb)
            Va.append(row); B_Va.append(brow)
        pT, B_pT = [], []
        for i in range(3):
            v, bb = A("pT%d" % i, [TC], BF16); pT.append(v); B_pT.append(bb)
        rc, B_rc = A("rc", [TC]); on, B_on = A("on", [TC])

        def dv(fn, reads, writes):
            P.op("dve", fn, reads, writes)

        def qk_prep(ct, gcol):
            P.dma("sp", src, proj_s[ct], [B_proj[ct]], [B_src], B_src)
            dv(lambda e: e.tensor_tensor(out=t1, in0=src, in1=src, op=ALU.mult), [B_src], [B_t1])
            for tc in range(NTC):
                sl = slice(tc * TC, (tc + 1) * TC)
                P.op("pe", lambda e, sl=sl: e.matmul(psb[5][:, 0:TC], lhsT=bdones[:, :], rhs=t1[:, sl], start=True,
                                                     stop=True), [B_bd, B_t1], [B_ps[5]])
                P.op("act", lambda e, sl=sl: e.activation(out=t2[:, sl], in_=psb[5][:, 0:TC], func=AF.Sqrt,
                                                           scale=1.0 / HD, bias=NORM_EPS), [B_ps[5]], [B_t2])
            dv(lambda e: e.reciprocal(out=t2, in_=t2), [B_t2], [B_t2])
            dv(lambda e: e.scalar_tensor_tensor(out=t2, in0=src, scalar=pc[:, gcol:gcol + 1], in1=t2, op0=ALU.mult,
                                                op1=ALU.mult), [B_src, B_pc, B_t2], [B_t2])
            for tc in range(NTC):
                sl = slice(tc * TC, (tc + 1) * TC)
                P.op("pe", lambda e, sl=sl: e.matmul(psb[6][:, 0:TC], lhsT=rotm[:, :], rhs=t2[:, sl], start=True,
                                                     stop=True), [B_rot, B_t2], [B_ps[6]])
                dv(lambda e, sl=sl: e.tensor_tensor(out=t1[:, sl], in0=psb[6][:, 0:TC], in1=sinT[:, sl], op=ALU.mult),
                   [B_ps[6], B_sin], [B_t1])
            dv(lambda e: e.tensor_tensor(out=t2, in0=t2, in1=cosT, op=ALU.mult), [B_t2, B_cos], [B_t2])
            dv(lambda e: e.tensor_tensor(out=t2, in0=t2, in1=t1, op=ALU.add), [B_t2, B_t1], [B_t2])
            dv(lambda e: e.tensor_copy(out=nb, in_=t2), [B_t2], [B_nb])

        GQ = PCB + 9 * NHP
        for kp in range(NKP):
            qk_prep(c.O_AK // 128 + kp, GQ + 1)
            for h2 in range(2):
                kvh = kp * 2 + h2
                hsl = slice(h2 * 64, (h2 + 1) * 64)
                osl = slice((1 - h2) * 64, (2 - h2) * 64)
                dv(lambda e, kvh=kvh, hsl=hsl: e.tensor_copy(out=kd[kvh][hsl, :], in_=t2[hsl, :]), [B_t2], [B_kd[kvh]])
                dv(lambda e, hsl=hsl, osl=osl: e.tensor_copy(out=t1[osl, :], in_=t2[hsl, :]), [B_t2], [B_t1])
                dv(lambda e, kvh=kvh, osl=osl: e.tensor_copy(out=kd[kvh][osl, :], in_=t1[osl, :]), [B_t1], [B_kd[kvh]])
        if stop == "T1":
            return
        for kvh in range(c.KVH):
            for p in range(2):
                dv(lambda e, kvh=kvh, p=p: e.memset(Va[kvh][p].rearrange("p a b -> p (a b)"), 1.0), [], [B_Va[kvh][p]])
        if stop == "V1":
            return
        for kp in range(NKP):
            ct = c.O_AV // 128 + kp
            P.dma("sp", src, proj_s[ct], [B_proj[ct]], [B_src], B_src)
            if stop == "V2":
                return
            for it in range(NT):
                P.op("pe", lambda e, it=it: e.matmul(psb[7][:, 0:128], lhsT=src[:, it * 128:(it + 1) * 128],
                                                     rhs=ident_f[:, :], start=True, stop=True),
                     [B_src, B_identf], [B_ps[7]])
                if stop == "V3":
                    continue
                for h2 in range(2):
                    kvh = kp * 2 + h2
                    cs_ = slice(h2 * 64, (h2 + 1) * 64)
                    dv(lambda e, kvh=kvh, it=it, cs_=cs_: e.tensor_copy(out=Va[kvh][0][:, it, 0:64], in_=psb[7][:, cs_]),
                       [B_ps[7]], [B_Va[kvh][0]])
                    dv(lambda e, kvh=kvh, it=it, cs_=cs_: e.tensor_copy(out=Va[kvh][1][:, it, 64:128], in_=psb[7][:, cs_]),
                       [B_ps[7]], [B_Va[kvh][1]])
        if stop == "T2":
            return
        cnt = 0
        for qp in range(NQP):
            qk_prep(c.O_Q // 128 + qp, GQ)
            ctz = c.O_ZA // 128 + qp
            P.dma("sp", za, proj_s[ctz], [B_proj[ctz]], [B_za], B_za)
            for p in range(2):
                dv(lambda e, p=p: e.memset(qz[p], 0.0), [], [B_qz[p]])
                dv(lambda e, p=p: e.tensor_copy(out=qz[p][p * 64:(p + 1) * 64, :], in_=nb[p * 64:(p + 1) * 64, :]),
                   [B_nb], [B_qz[p]])
            for p in range(2):
                qh = qp * 2 + p
                kvh = qh // c.GROUP
                qsl = slice(p * 64, (p + 1) * 64)
                ssl = slice((1 - p) * 64, (2 - p) * 64)
                for qc in range(NTC):
                    qs = slice(qc * TC, (qc + 1) * TC)
                    acc = 3 + (cnt % 2)
                    for kt in range(NT):
                        sb_ = cnt_s = (cnt * NT + kt) % 3
                        P.op("pe", lambda e, sb_=sb_, kt=kt, qs=qs, kvh=kvh, p=p: e.matmul(
                            psb[sb_][:, 0:TC], lhsT=kd[kvh][:, kt * 128:(kt + 1) * 128], rhs=qz[p][:, qs], start=True,
                            stop=True), [B_kd[kvh], B_qz[p]], [B_ps[sb_]])
                        P.op("act", lambda e, sb_=sb_: e.activation(out=pT[sb_], in_=psb[sb_][:, 0:TC], func=AF.Exp,
                                                                    scale=SCALE), [B_ps[sb_]], [B_pT[sb_]])
                        P.op("pe", lambda e, sb_=sb_, kt=kt, acc=acc, kvh=kvh: e.matmul(
                            psb[acc][:, 0:TC], lhsT=Va[kvh][p][:, kt, :], rhs=pT[sb_], start=(kt == 0),
                            stop=(kt == NT - 1)), [B_Va[kvh][p], B_pT[sb_]], [B_ps[acc]], inc=(kt == NT - 1))
                    dv(lambda e, acc=acc: e.reciprocal(out=rc[ssl, :], in_=psb[acc][ssl, 0:TC]), [B_ps[acc]], [B_rc])
                    dv(lambda e: e.tensor_copy(out=rc[qsl, :], in_=rc[ssl, :]), [B_rc], [B_rc])
                    dv(lambda e, acc=acc: e.tensor_tensor(out=on[qsl, :], in0=psb[acc][qsl, 0:TC], in1=rc[qsl, :],
                                                          op=ALU.mult), [B_ps[acc], B_rc], [B_on])
                    dv(lambda e, qs=qs: e.tensor_tensor(out=og[qsl, qs], in0=on[qsl, :], in1=za[qsl, qs], op=ALU.mult),
                       [B_on, B_za], [B_og])
                    cnt += 1
            P.dma("sp", oaT_s[b][qp * 128:(qp + 1) * 128, :], og, [B_og], [B_oaTs[b]], B_og)

    CG = min(512, D)
    NCG = D // CG
    KR = RW // 128
    KA = c.AW // 128

    def phase_d_all():
        AR.reset()
        A = AR.alloc
        wout, B_wout = A("wout", [KC, D], BF16)
        fbc, B_fbc = A("fbc", [D])
        for k0 in range(0, KC, KG):
            k1 = min(KC, k0 + KG)
            P.dma("sp", wout[:, k0:k1, :], wq_out[k0 * 128:k1 * 128, :].rearrange("(kc kp) n -> kp kc n", kp=128),
                  [B_wq], [B_wout], B_wout)
        P.dma("sp", fbc, fing_d.partition_broadcast(128), [], [B_fbc], B_fbc)
        hTc, B_hTc = A("hTc", [KC, TC], BF16)
        orc, B_orc = A("orc", [KR, TC], BF16); oac, B_oac = A("oac", [KA, TC], BF16)
        mg, B_mg = A("mg", [KC, TC], BF16)
        wgr, B_wgr, wga, B_wga, wbr, B_wbr, wba, B_wba = ([] for _ in range(8))
        for i in range(2):
            v, bb = A("wgr%d" % i, [KC, 128], BF16); wgr.append(v); B_wgr.append(bb)
            v, bb = A("wga%d" % i, [KC, 128], BF16); wga.append(v); B_wga.append(bb)
            v, bb = A("wbr%d" % i, [KR, 128], BF16); wbr.append(v); B_wbr.append(bb)
            v, bb = A("wba%d" % i, [KA, 128], BF16); wba.append(v); B_wba.append(bb)
        s1, B_s1 = A("s1", [TC]); s2, B_s2 = A("s2", [TC]); u1, B_u1 = A("u1", [TC]); u2, B_u2 = A("u2", [TC])
        xt, B_xt = A("dxt", [D]); yp, B_yp = A("yp", [D]); junk, B_junk = A("djunk", [D], BF16)
        st2, B_st2 = A("st2", [4])

        def dv(fn, reads, writes):
            P.op("dve", fn, reads, writes)

        for b in range(c.BPC):
            for tcn in range(NTC):
                ts = slice(tcn * TC, (tcn + 1) * TC)
                for dst_, src_, nk, B_s, B_d in ((hTc, hT_s, KC, B_hTs[b], B_hTc), (orc, orT_s, KR, B_orTs[b], B_orc),
                                                 (oac, oaT_s, KA, B_oaTs[b], B_oac)):
                    for k0 in range(0, nk, KG):
                        k1 = min(nk, k0 + KG)
                        P.dma("sp", dst_[:, k0:k1, :],
                              src_[b][k0 * 128:k1 * 128, ts].rearrange("(kc kp) t -> kp kc t", kp=128),
                              [B_s], [B_d], B_d)
                for j in range(KC):
                    i = j % 2
                    P.dma("sp", wgr[i], wq_in[c.O_GR // 128 + j], [B_wq], [B_wgr[i]], B_wgr[i])
                    P.dma("sp", wga[i], wq_in[c.O_GA // 128 + j], [B_wq], [B_wga[i]], B_wga[i])
                    P.dma("sp", wbr[i], wq_br[:, j * 128:(j + 1) * 128].rearrange("(kc kp) n -> kp kc n", kp=128),
                          [B_wq], [B_wbr[i]], B_wbr[i])
                    P.dma("sp", wba[i], wq_ba[:, j * 128:(j + 1) * 128].rearrange("(kc kp) n -> kp kc n", kp=128),
                          [B_wq], [B_wba[i]], B_wba[i])
                    pb0 = 4 * i
                    for kc in range(KC):
                        P.op("pe", lambda e, kc=kc, i=i, pb0=pb0: e.matmul(
                            psb[pb0][:, 0:TC], lhsT=wgr[i][:, kc, :], rhs=hTc[:, kc, :], start=(kc == 0),
                            stop=(kc == KC - 1)), [B_wgr[i], B_hTc], [B_ps[pb0]], inc=(kc == KC - 1))
                    for kc in range(KC):
                        P.op("pe", lambda e, kc=kc, i=i, pb0=pb0: e.matmul(
                            psb[pb0 + 1][:, 0:TC], lhsT=wga[i][:, kc, :], rhs=hTc[:, kc, :], start=(kc == 0),
                            stop=(kc == KC - 1)), [B_wga[i], B_hTc], [B_ps[pb0 + 1]], inc=(kc == KC - 1))
                    for kc in range(KR):
                        P.op("pe", lambda e, kc=kc, i=i, pb0=pb0: e.matmul(
                            psb[pb0 + 2][:, 0:TC], lhsT=wbr[i][:, kc, :], rhs=orc[:, kc, :], start=(kc == 0),
                            stop=(kc == KR - 1)), [B_wbr[i], B_orc], [B_ps[pb0 + 2]], inc=(kc == KR - 1))
                    for kc in range(KA):
                        P.op("pe", lambda e, kc=kc, i=i, pb0=pb0: e.matmul(
                            psb[pb0 + 3][:, 0:TC], lhsT=wba[i][:, kc, :], rhs=oac[:, kc, :], start=(kc == 0),
                            stop=(kc == KA - 1)), [B_wba[i], B_oac], [B_ps[pb0 + 3]], inc=(kc == KA - 1))
                    P.op("act", lambda e, pb0=pb0: e.activation(out=s1, in_=psb[pb0][:, 0:TC], func=AF.Sigmoid),
                         [B_ps[pb0]], [B_s1])
                    P.op("act", lambda e, pb0=pb0: e.activation(out=s2, in_=psb[pb0 + 1][:, 0:TC], func=AF.Sigmoid),
                         [B_ps[pb0 + 1]], [B_s2])
                    dv(lambda e, pb0=pb0: e.tensor_tensor(out=u1, in0=psb[pb0 + 2][:, 0:TC], in1=s1, op=ALU.mult),
                       [B_ps[pb0 + 2], B_s1], [B_u1])
                    dv(lambda e, pb0=pb0: e.tensor_tensor(out=u2, in0=psb[pb0 + 3][:, 0:TC], in1=s2, op=ALU.mult),
                       [B_ps[pb0 + 3], B_s2], [B_u2])
                    dv(lambda e, j=j: e.tensor_tensor(out=mg[:, j, :], in0=u1, in1=u2, op=ALU.add), [B_u1, B_u2], [B_mg])
                for it in range(TC // 128):
                    t0 = tcn * TC + it * 128
                    P.dma("sp", xt, x_d[b, t0:t0 + 128, :], [], [B_xt], B_xt)
                    for cg in range(NCG):
                        pb = (it * NCG + cg) % 2
                        for kc in range(KC):
                            P.op("pe", lambda e, kc=kc, it=it, cg=cg, pb=pb: e.matmul(
                                psb[pb][:, 0:CG], lhsT=mg[:, kc, it * 128:(it + 1) * 128],
                                rhs=wout[:, kc, cg * CG:(cg + 1) * CG], start=(kc == 0), stop=(kc == KC - 1)),
                                [B_mg, B_wout], [B_ps[pb]], inc=(kc == KC - 1))
                        dv(lambda e, cg=cg, pb=pb: e.tensor_tensor(out=yp[:, cg * CG:(cg + 1) * CG], in0=psb[pb][:, 0:CG],
                                                                   in1=xt[:, cg * CG:(cg + 1) * CG], op=ALU.add),
                           [B_ps[pb], B_xt], [B_yp])
                    P.op("act", lambda e: e.activation(out=junk, in_=yp, func=AF.Square, accum_out=st2[:, 0:1]),
                         [B_yp], [B_junk, B_st2])
                    P.op("act", lambda e: e.activation(out=st2[:, 1:2], in_=st2[:, 0:1], func=AF.Sqrt, scale=1.0 / D,
                                                       bias=NORM_EPS), [B_st2], [B_st2])
                    dv(lambda e: e.reciprocal(out=st2[:, 2:3], in_=st2[:, 1:2]), [B_st2], [B_st2])
                    dv(lambda e: e.scalar_tensor_tensor(out=yp, in0=yp, scalar=st2[:, 2:3], in1=fbc, op0=ALU.mult,
                                                        op1=ALU.mult), [B_yp, B_st2, B_fbc], [B_yp])
                    P.dma("sp", y_d[b, t0:t0 + 128, :], yp, [B_yp], [B_y], B_yp)

    stop = getattr(c, "stop", None)
    for b in range(c.BPC):
        phase_a_proj(b)
        if stop == "A":
            continue
        rwkv_phase(b)
        if stop in ("R", "R1", "R2", "P1", "P2", "P3"):
            continue
        attn_phase(b)
    if stop is None or stop == "D":
        phase_d_all()

    P.finish()
    P.emit()
    return nc


def host_consts(cfg):
    c = cfg
    T = c.T
    ident = np.eye(128, dtype=np.float32)
    j64 = np.eye(64, dtype=np.float32)[::-1].copy()
    rot = np.zeros((128, 128), np.float32)
    for i in range(64):
        rot[2 * i + 1, 2 * i] = -1.0
        rot[2 * i, 2 * i + 1] = 1.0
    rows = T // c.GRID_W
    row = np.repeat(np.arange(rows, dtype=np.float32), c.GRID_W)
    col = np.tile(np.arange(c.GRID_W, dtype=np.float32), rows)
    axis_dim = HD // 2
    freqs = (10000.0 ** (-np.arange(0, axis_dim, 2, dtype=np.float32) / axis_dim)).astype(np.float32)
    ang = np.concatenate([row[:, None] * freqs, col[:, None] * freqs], axis=-1).astype(np.float32)
    cosT = np.repeat(np.cos(ang).T, 2, axis=0)
    sinT = np.repeat(np.sin(ang).T, 2, axis=0)
    cos2 = np.concatenate([cosT, cosT], 0).astype(np.float32)
    sin2 = np.concatenate([sinT, sinT], 0).astype(np.float32)
    bd = np.zeros((128, 128), np.float32)
    bd[:64, :64] = 1.0
    bd[64:, 64:] = 1.0
    m0 = np.ones((128, T), np.float32)
    m0[:, ::CL] = 0.0
    i64 = np.arange(64)
    strict_st = (i64[None, :] > i64[:, None]).astype(np.float32)
    strict_ts = (i64[None, :] < i64[:, None]).astype(np.float32)
    incl_st = (i64[None, :] >= i64[:, None]).astype(np.float32)
    masks = np.stack([np.tile(m, (1, 8)) for m in (strict_st, strict_ts, incl_st)], axis=1)
    return dict(c_ident=ident, c_j64=j64, c_rot=rot, c_cos=cos2, c_sin=sin2, c_bdones=bd, c_m0=m0,
                c_masks=np.ascontiguousarray(masks.astype(np.float32)))


def make_in_maps(cfg, inputs, n_cores):
    c = cfg
    consts = host_consts(c)
    f = lambda a: np.ascontiguousarray(np.asarray(a, dtype=np.float32))
    NST, NHP = c.SHIFT_W // 128, c.RW // 128
    smu = f(inputs["shift_mu"][0])
    cols = [smu.reshape(2, NST, 128).transpose(2, 1, 0).reshape(128, 2 * NST)]
    per = [f(inputs["w0"][0])[0], f(inputs["w0"][0])[1], f(inputs["a0"][0])[0], f(inputs["a0"][0])[1],
           f(inputs["k_k"][0]), f(inputs["k_a"][0]), f(inputs["r_k"][0]), f(inputs["gn_w"][0]), f(inputs["gn_b"][0])]
    per = np.stack([p.reshape(NHP, 128) for p in per], axis=-1)
    cols.append(per.transpose(1, 0, 2).reshape(128, 9 * NHP))
    qg = np.tile(f(inputs["q_norm_g"][0]), 2)[:, None]
    kg = np.tile(f(inputs["k_norm_g"][0]), 2)[:, None]
    pc = np.ascontiguousarray(np.concatenate(cols + [qg, kg], axis=1).astype(np.float32))
    shared = dict(
        w_in=f(inputs["w_in"][0]), w_br=f(inputs["w_branch_rwkv"][0]), w_ba=f(inputs["w_branch_attn"][0]),
        w_out=f(inputs["w_out"][0]), norm_g=f(inputs["norm_g"][0]), final_norm_g=f(inputs["final_norm_g"]),
        pc=pc, w_up=f(inputs["w_up"][0]).reshape(2 * LORA, c.RW), a_up=f(inputs["a_up"][0]).reshape(2 * LORA, c.RW),
        **consts)
    x = np.asarray(inputs["x"], dtype=np.float32)
    maps = []
    for i in range(n_cores):
        m = dict(shared)
        m["x"] = np.ascontiguousarray(x[i * c.BPC:(i + 1) * c.BPC])
        maps.append(m)
    return maps


def kernel(**inputs):
    cfg = Cfg()
    n = 8
    nc = build(cfg)
    in_maps = make_in_maps(cfg, inputs, n)
    res = run_bass_kernel_spmd(nc, in_maps, core_ids=list(range(n)))
    return np.concatenate([r["y"] for r in res.results], axis=0)
```

```python
import math
from contextlib import ExitStack
import numpy as np
import ml_dtypes
import concourse.bass as bass
import concourse.mybir as mybir
from concourse.bass_utils import run_bass_kernel_spmd

F32 = mybir.dt.float32
BF16 = mybir.dt.bfloat16
ALU = mybir.AluOpType
AF = mybir.ActivationFunctionType
AX = mybir.AxisListType

NORM_EPS = 1e-6
GN_EPS = 64e-5
HD = 64
LORA = 64
CL = 64


class Cfg:
    def __init__(self, T=2048, D=2048, RH=16, QH=16, KVH=4, BPC=2, GRID_W=64, debug=False):
        self.T, self.D, self.RH, self.QH, self.KVH, self.BPC, self.GRID_W = T, D, RH, QH, KVH, BPC, GRID_W
        self.debug = debug
        self.RW = RH * HD
        self.AW = QH * HD
        self.KVW = KVH * HD
        self.GROUP = QH // KVH
        self.SHIFT_W = 3 * self.RW + 4 * LORA
        self.O_R, self.O_K, self.O_V = 0, self.RW, 2 * self.RW
        self.O_WD, self.O_AD = 3 * self.RW, 3 * self.RW + 2 * LORA
        self.O_ZR = self.SHIFT_W
        self.O_Q = self.O_ZR + self.RW
        self.O_AK = self.O_Q + self.AW
        self.O_AV = self.O_AK + self.KVW
        self.O_ZA = self.O_AV + self.KVW
        self.O_GR = self.O_ZA + self.AW
        self.O_GA = self.O_GR + D
        self.D_IN = self.O_GA + D
        self.KC = D // 128
        self.NCT = self.D_IN // 128
        self.TC = min(512, T)
        self.NTC = T // self.TC
        self.NT = T // 128
        self.NCH = T // CL


class Buf:
    __slots__ = ("name", "w", "r", "chan")

    def __init__(self, name):
        self.name, self.w, self.r, self.chan = name, None, {}, None


class Chan:
    __slots__ = ("sem", "cnt", "key")


class _Rec:
    def __getattr__(self, name):
        def f(*a, **k):
            self.call = (name, a, k)
            return self
        return f


class Prog:
    ENG = ("pe", "dve", "act", "pool", "sp")

    def __init__(self, nc, stack):
        self.nc, self.stack = nc, stack
        self.ops = {e: [] for e in self.ENG}
        self.sem = {e: stack.enter_context(nc.semaphore("s_" + e)) for e in self.ENG}
        self.cnt = {e: 0 for e in self.ENG}
        self.pending = {e: False for e in self.ENG}
        self.known = {e: {} for e in self.ENG}
        self.chans = {}
        self.chan_of = {}
        self.NCHAN = 24
        self.nbuf = 0

    def buf(self, name=None):
        self.nbuf += 1
        return Buf(name or "b%d" % self.nbuf)

    def _chan(self, b, dedicated=False):
        if b.chan is None and dedicated:
            c = Chan()
            c.key = "cd_" + b.name
            c.sem = self.stack.enter_context(self.nc.semaphore(c.key))
            c.cnt = 0
            self.chans[c.key] = c
            b.chan = c
        if b.chan is None:
            if b.name not in self.chan_of:
                idx = len(self.chan_of) % self.NCHAN
                key = "c_%d" % idx
                if key not in self.chans:
                    c = Chan()
                    c.key = key
                    c.sem = self.stack.enter_context(self.nc.semaphore(key))
                    c.cnt = 0
                    self.chans[key] = c
                self.chan_of[b.name] = key
            b.chan = self.chans[self.chan_of[b.name]]
        return b.chan

    def _waits(self, eng, reads, writes):
        need = {}

        def add(ev, raw):
            if ev is None:
                return
            key, val = ev
            if key == eng and eng == "pe":
                return
            if key in self.chans:
                val = self.chans[key].cnt * 16
            if need.get(key, 0) < val:
                need[key] = val

        for b in reads:
            add(b.w, True)
        for b in writes:
            add(b.w, False)
            for k, v in b.r.items():
                add((k, v), False)
        out = []
        kn = self.known[eng]
        for key, val in need.items():
            if kn.get(key, 0) >= val:
                continue
            kn[key] = val
            sem = self.chans[key].sem if key in self.chans else self.sem[key]
            out.append((sem, val))
        return out

    def _mark(self, ev, reads, writes):
        k, v = ev
        for b in reads:
            if b.r.get(k, 0) < v:
                b.r[k] = v
        for b in writes:
            b.w = ev
            b.r = {}

    def op(self, eng, fn, reads=(), writes=(), inc=True):
        rec = _Rec()
        fn(rec)
        name_, a_, k_ = rec.call
        fn = lambda e, name_=name_, a_=a_, k_=k_: getattr(e, name_)(*a_, **k_)
        waits = self._waits(eng, reads, writes)
        if inc:
            self.cnt[eng] += 1
            ev = (eng, self.cnt[eng])
            self.pending[eng] = False
            self.ops[eng].append((waits, fn, (self.sem[eng], 1)))
        else:
            ev = (eng, self.cnt[eng] + 1)
            self.pending[eng] = True
            self.ops[eng].append((waits, fn, None))
        self._mark(ev, reads, writes)

    def dma(self, q, out_ap, in_ap, reads, writes, chanbuf):
        waits = self._waits(q, reads, writes)
        c = self._chan(chanbuf, dedicated=(q == "pool"))
        c.cnt += 1
        ev = (c.key, c.cnt * 16)
        self.ops[q].append((waits, lambda e: e.dma_start(out=out_ap, in_=in_ap), (c.sem, 16)))
        self._mark(ev, reads, writes)

    def finish(self):
        for e in self.ENG:
            if self.pending[e]:
                raise RuntimeError("engine %s ends with a non-incrementing op" % e)
        waits = []
        for e in self.ENG:
            if e != "sp" and self.cnt[e] > 0:
                waits.append((self.sem[e], self.cnt[e]))
        for c in self.chans.values():
            if c.cnt:
                waits.append((c.sem, c.cnt * 16))
        self.ops["sp"].append((waits, None, None))

    def emit(self):
        nc = self.nc
        engmap = {"pe": "tensor", "dve": "vector", "act": "scalar", "pool": "gpsimd", "sp": "sync"}
        with nc.Block() as block:
            for e in self.ENG:
                ops = self.ops[e]

                def body(eng, ops=ops):
                    for waits, fn, inc in ops:
                        for sem, val in waits:
                            eng.wait_ge(sem, val)
                        if fn is None:
                            continue
                        try:
                            ins = fn(eng)
                        except Exception:
                            print("FAILED OP:", getattr(fn, "__defaults__", None))
                            raise
                        if inc is not None:
                            ins.then_inc(inc[0], inc[1])

                getattr(block, engmap[e])(body)


def rev_ap(ap):
    pat = [list(p) for p in ap.ap]
    step, n = pat[-1]
    pat[-1] = [-step, n]
    return bass.AP(tensor=ap.tensor, offset=ap.offset + step * (n - 1), ap=pat)


def build(cfg):
    c = cfg
    T, D, KC, TC, NTC, NT, RW = c.T, c.D, c.KC, c.TC, c.NTC, c.NT, c.RW
    nc = bass.Bass("TRN2", target_bir_lowering=False)
    st = ExitStack()
    P = Prog(nc, st)

    def din(name, shape, dt=F32):
        return nc.dram_tensor(name, list(shape), dt, kind="ExternalInput").ap()

    def dscr(name, shape, dt, dbg=False):
        kind = "ExternalOutput" if (dbg and c.debug) else "Internal"
        return nc.dram_tensor(name, list(shape), dt, kind=kind).ap()

    x_d = din("x", [c.BPC, T, D])
    w_in_d = din("w_in", [D, c.D_IN])
    w_br_d = din("w_br", [RW, D])
    w_ba_d = din("w_ba", [c.AW, D])
    w_out_d = din("w_out", [D, D])
    normg_d = din("norm_g", [D])
    fing_d = din("final_norm_g", [D])
    NST = c.SHIFT_W // 128
    NHP = RW // 128
    NPC = 2 * NST + 9 * NHP + 2
    pc_d = din("pc", [128, NPC])
    wup_d = din("w_up", [2 * LORA, RW])
    aup_d = din("a_up", [2 * LORA, RW])
    cmask_d = din("c_masks", [64, 3, 512])
    cident_d = din("c_ident", [128, 128])
    cj_d = din("c_j64", [64, 64])
    crot_d = din("c_rot", [128, 128])
    ccos_d = din("c_cos", [128, T])
    csin_d = din("c_sin", [128, T])
    cbd_d = din("c_bdones", [128, 128])
    cm0_d = din("c_m0", [128, T])
    y_d = nc.dram_tensor("y", [c.BPC, T, D], F32, kind="ExternalOutput").ap()

    wq_in = dscr("wq_in", [c.NCT, 128, KC, 128], BF16)
    wq_br = dscr("wq_br", [RW, D], BF16)
    wq_ba = dscr("wq_ba", [c.AW, D], BF16)
    wq_out = dscr("wq_out", [D, D], BF16)
    NPT = c.O_GR // 128
    proj_s = dscr("proj_s", [NPT, 128, T], F32)
    hT_s = dscr("hT_s", [c.BPC, D, T], BF16, dbg=True)
    orT_s = dscr("orT_s", [c.BPC, RW, T], BF16, dbg=True)
    oaT_s = dscr("oaT_s", [c.BPC, c.AW, T], BF16, dbg=True)

    def sb(name, shape, dt=F32):
        return st.enter_context(nc.sbuf_tensor(name, list(shape), dt))

    def ps(name, shape, dt=F32):
        return st.enter_context(nc.psum_tensor(name, list(shape), dt))

    B_wq = P.buf("wq")
    B_hTs = [P.buf("hTs%d" % b) for b in range(c.BPC)]
    B_orTs = [P.buf("orTs%d" % b) for b in range(c.BPC)]
    B_oaTs = [P.buf("oaTs%d" % b) for b in range(c.BPC)]
    B_proj = [P.buf("proj%d" % i) for i in range(NPT)]
    B_y = P.buf("y")

    ident_f = sb("ident_f", [128, 128]); B_identf = P.buf("identf")
    ident_b = sb("ident_b", [128, 128], BF16); B_identb = P.buf("identb")
    pc = sb("pc_sb", [128, NPC]); B_pc = P.buf("pc")
    ccT = sb("ccT", [128, NST]); B_cc = P.buf("cc")
    omka = sb("omka", [128, NHP]); B_omka = P.buf("omka")
    bdones = sb("bdones", [128, 128]); B_bd = P.buf("bd")
    rotm = sb("rotm", [128, 128]); B_rot = P.buf("rot")
    masks = sb("masks", [64, 3, 512]); B_mask = P.buf("mask")
    I8 = sb("I8", [64, 512]); B_I8 = P.buf("I8")
    j64 = sb("j64", [64, 64]); B_j64 = P.buf("j64")
    P.dma("sp", ident_f[:, :], cident_d[:, :], [], [B_identf], B_identf)
    P.dma("sp", pc[:, :], pc_d[:, :], [], [B_pc], B_pc)
    P.dma("sp", bdones[:, :], cbd_d[:, :], [], [B_bd], B_bd)
    P.dma("sp", rotm[:, :], crot_d[:, :], [], [B_rot], B_rot)
    P.dma("sp", masks[:, :, :], cmask_d[:, :, :], [], [B_mask], B_mask)
    P.dma("sp", j64[:, :], cj_d[:, :], [], [B_j64], B_j64)
    P.op("dve", lambda e: e.tensor_copy(out=ident_b[:, :], in_=ident_f[:, :]), [B_identf], [B_identb])
    for g8 in range(8):
        P.op("dve", lambda e, g8=g8: e.tensor_copy(out=I8[:, g8 * 64:(g8 + 1) * 64], in_=ident_f[0:64, 0:64]),
             [B_identf], [B_I8])
    pcv = pc[:, 0:2 * NST].rearrange("p (n j) -> p n j", j=2)
    P.op("dve", lambda e: e.tensor_tensor(out=ccT[:, :], in0=pcv[:, :, 0], in1=pcv[:, :, 1], op=ALU.add),
         [B_pc], [B_cc])
    P.op("dve", lambda e: e.tensor_scalar(out=ccT[:, :], in0=ccT[:, :], scalar1=-1.0, scalar2=1.0,
                                          op0=ALU.mult, op1=ALU.add), [B_cc], [B_cc])
    PCB = 2 * NST

    def pcc(hp, j):
        return pc[:, PCB + 9 * hp + j:PCB + 9 * hp + j + 1]

    for hp in range(NHP):
        P.op("dve", lambda e, hp=hp: e.tensor_scalar(out=omka[:, hp:hp + 1], in0=pcc(hp, 5), scalar1=-1.0,
                                                     scalar2=1.0, op0=ALU.mult, op1=ALU.add), [B_pc], [B_omka])

    KG = 4
    for ct in range(c.NCT):
        for k0 in range(0, KC, KG):
            k1 = min(KC, k0 + KG)
            P.dma("pool", wq_in[ct][:, k0:k1, :],
                  w_in_d[k0 * 128:k1 * 128, ct * 128:(ct + 1) * 128].rearrange("(kc kp) c -> kp kc c", kp=128),
                  [], [B_wq], B_wq)
    for dst_, src_, rows in ((wq_br, w_br_d, RW), (wq_ba, w_ba_d, c.AW), (wq_out, w_out_d, D)):
        for r0 in range(0, rows, 256):
            r1 = min(rows, r0 + 256)
            P.dma("pool", dst_[r0:r1, :], src_[r0:r1, :], [], [B_wq], B_wq)

    ARENA_BYTES = 196 * 1024
    arena_t = sb("arena", [128, ARENA_BYTES // 2], BF16)

    class Arena:
        def __init__(self):
            self.off, self.prev, self.live = 0, {}, []

        def reset(self):
            for b in self.live:
                evs = list(b.r.items())
                if b.w is not None:
                    evs.append(b.w)
                for k, v in evs:
                    if self.prev.get(k, 0) < v:
                        self.prev[k] = v
            self.live, self.off = [], 0

        def alloc(self, name, fshape, dt=F32, parts=128):
            n = 1
            for s in fshape:
                n *= s
            nbytes = n * (4 if dt == F32 else 2)
            nbytes = (nbytes + 63) // 64 * 64
            assert self.off + nbytes <= ARENA_BYTES, ("arena overflow", name, self.off, nbytes)
            v = arena_t[0:parts, self.off // 2:(self.off + n * (4 if dt == F32 else 2)) // 2]
            if dt == F32:
                v = v.bitcast(F32)
            if len(fshape) == 2:
                v = v.rearrange("p (a b) -> p a b", a=fshape[0])
            elif len(fshape) == 3:
                v = v.rearrange("p (a b c) -> p a b c", a=fshape[0], b=fshape[1])
            self.off += nbytes
            b = P.buf(name)
            b.r = dict(self.prev)
            self.live.append(b)
            return v, b

    AR = Arena()
    dbg_cnt = [0]

    def dbg(name, ap, bb, shape):
        if not c.debug:
            return
        t = nc.dram_tensor("dbg_" + name, list(shape), F32, kind="ExternalOutput").ap()
        P.dma("sp", t, ap, [bb], [P.buf("dbgd_" + name)], bb)

    psb = [ps("psb%d" % i, [128, 512]) for i in range(8)]
    B_ps = [P.buf("ps%d" % i) for i in range(8)]

    def phase_a_proj(b):
        AR.reset()
        hT, B_hT = AR.alloc("hT", [KC, T], BF16)
        gbc, B_gbc = AR.alloc("gbc", [D])
        P.dma("sp", gbc, normg_d.partition_broadcast(128), [], [B_gbc], B_gbc)
        xt, B_xt, xn, B_xn, st1, B_st1 = [], [], [], [], [], []
        for i in range(2):
            v, bb = AR.alloc("xt%d" % i, [D]); xt.append(v); B_xt.append(bb)
            v, bb = AR.alloc("xn%d" % i, [D], BF16); xn.append(v); B_xn.append(bb)
            v, bb = AR.alloc("st1_%d" % i, [4]); st1.append(v); B_st1.append(bb)
        junk, B_junk = AR.alloc("junk", [D], BF16)
        for it in range(NT):
            i = it % 2
            P.dma("sp", xt[i], x_d[b, it * 128:(it + 1) * 128, :], [], [B_xt[i]], B_xt[i])
            P.op("act", lambda e, i=i: e.activation(out=junk, in_=xt[i], func=AF.Square, accum_out=st1[i][:, 0:1]),
                 [B_xt[i]], [B_junk, B_st1[i]])
            P.op("act", lambda e, i=i: e.activation(out=st1[i][:, 1:2], in_=st1[i][:, 0:1], func=AF.Sqrt,
                                                     scale=1.0 / D, bias=NORM_EPS), [B_st1[i]], [B_st1[i]])
            P.op("dve", lambda e, i=i: e.reciprocal(out=st1[i][:, 2:3], in_=st1[i][:, 1:2]), [B_st1[i]], [B_st1[i]])
            P.op("dve", lambda e, i=i: e.scalar_tensor_tensor(out=xn[i], in0=xt[i], scalar=st1[i][:, 2:3], in1=gbc,
                                                               op0=ALU.mult, op1=ALU.mult),
                 [B_xt[i], B_st1[i], B_gbc], [B_xn[i]])
            for k0 in range(0, KC, 4):
                nk = min(4, KC - k0)
                pi = (k0 // 4) % 2
                pt = psb[pi][:, :].bitcast(BF16)
                for kk in range(nk):
                    P.op("pe", lambda e, i=i, kk=kk, k0=k0, pt=pt: e.transpose(
                        out=pt[:, kk * 128:(kk + 1) * 128], in_=xn[i][:, (k0 + kk) * 128:(k0 + kk + 1) * 128],
                        identity=ident_b[:, :]), [B_xn[i], B_identb], [B_ps[pi]], inc=(kk == nk - 1))
                src = pt[:, 0:nk * 128].rearrange("p (k t) -> p k t", k=nk)
                dst = hT[:, k0:k0 + nk, it * 128:(it + 1) * 128]
                if (k0 // 4) % 2 == 0:
                    P.op("act", lambda e, src=src, dst=dst: e.activation(out=dst, in_=src, func=AF.Copy),
                         [B_ps[pi]], [B_hT])
                else:
                    P.op("dve", lambda e, src=src, dst=dst: e.tensor_copy(out=dst, in_=src), [B_ps[pi]], [B_hT])
        for k0 in range(0, KC, KG):
            k1 = min(KC, k0 + KG)
            P.dma("sp", hT_s[b][k0 * 128:k1 * 128, :].rearrange("(kc kp) t -> kp kc t", kp=128), hT[:, k0:k1, :],
                  [B_hT], [B_hTs[b]], B_hT)

        wt, B_wt, pdst, B_pdst = [], [], [], []
        for i in range(2):
            v, bb = AR.alloc("wt%d" % i, [KC, 128], BF16); wt.append(v); B_wt.append(bb)
            v, bb = AR.alloc("pdst%d" % i, [T]); pdst.append(v); B_pdst.append(bb)
        ypad, B_ypad = AR.alloc("ypad", [T + 2])
        P.op("dve", lambda e: e.memset(ypad[:, 0:1], 0.0), [], [B_ypad])
        P.op("dve", lambda e: e.memset(ypad[:, T + 1:T + 2], 0.0), [], [B_ypad])
        ct_wd = c.O_WD // 128
        silu_tiles = set(range(c.O_ZR // 128, c.O_Q // 128)) | set(range(c.O_ZA // 128, c.O_GR // 128))
        for ct in range(NPT):
            i = ct % 2
            P.dma("sp", wt[i], wq_in[ct], [B_wq], [B_wt[i]], B_wt[i])
            shifted = ct < NST
            dst = pdst[i]
            for tc in range(NTC):
                pb = (ct * NTC + tc) % 2
                for kc in range(KC):
                    P.op("pe", lambda e, i=i, kc=kc, tc=tc, pb=pb: e.matmul(
                        psb[pb][:, 0:TC], lhsT=wt[i][:, kc, :], rhs=hT[:, kc, tc * TC:(tc + 1) * TC],
                        start=(kc == 0), stop=(kc == KC - 1)), [B_wt[i], B_hT], [B_ps[pb]], inc=(kc == KC - 1))
                if shifted:
                    P.op("act", lambda e, tc=tc, pb=pb: e.activation(out=ypad[:, 1 + tc * TC:1 + (tc + 1) * TC],
                                                                     in_=psb[pb][:, 0:TC], func=AF.Copy),
                         [B_ps[pb]], [B_ypad])
                else:
                    fn = AF.Silu if ct in silu_tiles else AF.Copy
                    P.op("act", lambda e, tc=tc, pb=pb, dst=dst, fn=fn: e.activation(
                        out=dst[:, tc * TC:(tc + 1) * TC], in_=psb[pb][:, 0:TC], func=fn), [B_ps[pb]], [B_pdst[i]])
            if shifted:
                P.op("dve", lambda e, ct=ct, dst=dst: e.tensor_scalar(out=dst, in0=ypad[:, 1:T + 1],
                                                                      scalar1=ccT[:, ct:ct + 1], scalar2=None,
                                                                      op0=ALU.mult), [B_ypad, B_cc], [B_pdst[i]])
                P.op("dve", lambda e, ct=ct, dst=dst: e.scalar_tensor_tensor(
                    out=dst, in0=ypad[:, 0:T], scalar=pc[:, 2 * ct:2 * ct + 1], in1=dst, op0=ALU.mult, op1=ALU.add),
                    [B_ypad, B_pc, B_pdst[i]], [B_pdst[i]])
                P.op("dve", lambda e, ct=ct, dst=dst: e.scalar_tensor_tensor(
                    out=dst, in0=ypad[:, 2:T + 2], scalar=pc[:, 2 * ct + 1:2 * ct + 2], in1=dst, op0=ALU.mult,
                    op1=ALU.add), [B_ypad, B_pc, B_pdst[i]], [B_pdst[i]])
                if ct == ct_wd:
                    P.op("act", lambda e, dst=dst: e.activation(out=dst, in_=dst, func=AF.Tanh),
                         [B_pdst[i]], [B_pdst[i]])
            P.dma("sp", proj_s[ct], dst, [B_pdst[i]], [B_proj[ct]], B_pdst[i])

    stop = getattr(c, "stop", None)
    NCH = c.NCH
    G = 4
    NG = NCH // G

    def rwkv_phase(b):
        AR.reset()
        A = AR.alloc
        wdT, B_wd = A("wdT", [T]); adT, B_ad = A("adT", [T])
        rT, B_r = A("rT", [T]); kT, B_k = A("kT", [T]); vT, B_v = A("vT", [T]); kkT, B_kk = A("kkT", [T])
        vrT, B_vr = A("vrT", [T])
        W_, B_W = A("W_", [T]); A_, B_A = A("A_", [T]); KE_, B_KE = A("KE_", [T]); G0_, B_G0 = A("G0_", [T])
        GM_, B_GM = A("GM_", [T + 1]); CS_, B_CS = A("CS_", [T])
        oT = []; B_oT = []
        for d in range(2):
            v, bb = A("oT%d" % d, [T]); oT.append(v); B_oT.append(bb)
        m0, B_m0 = A("m0", [T]); m1, B_m1 = A("m1", [T])
        wup, B_wup = A("wup", [RW]); aup, B_aup = A("aup", [RW])
        gL, B_gL = A("gL", [NCH])
        S_, B_S = A("S_", [64]); Sg_, B_Sg = A("Sg_", [64]); Sbd, B_Sbd = A("Sbd", [128])
        BTt, B_BTt, KTt, B_KTt, Vt, B_Vt, CTm, B_CTm, DTm, B_DTm, Nm, B_Nm, BVs, B_BVs = ([] for _ in range(14))
        for i in range(2):
            for lst, bl, nm, shp in ((BTt, B_BTt, "BTt", [G, 128]), (KTt, B_KTt, "KTt", [G, 128]),
                                     (Vt, B_Vt, "Vt", [G, 128]), (CTm, B_CTm, "CTm", [512]),
                                     (DTm, B_DTm, "DTm", [512]), (Nm, B_Nm, "Nm", [512]), (BVs, B_BVs, "BVs", [512])):
                v, bb = A("%s%d" % (nm, i), shp, F32, 64); lst.append(v); bl.append(bb)
        ATm, B_ATm = A("ATm", [512], F32, 64); Am, B_Am = A("Am", [512], F32, 64)
        BTm, B_BTm = A("BTm", [512], F32, 64)
        Pq, B_Pq, PTq, B_PTq = [], [], [], []
        for i in range(2):
            v, bb = A("Pq%d" % i, [512], F32, 64); Pq.append(v); B_Pq.append(bb)
            v, bb = A("PTq%d" % i, [512], F32, 64); PTq.append(v); B_PTq.append(bb)
        RHSs, B_RHS = A("RHSs", [128], F32, 64); Us, B_Us = A("Us", [128], F32, 64)
        ob, B_ob = A("ob", [T], BF16)

        P.dma("sp", m0, cm0_d[:, :], [], [B_m0], B_m0)
        P.op("dve", lambda e: e.tensor_scalar(out=m1, in0=m0, scalar1=-1.0, scalar2=1.0, op0=ALU.mult, op1=ALU.add),
             [B_m0], [B_m1])
        P.dma("sp", wup, wup_d[:, :], [], [B_wup], B_wup)
        P.dma("sp", aup, aup_d[:, :], [], [B_aup], B_aup)
        P.dma("sp", wdT, proj_s[c.O_WD // 128], [B_proj[c.O_WD // 128]], [B_wd], B_wd)
        P.dma("sp", adT, proj_s[c.O_AD // 128], [B_proj[c.O_AD // 128]], [B_ad], B_ad)
        P.op("dve", lambda e: e.memset(GM_[:, 0:1], 1.0), [], [B_GM])

        def dv(fn, reads, writes):
            P.op("dve", fn, reads, writes)

        for hp in range(NHP):
            hs = [slice(0, 64), slice(64, 128)]
            for buf, bb, off in ((rT, B_r, c.O_R), (kT, B_k, c.O_K), (vT, B_v, c.O_V)):
                ctl = off // 128 + hp
                P.dma("sp", buf, proj_s[ctl], [B_proj[ctl]], [bb], bb)
            dv(lambda e, hp=hp: e.tensor_scalar(out=kkT, in0=kT, scalar1=pcc(hp, 4), scalar2=None, op0=ALU.mult),
               [B_k, B_pc], [B_kk])
            dv(lambda e: e.tensor_tensor(out=W_, in0=kkT, in1=kkT, op=ALU.mult), [B_kk], [B_W])
            for tc in range(NTC):
                sl = slice(tc * TC, (tc + 1) * TC)
                P.op("pe", lambda e, sl=sl: e.matmul(psb[0][:, 0:TC], lhsT=bdones[:, :], rhs=W_[:, sl], start=True,
                                                     stop=True), [B_bd, B_W], [B_ps[0]])
                dv(lambda e, sl=sl: e.tensor_scalar_max(out=A_[:, sl], in0=psb[0][:, 0:TC], scalar1=1e-24),
                   [B_ps[0]], [B_A])
            P.op("act", lambda e: e.activation(out=A_, in_=A_, func=AF.Sqrt), [B_A], [B_A])
            dv(lambda e: e.reciprocal(out=A_, in_=A_), [B_A], [B_A])
            dv(lambda e: e.tensor_tensor(out=kkT, in0=kkT, in1=A_, op=ALU.mult), [B_kk, B_A], [B_kk])
            dv(lambda e: e.tensor_copy(out=vrT, in_=rev_ap(vT)), [B_v], [B_vr])

            for d in range(2):
                dsl = slice(d * 64, (d + 1) * 64)
                rsrc = (lambda ap: ap) if d == 0 else rev_ap
                vsrc = vT if d == 0 else vrT
                B_vsrc = B_v if d == 0 else B_vr
                for tc in range(NTC):
                    sl = slice(tc * TC, (tc + 1) * TC)
                    osl = sl if d == 0 else slice(T - (tc + 1) * TC, T - tc * TC)
                    P.op("pe", lambda e, sl=sl, hp=hp, dsl=dsl: e.matmul(
                        psb[0][:, 0:TC], lhsT=wup[dsl, hp * 128:(hp + 1) * 128], rhs=wdT[dsl, sl], start=True,
                        stop=True), [B_wup, B_wd], [B_ps[0]])
                    P.op("act", lambda e, osl=osl, hp=hp, d=d: e.activation(
                        out=W_[:, osl], in_=rsrc(psb[0][:, 0:TC]), func=AF.Sigmoid, bias=pcc(hp, 0 + d), scale=1.0),
                        [B_ps[0], B_pc], [B_W])
                    P.op("pe", lambda e, sl=sl, hp=hp, dsl=dsl: e.matmul(
                        psb[1][:, 0:TC], lhsT=aup[dsl, hp * 128:(hp + 1) * 128], rhs=adT[dsl, sl], start=True,
                        stop=True), [B_aup, B_ad], [B_ps[1]])
                    P.op("act", lambda e, osl=osl, hp=hp, d=d: e.activation(
                        out=A_[:, osl], in_=rsrc(psb[1][:, 0:TC]), func=AF.Sigmoid, bias=pcc(hp, 2 + d), scale=1.0),
                        [B_ps[1], B_pc], [B_A])
                P.op("act", lambda e: e.activation(out=W_, in_=W_, func=AF.Exp, scale=-math.exp(-0.5)), [B_W], [B_W])
                if hp == 0 and b == 0:
                    dbg("w%d" % d, W_, B_W, [128, T]); dbg("a%d" % d, A_, B_A, [128, T])
                dv(lambda e, hp=hp: e.tensor_scalar(out=KE_, in0=A_, scalar1=pcc(hp, 5), scalar2=omka[:, hp:hp + 1],
                                                    op0=ALU.mult, op1=ALU.add), [B_A, B_pc, B_omka], [B_KE])
                dv(lambda e: e.tensor_tensor(out=KE_, in0=KE_, in1=rsrc(kT), op=ALU.mult), [B_KE, B_k], [B_KE])
                dv(lambda e, hp=hp: e.scalar_tensor_tensor(out=G0_, in0=KE_, scalar=pcc(hp, 6), in1=rsrc(rT),
                                                           op0=ALU.mult, op1=ALU.mult), [B_KE, B_pc, B_r], [B_G0])
                if d == 0:
                    dv(lambda e: e.tensor_copy(out=CS_, in_=G0_), [B_G0], [B_CS])
                else:
                    dv(lambda e: e.tensor_tensor(out=CS_, in0=CS_, in1=rev_ap(G0_), op=ALU.add), [B_CS, B_G0], [B_CS])
                dv(lambda e: e.tensor_tensor(out=A_, in0=A_, in1=rsrc(kkT), op=ALU.mult), [B_A, B_kk], [B_A])
                dv(lambda e: e.tensor_tensor(out=G0_, in0=W_, in1=m0, op=ALU.mult), [B_W, B_m0], [B_G0])
                dv(lambda e: e.tensor_tensor(out=W_, in0=W_, in1=G0_, op=ALU.subtract), [B_W, B_G0], [B_W])
                dv(lambda e: e.tensor_tensor_scan(out=GM_[:, 1:T + 1], data0=G0_, data1=W_, initial=0.0,
                                                  op0=ALU.mult, op1=ALU.add), [B_G0, B_W], [B_GM])
                dv(lambda e: e.tensor_tensor(out=G0_, in0=GM_[:, 0:T], in1=m0, op=ALU.mult), [B_GM, B_m0], [B_G0])
                dv(lambda e: e.tensor_tensor(out=G0_, in0=G0_, in1=m1, op=ALU.add), [B_G0, B_m1], [B_G0])
                dv(lambda e: e.scalar_tensor_tensor(out=G0_, in0=rsrc(kkT), scalar=-1.0, in1=G0_, op0=ALU.mult,
                                                    op1=ALU.mult), [B_kk, B_G0], [B_G0])
                dv(lambda e: e.tensor_tensor(out=W_, in0=rsrc(rT), in1=GM_[:, 1:T + 1], op=ALU.mult),
                   [B_r, B_GM], [B_W])
                dv(lambda e: e.tensor_copy(out=gL, in_=GM_[:, 1:T + 1].rearrange("p (n l) -> p n l", l=CL)[:, :, CL - 1]),
                   [B_GM], [B_gL])
                dv(lambda e: e.reciprocal(out=GM_[:, 1:T + 1], in_=GM_[:, 1:T + 1]), [B_GM], [B_GM])
                dv(lambda e: e.tensor_tensor(out=A_, in0=A_, in1=GM_[:, 1:T + 1], op=ALU.mult), [B_A, B_GM], [B_A])
                dv(lambda e: e.tensor_tensor(out=KE_, in0=KE_, in1=GM_[:, 1:T + 1], op=ALU.mult), [B_KE, B_GM], [B_KE])
                at, bt, kt, rt = G0_, A_, KE_, W_
                B_at, B_bt, B_kt, B_rt = B_G0, B_A, B_KE, B_W
                dv(lambda e: e.memset(S_, 0.0), [], [B_S])
                dv(lambda e: e.memset(Sbd, 0.0), [], [B_Sbd])

                def precompute(g, s):
                    t0 = g * G * CL
                    for src, B_src, dstl, B_dstl, pb in ((bt, B_bt, BTt, B_BTt, 3), (kt, B_kt, KTt, B_KTt, 4),
                                                         (vsrc, B_vsrc, Vt, B_Vt, 3)):
                        for j in range(G):
                            P.op("pe", lambda e, src=src, j=j, pb=pb: e.matmul(
                                psb[pb][0:64, j * 128:(j + 1) * 128], lhsT=src[:, t0 + j * CL:t0 + (j + 1) * CL],
                                rhs=ident_f[:, :], start=True, stop=True), [B_src, B_identf], [B_ps[pb]],
                                inc=(j == G - 1))
                        P.op("act", lambda e, dstl=dstl, pb=pb: e.activation(
                            out=dstl[s], in_=psb[pb][0:64, :].rearrange("p (g c) -> p g c", g=G), func=AF.Copy),
                            [B_ps[pb]], [B_dstl[s]])
                        yield
                    if stop == "P1":
                        return
                    specs = ((0, bt, B_bt, at, B_at), (1, at, B_at, bt, B_bt), (2, kt, B_kt, at, B_at),
                             (3, bt, B_bt, rt, B_rt), (4, kt, B_kt, rt, B_rt))
                    for h in range(2):
                        for pb, L, B_L, R, B_R in specs:
                            for j in range(G):
                                tsl = slice(t0 + j * CL, t0 + (j + 1) * CL)
                                col = (j * 2 + h) * 64
                                P.op("pe", lambda e, pb=pb, L=L, R=R, tsl=tsl, h=h, col=col: e.matmul(
                                    psb[pb][0:64, col:col + 64], lhsT=L[hs[h], tsl], rhs=R[hs[h], tsl], start=True,
                                    stop=True), [B_L, B_R], [B_ps[pb]], inc=(j == G - 1 and h == 1))
                    yield
                    dv(lambda e: e.tensor_tensor(out=ATm, in0=psb[0][0:64, :], in1=masks[:, 0, :], op=ALU.mult),
                       [B_ps[0], B_mask], [B_ATm])
                    dv(lambda e: e.tensor_tensor(out=Am, in0=psb[1][0:64, :], in1=masks[:, 1, :], op=ALU.mult),
                       [B_ps[1], B_mask], [B_Am])
                    dv(lambda e: e.tensor_tensor(out=BTm, in0=psb[2][0:64, :], in1=masks[:, 0, :], op=ALU.mult),
                       [B_ps[2], B_mask], [B_BTm])
                    dv(lambda e: e.tensor_tensor(out=CTm[s], in0=psb[3][0:64, :], in1=masks[:, 2, :], op=ALU.mult),
                       [B_ps[3], B_mask], [B_CTm[s]])
                    dv(lambda e: e.tensor_tensor(out=DTm[s], in0=psb[4][0:64, :], in1=masks[:, 2, :], op=ALU.mult),
                       [B_ps[4], B_mask], [B_DTm[s]])
                    yield
                    dv(lambda e: e.tensor_tensor(out=Nm[s], in0=ATm, in1=I8[:, :], op=ALU.add), [B_ATm, B_I8], [B_Nm[s]])
                    if stop == "P2":
                        return
                    for j in range(G):
                        for h in range(2):
                            col = (j * 2 + h) * 64
                            P.op("pe", lambda e, j=j, h=h, col=col: e.matmul(
                                psb[2][0:64, col:col + 64], lhsT=BTm[:, col:col + 64],
                                rhs=Vt[s][:, j, h * 64:(h + 1) * 64], start=True, stop=True),
                                [B_BTm, B_Vt[s]], [B_ps[2]], inc=(j == G - 1 and h == 1))
                    P.op("act", lambda e: e.activation(out=BVs[s], in_=psb[2][0:64, :], func=AF.Copy),
                         [B_ps[2]], [B_BVs[s]])
                    yield
                    if stop == "P3":
                        return
                    Pc, B_Pc, PTc, B_PTc = Am, B_Am, ATm, B_ATm
                    for lev in range(5):
                        q = lev % 2
                        last = lev == 4
                        for p8 in range(2 * G):
                            col = p8 * 64
                            P.op("pe", lambda e, Pc=Pc, PTc=PTc, col=col: e.matmul(
                                psb[0][0:64, col:col + 64], lhsT=PTc[:, col:col + 64], rhs=Pc[:, col:col + 64],
                                start=True, stop=True), [B_Pc, B_PTc], [B_ps[0]], inc=(p8 == 2 * G - 1))
                        if not last:
                            for p8 in range(2 * G):
                                col = p8 * 64
                                P.op("pe", lambda e, Pc=Pc, PTc=PTc, col=col: e.matmul(
                                    psb[1][0:64, col:col + 64], lhsT=Pc[:, col:col + 64], rhs=PTc[:, col:col + 64],
                                    start=True, stop=True), [B_Pc, B_PTc], [B_ps[1]], inc=(p8 == 2 * G - 1))
                        yield
                        dv(lambda e, q=q: e.tensor_copy(out=Pq[q], in_=psb[0][0:64, :]), [B_ps[0]], [B_Pq[q]])
                        if not last:
                            P.op("act", lambda e, q=q: e.activation(out=PTq[q], in_=psb[1][0:64, :], func=AF.Copy),
                                 [B_ps[1]], [B_PTq[q]])
                        for p8 in range(2 * G):
                            col = p8 * 64
                            P.op("pe", lambda e, q=q, col=col: e.matmul(
                                psb[4][0:64, col:col + 64], lhsT=Pq[q][:, col:col + 64], rhs=Nm[s][:, col:col + 64],
                                start=True, stop=True), [B_Pq[q], B_Nm[s]], [B_ps[4]], inc=(p8 == 2 * G - 1))
                        yield
                        dv(lambda e: e.tensor_tensor(out=Nm[s], in0=Nm[s], in1=psb[4][0:64, :], op=ALU.add),
                           [B_Nm[s], B_ps[4]], [B_Nm[s]])
                        Pc, B_Pc, PTc, B_PTc = Pq[q], B_Pq[q], PTq[q], B_PTq[q]

                def chain(g, s):
                    for j in range(G):
                        ch = g * G + j
                        tsl = slice(ch * CL, (ch + 1) * CL)
                        P.op("pe", lambda e, tsl=tsl: e.matmul(
                            psb[5][0:64, 0:128], lhsT=at[:, tsl], rhs=Sbd, start=True, stop=True),
                            [B_at, B_Sbd], [B_ps[5]])
                        yield
                        dv(lambda e, j=j: e.tensor_tensor(out=RHSs, in0=psb[5][0:64, 0:128],
                                                          in1=BVs[s][:, j * 128:(j + 1) * 128], op=ALU.add),
                           [B_ps[5], B_BVs[s]], [B_RHS])
                        for h in range(2):
                            col = (j * 2 + h) * 64
                            P.op("pe", lambda e, h=h, col=col: e.matmul(
                                psb[5][0:64, 128 + h * 64:128 + (h + 1) * 64], lhsT=Nm[s][:, col:col + 64],
                                rhs=RHSs[:, h * 64:(h + 1) * 64], start=True, stop=True),
                                [B_Nm[s], B_RHS], [B_ps[5]], inc=(h == 1))
                        yield
                        P.op("act", lambda e: e.activation(out=Us, in_=psb[5][0:64, 128:256], func=AF.Copy),
                             [B_ps[5]], [B_Us])
                        dv(lambda e, ch=ch: e.tensor_scalar(out=Sg_, in0=S_, scalar1=gL[:, ch:ch + 1], scalar2=None,
                                                            op0=ALU.mult), [B_S, B_gL], [B_Sg])
                        for h in range(2):
                            hc = slice(h * 64, (h + 1) * 64)
                            P.op("pe", lambda e, h=h, hc=hc, j=j: e.matmul(
                                psb[6][hs[h], 0:64], lhsT=KTt[s][:, j, hc], rhs=Vt[s][:, j, hc], start=True,
                                stop=False, tile_position=(0, h * 64)), [B_KTt[s], B_Vt[s]], [B_ps[6]], inc=False)
                            P.op("pe", lambda e, h=h, hc=hc, j=j: e.matmul(
                                psb[6][hs[h], 0:64], lhsT=BTt[s][:, j, hc], rhs=Us[:, hc], start=False, stop=True,
                                tile_position=(0, h * 64)), [B_BTt[s], B_Us], [B_ps[6]], inc=(h == 1))
                        oc = (ch % 8) * 64
                        for h in range(2):
                            hc = slice(h * 64, (h + 1) * 64)
                            col = (j * 2 + h) * 64
                            P.op("pe", lambda e, h=h, hc=hc, tsl=tsl, oc=oc: e.matmul(
                                psb[7][hs[h], oc:oc + 64], lhsT=Sbd[:, hc], rhs=rt[:, tsl], start=True,
                                stop=False, tile_position=(0, h * 64)), [B_Sbd, B_rt], [B_ps[7]], inc=False)
                            P.op("pe", lambda e, h=h, hc=hc, col=col, oc=oc: e.matmul(
                                psb[7][hs[h], oc:oc + 64], lhsT=Us[:, hc], rhs=CTm[s][:, col:col + 64], start=False,
                                stop=False, tile_position=(0, h * 64)), [B_Us, B_CTm[s]], [B_ps[7]], inc=False)
                            P.op("pe", lambda e, h=h, hc=hc, col=col, oc=oc, j=j: e.matmul(
                                psb[7][hs[h], oc:oc + 64], lhsT=Vt[s][:, j, hc], rhs=DTm[s][:, col:col + 64],
                                start=False, stop=True, tile_position=(0, h * 64)), [B_Vt[s], B_DTm[s]], [B_ps[7]],
                                inc=(h == 1))
                        yield
                        dv(lambda e, ch=ch: e.scalar_tensor_tensor(out=S_, in0=psb[6][:, 0:64], scalar=gL[:, ch:ch + 1],
                                                                   in1=Sg_, op0=ALU.mult, op1=ALU.add),
                           [B_ps[6], B_gL, B_Sg], [B_S])
                        for h in range(2):
                            P.op("act", lambda e, h=h: e.activation(out=Sbd[hs[h], h * 64:(h + 1) * 64], in_=S_[hs[h], :],
                                                                    func=AF.Copy), [B_S], [B_Sbd])
                        if ch % 8 == 7 or ch == NCH - 1:
                            nch8 = ch % 8 + 1
                            c0 = (ch - nch8 + 1) * CL
                            dst = oT[d][:, c0:c0 + nch8 * CL]
                            if d == 1:
                                dst = rev_ap(oT[d][:, T - c0 - nch8 * CL:T - c0])
                            P.op("act", lambda e, dst=dst, nch8=nch8: e.activation(
                                out=dst, in_=psb[7][:, 0:nch8 * CL], func=AF.Copy), [B_ps[7]], [B_oT[d]])

                if hp == 0 and b == 0 and d == 0:
                    dbg("at", at, B_at, [128, T]); dbg("bt", bt, B_bt, [128, T]); dbg("kt", kt, B_kt, [128, T])
                    dbg("rt", rt, B_rt, [128, T]); dbg("v", vT, B_v, [128, T]); dbg("gL", gL, B_gL, [128, NCH])
                if stop == "R1":
                    break
                for _ in precompute(0, 0):
                    pass
                for g in range(NG):
                    if stop in ("R2", "P1", "P2", "P3"):
                        break
                    ch_it = chain(g, g % 2)
                    pre_it = precompute(g + 1, (g + 1) % 2) if g + 1 < NG else iter(())
                    ch_done = pre_done = False
                    while not (ch_done and pre_done):
                        if not ch_done:
                            try:
                                next(ch_it)
                            except StopIteration:
                                ch_done = True
                        for _ in range(2):
                            if not pre_done:
                                try:
                                    next(pre_it)
                                except StopIteration:
                                    pre_done = True

            if stop in ("R1", "R2", "P1", "P2", "P3"):
                break
            if hp == 0 and b == 0:
                dbg("o0", oT[0], B_oT[0], [128, T]); dbg("o1", oT[1], B_oT[1], [128, T])
                dbg("kk", kkT, B_kk, [128, T]); dbg("r", rT, B_r, [128, T]); dbg("cs", CS_, B_CS, [128, T])
            zr, B_zr = W_, B_W
            ctl = c.O_ZR // 128 + hp
            P.dma("sp", zr, proj_s[ctl], [B_proj[ctl]], [B_zr], B_zr)
            wkv, B_wkv = oT[0], B_oT[0]
            dv(lambda e: e.tensor_tensor(out=wkv, in0=oT[0], in1=oT[1], op=ALU.add), [B_oT[0], B_oT[1]], [B_wkv])
            cen, B_cen = A_, B_A
            for tc in range(NTC):
                sl = slice(tc * TC, (tc + 1) * TC)
                P.op("pe", lambda e, sl=sl: e.matmul(psb[0][:, 0:TC], lhsT=bdones[:, :], rhs=wkv[:, sl], start=True,
                                                     stop=True), [B_bd, B_wkv], [B_ps[0]])
                dv(lambda e, sl=sl: e.scalar_tensor_tensor(out=cen[:, sl], in0=psb[0][:, 0:TC], scalar=-1.0 / HD,
                                                           in1=wkv[:, sl], op0=ALU.mult, op1=ALU.add),
                   [B_ps[0], B_wkv], [B_cen])
            sq, B_sq = KE_, B_KE
            dv(lambda e: e.tensor_tensor(out=sq, in0=cen, in1=cen, op=ALU.mult), [B_cen], [B_sq])
            rs, B_rs = G0_, B_G0
            for tc in range(NTC):
                sl = slice(tc * TC, (tc + 1) * TC)
                P.op("pe", lambda e, sl=sl: e.matmul(psb[1][:, 0:TC], lhsT=bdones[:, :], rhs=sq[:, sl], start=True,
                                                     stop=True), [B_bd, B_sq], [B_ps[1]])
                P.op("act", lambda e, sl=sl: e.activation(out=rs[:, sl], in_=psb[1][:, 0:TC], func=AF.Sqrt,
                                                           scale=1.0 / HD, bias=GN_EPS), [B_ps[1]], [B_rs])
            dv(lambda e: e.reciprocal(out=rs, in_=rs), [B_rs], [B_rs])
            dv(lambda e: e.tensor_tensor(out=cen, in0=cen, in1=rs, op=ALU.mult), [B_cen, B_rs], [B_cen])
            dv(lambda e, hp=hp: e.tensor_scalar(out=cen, in0=cen, scalar1=pcc(hp, 7), scalar2=pcc(hp, 8),
                                                op0=ALU.mult, op1=ALU.add), [B_cen, B_pc], [B_cen])
            for tc in range(NTC):
                sl = slice(tc * TC, (tc + 1) * TC)
                P.op("pe", lambda e, sl=sl: e.matmul(psb[0][:, 0:TC], lhsT=bdones[:, :], rhs=CS_[:, sl], start=True,
                                                     stop=True), [B_bd, B_CS], [B_ps[0]])
                dv(lambda e, sl=sl: e.tensor_tensor(out=sq[:, sl], in0=psb[0][:, 0:TC], in1=vT[:, sl], op=ALU.mult),
                   [B_ps[0], B_v], [B_sq])
            dv(lambda e: e.tensor_tensor(out=cen, in0=cen, in1=sq, op=ALU.add), [B_cen, B_sq], [B_cen])
            dv(lambda e: e.tensor_tensor(out=ob, in0=cen, in1=zr, op=ALU.mult), [B_cen, B_zr], [B_ob])
            P.dma("sp", orT_s[b][hp * 128:(hp + 1) * 128, :], ob, [B_ob], [B_orTs[b]], B_ob)

    NKP = c.KVW // 128
    NQP = c.AW // 128
    SCALE = float(HD) ** -0.5

    def attn_phase(b):
        AR.reset()
        A = AR.alloc
        cosT, B_cos = A("cosT", [T]); sinT, B_sin = A("sinT", [T])
        P.dma("sp", cosT, ccos_d[:, :], [], [B_cos], B_cos)
        P.dma("sp", sinT, csin_d[:, :], [], [B_sin], B_sin)
        src, B_src = A("asrc", [T]); t1, B_t1 = A("at1", [T]); t2, B_t2 = A("at2", [T])
        za, B_za = A("za", [T])
        nb, B_nb = A("anb", [T], BF16)
        og, B_og = A("og", [T], BF16)
        qz, B_qz = [], []
        for p in range(2):
            v, bb = A("qz%d" % p, [T], BF16); qz.append(v); B_qz.append(bb)
        kd, B_kd = [], []
        for kvh in range(c.KVH):
            v, bb = A("kd%d" % kvh, [T], BF16); kd.append(v); B_kd.append(bb)
        Va, B_Va = [], []
        for kvh in range(c.KVH):
            row, brow = [], []
            for p in range(2):
                v, bb = A("Va%d_%d" % (kvh, p), [NT, 128], BF16); row.append(v); brow.append(bb)
            Va.append(row); B_Va.append(brow)
        pT, B_pT = [], []
        for i in range(3):
            v, bb = A("pT%d" % i, [TC], BF16); pT.append(v); B_pT.append(bb)
        rc, B_rc = A("rc", [TC]); on, B_on = A("on", [TC])

        def dv(fn, reads, writes):
            P.op("dve", fn, reads, writes)

        def qk_prep(ct, gcol):
            P.dma("sp", src, proj_s[ct], [B_proj[ct]], [B_src], B_src)
            dv(lambda e: e.tensor_tensor(out=t1, in0=src, in1=src, op=ALU.mult), [B_src], [B_t1])
            for tc in range(NTC):
                sl = slice(tc * TC, (tc + 1) * TC)
                P.op("pe", lambda e, sl=sl: e.matmul(psb[5][:, 0:TC], lhsT=bdones[:, :], rhs=t1[:, sl], start=True,
                                                     stop=True), [B_bd, B_t1], [B_ps[5]])
                P.op("act", lambda e, sl=sl: e.activation(out=t2[:, sl], in_=psb[5][:, 0:TC], func=AF.Sqrt,
                                                           scale=1.0 / HD, bias=NORM_EPS), [B_ps[5]], [B_t2])
            dv(lambda e: e.reciprocal(out=t2, in_=t2), [B_t2], [B_t2])
            dv(lambda e: e.scalar_tensor_tensor(out=t2, in0=src, scalar=pc[:, gcol:gcol + 1], in1=t2, op0=ALU.mult,
                                                op1=ALU.mult), [B_src, B_pc, B_t2], [B_t2])
            for tc in range(NTC):
                sl = slice(tc * TC, (tc + 1) * TC)
                P.op("pe", lambda e, sl=sl: e.matmul(psb[6][:, 0:TC], lhsT=rotm[:, :], rhs=t2[:, sl], start=True,
                                                     stop=True), [B_rot, B_t2], [B_ps[6]])
                dv(lambda e, sl=sl: e.tensor_tensor(out=t1[:, sl], in0=psb[6][:, 0:TC], in1=sinT[:, sl], op=ALU.mult),
                   [B_ps[6], B_sin], [B_t1])
            dv(lambda e: e.tensor_tensor(out=t2, in0=t2, in1=cosT, op=ALU.mult), [B_t2, B_cos], [B_t2])
            dv(lambda e: e.tensor_tensor(out=t2, in0=t2, in1=t1, op=ALU.add), [B_t2, B_t1], [B_t2])
            dv(lambda e: e.tensor_copy(out=nb, in_=t2), [B_t2], [B_nb])

        GQ = PCB + 9 * NHP
        for kp in range(NKP):
            qk_prep(c.O_AK // 128 + kp, GQ + 1)
            for h2 in range(2):
                kvh = kp * 2 + h2
                hsl = slice(h2 * 64, (h2 + 1) * 64)
                osl = slice((1 - h2) * 64, (2 - h2) * 64)
                dv(lambda e, kvh=kvh, hsl=hsl: e.tensor_copy(out=kd[kvh][hsl, :], in_=t2[hsl, :]), [B_t2], [B_kd[kvh]])
                dv(lambda e, hsl=hsl, osl=osl: e.tensor_copy(out=t1[osl, :], in_=t2[hsl, :]), [B_t2], [B_t1])
                dv(lambda e, kvh=kvh, osl=osl: e.tensor_copy(out=kd[kvh][osl, :], in_=t1[osl, :]), [B_t1], [B_kd[kvh]])
        if stop == "T1":
            return
        for kvh in range(c.KVH):
            for p in range(2):
                dv(lambda e, kvh=kvh, p=p: e.memset(Va[kvh][p].rearrange("p a b -> p (a b)"), 1.0), [], [B_Va[kvh][p]])
        if stop == "V1":
            return
        for kp in range(NKP):
            ct = c.O_AV // 128 + kp
            P.dma("sp", src, proj_s[ct], [B_proj[ct]], [B_src], B_src)
            if stop == "V2":
                return
            for it in range(NT):
                P.op("pe", lambda e, it=it: e.matmul(psb[7][:, 0:128], lhsT=src[:, it * 128:(it + 1) * 128],
                                                     rhs=ident_f[:, :], start=True, stop=True),
                     [B_src, B_identf], [B_ps[7]])
                if stop == "V3":
                    continue
                for h2 in range(2):
                    kvh = kp * 2 + h2
                    cs_ = slice(h2 * 64, (h2 + 1) * 64)
                    dv(lambda e, kvh=kvh, it=it, cs_=cs_: e.tensor_copy(out=Va[kvh][0][:, it, 0:64], in_=psb[7][:, cs_]),
                       [B_ps[7]], [B_Va[kvh][0]])
                    dv(lambda e, kvh=kvh, it=it, cs_=cs_: e.tensor_copy(out=Va[kvh][1][:, it, 64:128], in_=psb[7][:, cs_]),
                       [B_ps[7]], [B_Va[kvh][1]])
        if stop == "T2":
            return
        cnt = 0
        for qp in range(NQP):
            qk_prep(c.O_Q // 128 + qp, GQ)
            ctz = c.O_ZA // 128 + qp
            P.dma("sp", za, proj_s[ctz], [B_proj[ctz]], [B_za], B_za)
            for p in range(2):
                dv(lambda e, p=p: e.memset(qz[p], 0.0), [], [B_qz[p]])
                dv(lambda e, p=p: e.tensor_copy(out=qz[p][p * 64:(p + 1) * 64, :], in_=nb[p * 64:(p + 1) * 64, :]),
                   [B_nb], [B_qz[p]])
            for p in range(2):
                qh = qp * 2 + p
                kvh = qh // c.GROUP
                qsl = slice(p * 64, (p + 1) * 64)
                ssl = slice((1 - p) * 64, (2 - p) * 64)
                for qc in range(NTC):
                    qs = slice(qc * TC, (qc + 1) * TC)
                    acc = 3 + (cnt % 2)
                    def emit_s(kt):
                        sb_ = (cnt * NT + kt) % 3
                        P.op("pe", lambda e, sb_=sb_, kt=kt, qs=qs, kvh=kvh, p=p: e.matmul(
                            psb[sb_][:, 0:TC], lhsT=kd[kvh][:, kt * 128:(kt + 1) * 128], rhs=qz[p][:, qs], start=True,
                            stop=True), [B_kd[kvh], B_qz[p]], [B_ps[sb_]])

                    emit_s(0)
                    for kt in range(NT):
                        sb_ = (cnt * NT + kt) % 3
                        if kt + 1 < NT:
                            emit_s(kt + 1)
                        P.op("act", lambda e, sb_=sb_: e.activation(out=pT[sb_], in_=psb[sb_][:, 0:TC], func=AF.Exp,
                                                                    scale=SCALE), [B_ps[sb_]], [B_pT[sb_]])
                        P.op("pe", lambda e, sb_=sb_, kt=kt, acc=acc, kvh=kvh: e.matmul(
                            psb[acc][:, 0:TC], lhsT=Va[kvh][p][:, kt, :], rhs=pT[sb_], start=(kt == 0),
                            stop=(kt == NT - 1)), [B_Va[kvh][p], B_pT[sb_]], [B_ps[acc]], inc=(kt == NT - 1))
                    dv(lambda e, acc=acc: e.reciprocal(out=rc[ssl, :], in_=psb[acc][ssl, 0:TC]), [B_ps[acc]], [B_rc])
                    dv(lambda e: e.tensor_copy(out=rc[qsl, :], in_=rc[ssl, :]), [B_rc], [B_rc])
                    dv(lambda e, acc=acc: e.tensor_tensor(out=on[qsl, :], in0=psb[acc][qsl, 0:TC], in1=rc[qsl, :],
                                                          op=ALU.mult), [B_ps[acc], B_rc], [B_on])
                    dv(lambda e, qs=qs: e.tensor_tensor(out=og[qsl, qs], in0=on[qsl, :], in1=za[qsl, qs], op=ALU.mult),
                       [B_on, B_za], [B_og])
                    cnt += 1
            P.dma("sp", oaT_s[b][qp * 128:(qp + 1) * 128, :], og, [B_og], [B_oaTs[b]], B_og)

    CG = min(512, D)
    NCG = D // CG
    KR = RW // 128
    KA = c.AW // 128

    def phase_d_all():
        AR.reset()
        A = AR.alloc
        wout, B_wout = A("wout", [KC, D], BF16)
        fbc, B_fbc = A("fbc", [D])
        for k0 in range(0, KC, KG):
            k1 = min(KC, k0 + KG)
            P.dma("sp", wout[:, k0:k1, :], wq_out[k0 * 128:k1 * 128, :].rearrange("(kc kp) n -> kp kc n", kp=128),
                  [B_wq], [B_wout], B_wout)
        P.dma("sp", fbc, fing_d.partition_broadcast(128), [], [B_fbc], B_fbc)
        hTc, B_hTc = A("hTc", [KC, TC], BF16)
        orc, B_orc = A("orc", [KR, TC], BF16); oac, B_oac = A("oac", [KA, TC], BF16)
        mg, B_mg = A("mg", [KC, TC], BF16)
        wgr, B_wgr, wga, B_wga, wbr, B_wbr, wba, B_wba = ([] for _ in range(8))
        for i in range(2):
            v, bb = A("wgr%d" % i, [KC, 128], BF16); wgr.append(v); B_wgr.append(bb)
            v, bb = A("wga%d" % i, [KC, 128], BF16); wga.append(v); B_wga.append(bb)
            v, bb = A("wbr%d" % i, [KR, 128], BF16); wbr.append(v); B_wbr.append(bb)
            v, bb = A("wba%d" % i, [KA, 128], BF16); wba.append(v); B_wba.append(bb)
        s1, B_s1 = A("s1", [TC]); s2, B_s2 = A("s2", [TC]); u1, B_u1 = A("u1", [TC]); u2, B_u2 = A("u2", [TC])
        xt, B_xt = A("dxt", [D]); yp, B_yp = A("yp", [D]); junk, B_junk = A("djunk", [D], BF16)
        st2, B_st2 = A("st2", [4])

        def dv(fn, reads, writes):
            P.op("dve", fn, reads, writes)

        for b in range(c.BPC):
            for tcn in range(NTC):
                ts = slice(tcn * TC, (tcn + 1) * TC)
                for dst_, src_, nk, B_s, B_d in ((hTc, hT_s, KC, B_hTs[b], B_hTc), (orc, orT_s, KR, B_orTs[b], B_orc),
                                                 (oac, oaT_s, KA, B_oaTs[b], B_oac)):
                    for k0 in range(0, nk, KG):
                        k1 = min(nk, k0 + KG)
                        P.dma("sp", dst_[:, k0:k1, :],
                              src_[b][k0 * 128:k1 * 128, ts].rearrange("(kc kp) t -> kp kc t", kp=128),
                              [B_s], [B_d], B_d)
                for j in range(KC):
                    i = j % 2
                    P.dma("sp", wgr[i], wq_in[c.O_GR // 128 + j], [B_wq], [B_wgr[i]], B_wgr[i])
                    P.dma("sp", wga[i], wq_in[c.O_GA // 128 + j], [B_wq], [B_wga[i]], B_wga[i])
                    P.dma("sp", wbr[i], wq_br[:, j * 128:(j + 1) * 128].rearrange("(kc kp) n -> kp kc n", kp=128),
                          [B_wq], [B_wbr[i]], B_wbr[i])
                    P.dma("sp", wba[i], wq_ba[:, j * 128:(j + 1) * 128].rearrange("(kc kp) n -> kp kc n", kp=128),
                          [B_wq], [B_wba[i]], B_wba[i])
                    pb0 = 4 * i
                    for kc in range(KC):
                        P.op("pe", lambda e, kc=kc, i=i, pb0=pb0: e.matmul(
                            psb[pb0][:, 0:TC], lhsT=wgr[i][:, kc, :], rhs=hTc[:, kc, :], start=(kc == 0),
                            stop=(kc == KC - 1)), [B_wgr[i], B_hTc], [B_ps[pb0]], inc=(kc == KC - 1))
                    for kc in range(KC):
                        P.op("pe", lambda e, kc=kc, i=i, pb0=pb0: e.matmul(
                            psb[pb0 + 1][:, 0:TC], lhsT=wga[i][:, kc, :], rhs=hTc[:, kc, :], start=(kc == 0),
                            stop=(kc == KC - 1)), [B_wga[i], B_hTc], [B_ps[pb0 + 1]], inc=(kc == KC - 1))
                    for kc in range(KR):
                        P.op("pe", lambda e, kc=kc, i=i, pb0=pb0: e.matmul(
                            psb[pb0 + 2][:, 0:TC], lhsT=wbr[i][:, kc, :], rhs=orc[:, kc, :], start=(kc == 0),
                            stop=(kc == KR - 1)), [B_wbr[i], B_orc], [B_ps[pb0 + 2]], inc=(kc == KR - 1))
                    for kc in range(KA):
                        P.op("pe", lambda e, kc=kc, i=i, pb0=pb0: e.matmul(
                            psb[pb0 + 3][:, 0:TC], lhsT=wba[i][:, kc, :], rhs=oac[:, kc, :], start=(kc == 0),
                            stop=(kc == KA - 1)), [B_wba[i], B_oac], [B_ps[pb0 + 3]], inc=(kc == KA - 1))
                    P.op("act", lambda e, pb0=pb0: e.activation(out=s1, in_=psb[pb0][:, 0:TC], func=AF.Sigmoid),
                         [B_ps[pb0]], [B_s1])
                    P.op("act", lambda e, pb0=pb0: e.activation(out=s2, in_=psb[pb0 + 1][:, 0:TC], func=AF.Sigmoid),
                         [B_ps[pb0 + 1]], [B_s2])
                    dv(lambda e, pb0=pb0: e.tensor_tensor(out=u1, in0=psb[pb0 + 2][:, 0:TC], in1=s1, op=ALU.mult),
                       [B_ps[pb0 + 2], B_s1], [B_u1])
                    dv(lambda e, pb0=pb0: e.tensor_tensor(out=u2, in0=psb[pb0 + 3][:, 0:TC], in1=s2, op=ALU.mult),
                       [B_ps[pb0 + 3], B_s2], [B_u2])
                    dv(lambda e, j=j: e.tensor_tensor(out=mg[:, j, :], in0=u1, in1=u2, op=ALU.add), [B_u1, B_u2], [B_mg])
                for it in range(TC // 128):
                    t0 = tcn * TC + it * 128
                    P.dma("sp", xt, x_d[b, t0:t0 + 128, :], [], [B_xt], B_xt)
                    for cg in range(NCG):
                        pb = (it * NCG + cg) % 2
                        for kc in range(KC):
                            P.op("pe", lambda e, kc=kc, it=it, cg=cg, pb=pb: e.matmul(
                                psb[pb][:, 0:CG], lhsT=mg[:, kc, it * 128:(it + 1) * 128],
                                rhs=wout[:, kc, cg * CG:(cg + 1) * CG], start=(kc == 0), stop=(kc == KC - 1)),
                                [B_mg, B_wout], [B_ps[pb]], inc=(kc == KC - 1))
                        dv(lambda e, cg=cg, pb=pb: e.tensor_tensor(out=yp[:, cg * CG:(cg + 1) * CG], in0=psb[pb][:, 0:CG],
                                                                   in1=xt[:, cg * CG:(cg + 1) * CG], op=ALU.add),
                           [B_ps[pb], B_xt], [B_yp])
                    P.op("act", lambda e: e.activation(out=junk, in_=yp, func=AF.Square, accum_out=st2[:, 0:1]),
                         [B_yp], [B_junk, B_st2])
                    P.op("act", lambda e: e.activation(out=st2[:, 1:2], in_=st2[:, 0:1], func=AF.Sqrt, scale=1.0 / D,
                                                       bias=NORM_EPS), [B_st2], [B_st2])
                    dv(lambda e: e.reciprocal(out=st2[:, 2:3], in_=st2[:, 1:2]), [B_st2], [B_st2])
                    dv(lambda e: e.scalar_tensor_tensor(out=yp, in0=yp, scalar=st2[:, 2:3], in1=fbc, op0=ALU.mult,
                                                        op1=ALU.mult), [B_yp, B_st2, B_fbc], [B_yp])
                    P.dma("sp", y_d[b, t0:t0 + 128, :], yp, [B_yp], [B_y], B_yp)

    stop = getattr(c, "stop", None)
    for b in range(c.BPC):
        phase_a_proj(b)
        if stop == "A":
            continue
        rwkv_phase(b)
        if stop in ("R", "R1", "R2", "P1", "P2", "P3"):
            continue
        attn_phase(b)
    if stop is None or stop == "D":
        phase_d_all()

    P.finish()
    P.emit()
    return nc


def host_consts(cfg):
    c = cfg
    T = c.T
    ident = np.eye(128, dtype=np.float32)
    j64 = np.eye(64, dtype=np.float32)[::-1].copy()
    rot = np.zeros((128, 128), np.float32)
    for i in range(64):
        rot[2 * i + 1, 2 * i] = -1.0
        rot[2 * i, 2 * i + 1] = 1.0
    rows = T // c.GRID_W
    row = np.repeat(np.arange(rows, dtype=np.float32), c.GRID_W)
    col = np.tile(np.arange(c.GRID_W, dtype=np.float32), rows)
    axis_dim = HD // 2
    freqs = (10000.0 ** (-np.arange(0, axis_dim, 2, dtype=np.float32) / axis_dim)).astype(np.float32)
    ang = np.concatenate([row[:, None] * freqs, col[:, None] * freqs], axis=-1).astype(np.float32)
    cosT = np.repeat(np.cos(ang).T, 2, axis=0)
    sinT = np.repeat(np.sin(ang).T, 2, axis=0)
    cos2 = np.concatenate([cosT, cosT], 0).astype(np.float32)
    sin2 = np.concatenate([sinT, sinT], 0).astype(np.float32)
    bd = np.zeros((128, 128), np.float32)
    bd[:64, :64] = 1.0
    bd[64:, 64:] = 1.0
    m0 = np.ones((128, T), np.float32)
    m0[:, ::CL] = 0.0
    i64 = np.arange(64)
    strict_st = (i64[None, :] > i64[:, None]).astype(np.float32)
    strict_ts = (i64[None, :] < i64[:, None]).astype(np.float32)
    incl_st = (i64[None, :] >= i64[:, None]).astype(np.float32)
    masks = np.stack([np.tile(m, (1, 8)) for m in (strict_st, strict_ts, incl_st)], axis=1)
    return dict(c_ident=ident, c_j64=j64, c_rot=rot, c_cos=cos2, c_sin=sin2, c_bdones=bd, c_m0=m0,
                c_masks=np.ascontiguousarray(masks.astype(np.float32)))


def make_in_maps(cfg, inputs, n_cores):
    c = cfg
    consts = host_consts(c)
    f = lambda a: np.ascontiguousarray(np.asarray(a, dtype=np.float32))
    NST, NHP = c.SHIFT_W // 128, c.RW // 128
    smu = f(inputs["shift_mu"][0])
    cols = [smu.reshape(2, NST, 128).transpose(2, 1, 0).reshape(128, 2 * NST)]
    per = [f(inputs["w0"][0])[0], f(inputs["w0"][0])[1], f(inputs["a0"][0])[0], f(inputs["a0"][0])[1],
           f(inputs["k_k"][0]), f(inputs["k_a"][0]), f(inputs["r_k"][0]), f(inputs["gn_w"][0]), f(inputs["gn_b"][0])]
    per = np.stack([p.reshape(NHP, 128) for p in per], axis=-1)
    cols.append(per.transpose(1, 0, 2).reshape(128, 9 * NHP))
    qg = np.tile(f(inputs["q_norm_g"][0]), 2)[:, None]
    kg = np.tile(f(inputs["k_norm_g"][0]), 2)[:, None]
    pc = np.ascontiguousarray(np.concatenate(cols + [qg, kg], axis=1).astype(np.float32))
    shared = dict(
        w_in=f(inputs["w_in"][0]), w_br=f(inputs["w_branch_rwkv"][0]), w_ba=f(inputs["w_branch_attn"][0]),
        w_out=f(inputs["w_out"][0]), norm_g=f(inputs["norm_g"][0]), final_norm_g=f(inputs["final_norm_g"]),
        pc=pc, w_up=f(inputs["w_up"][0]).reshape(2 * LORA, c.RW), a_up=f(inputs["a_up"][0]).reshape(2 * LORA, c.RW),
        **consts)
    x = np.asarray(inputs["x"], dtype=np.float32)
    maps = []
    for i in range(n_cores):
        m = dict(shared)
        m["x"] = np.ascontiguousarray(x[i * c.BPC:(i + 1) * c.BPC])
        maps.append(m)
    return maps


def kernel(**inputs):
    cfg = Cfg()
    n = 8
    nc = build(cfg)
    in_maps = make_in_maps(cfg, inputs, n)
    res = run_bass_kernel_spmd(nc, in_maps, core_ids=list(range(n)))
    return np.concatenate([r["y"] for r in res.results], axis=0)
```

```python
import math
from contextlib import ExitStack
import numpy as np
import ml_dtypes
import concourse.bass as bass
import concourse.mybir as mybir
from concourse.bass_utils import run_bass_kernel_spmd

F32 = mybir.dt.float32
BF16 = mybir.dt.bfloat16
ALU = mybir.AluOpType
AF = mybir.ActivationFunctionType
AX = mybir.AxisListType

NORM_EPS = 1e-6
GN_EPS = 64e-5
HD = 64
LORA = 64
CL = 64


class Cfg:
    def __init__(self, T=2048, D=2048, RH=16, QH=16, KVH=4, BPC=2, GRID_W=64, debug=False):
        self.T, self.D, self.RH, self.QH, self.KVH, self.BPC, self.GRID_W = T, D, RH, QH, KVH, BPC, GRID_W
        self.debug = debug
        self.RW = RH * HD
        self.AW = QH * HD
        self.KVW = KVH * HD
        self.GROUP = QH // KVH
        self.SHIFT_W = 3 * self.RW + 4 * LORA
        self.O_R, self.O_K, self.O_V = 0, self.RW, 2 * self.RW
        self.O_WD, self.O_AD = 3 * self.RW, 3 * self.RW + 2 * LORA
        self.O_ZR = self.SHIFT_W
        self.O_Q = self.O_ZR + self.RW
        self.O_AK = self.O_Q + self.AW
        self.O_AV = self.O_AK + self.KVW
        self.O_ZA = self.O_AV + self.KVW
        self.O_GR = self.O_ZA + self.AW
        self.O_GA = self.O_GR + D
        self.D_IN = self.O_GA + D
        self.KC = D // 128
        self.NCT = self.D_IN // 128
        self.TC = min(512, T)
        self.NTC = T // self.TC
        self.NT = T // 128
        self.NCH = T // CL


class Buf:
    __slots__ = ("name", "w", "r", "chan")

    def __init__(self, name):
        self.name, self.w, self.r, self.chan = name, None, {}, None


class Chan:
    __slots__ = ("sem", "cnt", "key")


class _Rec:
    def __getattr__(self, name):
        def f(*a, **k):
            self.call = (name, a, k)
            return self
        return f


class Prog:
    ENG = ("pe", "dve", "act", "pool", "sp")

    def __init__(self, nc, stack):
        self.nc, self.stack = nc, stack
        self.ops = {e: [] for e in self.ENG}
        self.sem = {e: stack.enter_context(nc.semaphore("s_" + e)) for e in self.ENG}
        self.cnt = {e: 0 for e in self.ENG}
        self.pending = {e: False for e in self.ENG}
        self.known = {e: {} for e in self.ENG}
        self.chans = {}
        self.chan_of = {}
        self.NCHAN = 16
        self.nbuf = 0

    def buf(self, name=None):
        self.nbuf += 1
        return Buf(name or "b%d" % self.nbuf)

    def _chan(self, b, dedicated=False):
        if b.chan is None and dedicated:
            c = Chan()
            c.key = "cd_" + b.name
            c.sem = self.stack.enter_context(self.nc.semaphore(c.key))
            c.cnt = 0
            self.chans[c.key] = c
            b.chan = c
        if b.chan is None:
            if b.name not in self.chan_of:
                idx = len(self.chan_of) % self.NCHAN
                key = "c_%d" % idx
                if key not in self.chans:
                    c = Chan()
                    c.key = key
                    c.sem = self.stack.enter_context(self.nc.semaphore(key))
                    c.cnt = 0
                    self.chans[key] = c
                self.chan_of[b.name] = key
            b.chan = self.chans[self.chan_of[b.name]]
        return b.chan

    def _waits(self, eng, reads, writes):
        need = {}

        def add(ev, raw):
            if ev is None:
                return
            key, val = ev
            if key == eng and eng == "pe":
                return
            if key in self.chans:
                val = self.chans[key].cnt * 16
            if need.get(key, 0) < val:
                need[key] = val

        for b in reads:
            add(b.w, True)
        for b in writes:
            add(b.w, False)
            for k, v in b.r.items():
                add((k, v), False)
        out = []
        kn = self.known[eng]
        for key, val in need.items():
            if kn.get(key, 0) >= val:
                continue
            kn[key] = val
            sem = self.chans[key].sem if key in self.chans else self.sem[key]
            out.append((sem, val))
        return out

    def _mark(self, ev, reads, writes):
        k, v = ev
        for b in reads:
            if b.r.get(k, 0) < v:
                b.r[k] = v
        for b in writes:
            b.w = ev
            b.r = {}

    def op(self, eng, fn, reads=(), writes=(), inc=True):
        rec = _Rec()
        fn(rec)
        name_, a_, k_ = rec.call
        fn = lambda e, name_=name_, a_=a_, k_=k_: getattr(e, name_)(*a_, **k_)
        waits = self._waits(eng, reads, writes)
        if inc:
            self.cnt[eng] += 1
            ev = (eng, self.cnt[eng])
            self.pending[eng] = False
            self.ops[eng].append((waits, fn, (self.sem[eng], 1)))
        else:
            ev = (eng, self.cnt[eng] + 1)
            self.pending[eng] = True
            self.ops[eng].append((waits, fn, None))
        self._mark(ev, reads, writes)

    def dma(self, q, out_ap, in_ap, reads, writes, chanbuf):
        waits = self._waits(q, reads, writes)
        c = self._chan(chanbuf, dedicated=(q == "pool"))
        c.cnt += 1
        ev = (c.key, c.cnt * 16)
        self.ops[q].append((waits, lambda e: e.dma_start(out=out_ap, in_=in_ap), (c.sem, 16)))
        self._mark(ev, reads, writes)

    def finish(self):
        for e in self.ENG:
            if self.pending[e]:
                raise RuntimeError("engine %s ends with a non-incrementing op" % e)
        waits = []
        for e in self.ENG:
            if e != "sp" and self.cnt[e] > 0:
                waits.append((self.sem[e], self.cnt[e]))
        for c in self.chans.values():
            if c.cnt:
                waits.append((c.sem, c.cnt * 16))
        self.ops["sp"].append((waits, None, None))

    def emit(self):
        nc = self.nc
        engmap = {"pe": "tensor", "dve": "vector", "act": "scalar", "pool": "gpsimd", "sp": "sync"}
        with nc.Block() as block:
            for e in self.ENG:
                ops = self.ops[e]

                def body(eng, ops=ops):
                    for waits, fn, inc in ops:
                        for sem, val in waits:
                            eng.wait_ge(sem, val)
                        if fn is None:
                            continue
                        try:
                            ins = fn(eng)
                        except Exception:
                            print("FAILED OP:", getattr(fn, "__defaults__", None))
                            raise
                        if inc is not None:
                            ins.then_inc(inc[0], inc[1])

                getattr(block, engmap[e])(body)


def rev_ap(ap):
    pat = [list(p) for p in ap.ap]
    step, n = pat[-1]
    pat[-1] = [-step, n]
    return bass.AP(tensor=ap.tensor, offset=ap.offset + step * (n - 1), ap=pat)


def build(cfg):
    c = cfg
    T, D, KC, TC, NTC, NT, RW = c.T, c.D, c.KC, c.TC, c.NTC, c.NT, c.RW
    nc = bass.Bass("TRN2", target_bir_lowering=False)
    st = ExitStack()
    P = Prog(nc, st)

    def din(name, shape, dt=F32):
        return nc.dram_tensor(name, list(shape), dt, kind="ExternalInput").ap()

    def dscr(name, shape, dt, dbg=False):
        kind = "ExternalOutput" if (dbg and c.debug) else "Internal"
        return nc.dram_tensor(name, list(shape), dt, kind=kind).ap()

    x_d = din("x", [c.BPC, T, D])
    w_in_d = din("w_in", [D, c.D_IN])
    w_br_d = din("w_br", [RW, D])
    w_ba_d = din("w_ba", [c.AW, D])
    w_out_d = din("w_out", [D, D])
    normg_d = din("norm_g", [D])
    fing_d = din("final_norm_g", [D])
    NST = c.SHIFT_W // 128
    NHP = RW // 128
    NPC = 2 * NST + 9 * NHP + 2
    pc_d = din("pc", [128, NPC])
    wup_d = din("w_up", [2 * LORA, RW])
    aup_d = din("a_up", [2 * LORA, RW])
    cmask_d = din("c_masks", [64, 3, 512])
    cident_d = din("c_ident", [128, 128])
    cj_d = din("c_j64", [64, 64])
    crot_d = din("c_rot", [128, 128])
    ccos_d = din("c_cos", [128, T])
    csin_d = din("c_sin", [128, T])
    cbd_d = din("c_bdones", [128, 128])
    cm0_d = din("c_m0", [128, T])
    y_d = nc.dram_tensor("y", [c.BPC, T, D], F32, kind="ExternalOutput").ap()

    wq_in = dscr("wq_in", [c.NCT, 128, KC, 128], BF16)
    wq_br = dscr("wq_br", [RW, D], BF16)
    wq_ba = dscr("wq_ba", [c.AW, D], BF16)
    wq_out = dscr("wq_out", [D, D], BF16)
    NPT = c.O_GR // 128
    proj_s = dscr("proj_s", [NPT, 128, T], F32)
    hT_s = dscr("hT_s", [c.BPC, D, T], BF16, dbg=True)
    orT_s = dscr("orT_s", [c.BPC, RW, T], BF16, dbg=True)
    oaT_s = dscr("oaT_s", [c.BPC, c.AW, T], BF16, dbg=True)

    def sb(name, shape, dt=F32):
        return st.enter_context(nc.sbuf_tensor(name, list(shape), dt))

    def ps(name, shape, dt=F32):
        return st.enter_context(nc.psum_tensor(name, list(shape), dt))

    B_wq = P.buf("wq")
    WQG = 12
    B_wqg = [P.buf("wqg%d" % g) for g in range((c.NCT + WQG - 1) // WQG)]
    B_hTs = [P.buf("hTs%d" % b) for b in range(c.BPC)]
    B_orTs = [P.buf("orTs%d" % b) for b in range(c.BPC)]
    B_oaTs = [P.buf("oaTs%d" % b) for b in range(c.BPC)]
    B_proj = [P.buf("proj%d" % i) for i in range(NPT)]
    B_y = P.buf("y")

    ident_f = sb("ident_f", [128, 128]); B_identf = P.buf("identf")
    ident_b = sb("ident_b", [128, 128], BF16); B_identb = P.buf("identb")
    pc = sb("pc_sb", [128, NPC]); B_pc = P.buf("pc")
    ccT = sb("ccT", [128, NST]); B_cc = P.buf("cc")
    omka = sb("omka", [128, NHP]); B_omka = P.buf("omka")
    bdones = sb("bdones", [128, 128]); B_bd = P.buf("bd")
    rotm = sb("rotm", [128, 128]); B_rot = P.buf("rot")
    masks = sb("masks", [64, 3, 512]); B_mask = P.buf("mask")
    I8 = sb("I8", [64, 512]); B_I8 = P.buf("I8")
    j64 = sb("j64", [64, 64]); B_j64 = P.buf("j64")
    P.dma("sp", ident_f[:, :], cident_d[:, :], [], [B_identf], B_identf)
    P.dma("sp", pc[:, :], pc_d[:, :], [], [B_pc], B_pc)
    P.dma("sp", bdones[:, :], cbd_d[:, :], [], [B_bd], B_bd)
    P.dma("sp", rotm[:, :], crot_d[:, :], [], [B_rot], B_rot)
    P.dma("sp", masks[:, :, :], cmask_d[:, :, :], [], [B_mask], B_mask)
    P.dma("sp", j64[:, :], cj_d[:, :], [], [B_j64], B_j64)
    P.op("dve", lambda e: e.tensor_copy(out=ident_b[:, :], in_=ident_f[:, :]), [B_identf], [B_identb])
    for g8 in range(8):
        P.op("dve", lambda e, g8=g8: e.tensor_copy(out=I8[:, g8 * 64:(g8 + 1) * 64], in_=ident_f[0:64, 0:64]),
             [B_identf], [B_I8])
    pcv = pc[:, 0:2 * NST].rearrange("p (n j) -> p n j", j=2)
    P.op("dve", lambda e: e.tensor_tensor(out=ccT[:, :], in0=pcv[:, :, 0], in1=pcv[:, :, 1], op=ALU.add),
         [B_pc], [B_cc])
    P.op("dve", lambda e: e.tensor_scalar(out=ccT[:, :], in0=ccT[:, :], scalar1=-1.0, scalar2=1.0,
                                          op0=ALU.mult, op1=ALU.add), [B_cc], [B_cc])
    PCB = 2 * NST

    def pcc(hp, j):
        return pc[:, PCB + 9 * hp + j:PCB + 9 * hp + j + 1]

    for hp in range(NHP):
        P.op("dve", lambda e, hp=hp: e.tensor_scalar(out=omka[:, hp:hp + 1], in0=pcc(hp, 5), scalar1=-1.0,
                                                     scalar2=1.0, op0=ALU.mult, op1=ALU.add), [B_pc], [B_omka])

    KG = 4
    for ct in range(c.NCT):
        for k0 in range(0, KC, KG):
            k1 = min(KC, k0 + KG)
            P.dma("pool", wq_in[ct][:, k0:k1, :],
                  w_in_d[k0 * 128:k1 * 128, ct * 128:(ct + 1) * 128].rearrange("(kc kp) c -> kp kc c", kp=128),
                  [], [B_wqg[ct // WQG]], B_wqg[ct // WQG])
    for dst_, src_, rows in ((wq_br, w_br_d, RW), (wq_ba, w_ba_d, c.AW), (wq_out, w_out_d, D)):
        for r0 in range(0, rows, 256):
            r1 = min(rows, r0 + 256)
            P.dma("pool", dst_[r0:r1, :], src_[r0:r1, :], [], [B_wq], B_wq)

    ARENA_BYTES = 196 * 1024
    arena_t = sb("arena", [128, ARENA_BYTES // 2], BF16)

    class Arena:
        def __init__(self):
            self.off, self.prev, self.live = 0, {}, []

        def reset(self):
            for b in self.live:
                evs = list(b.r.items())
                if b.w is not None:
                    evs.append(b.w)
                for k, v in evs:
                    if self.prev.get(k, 0) < v:
                        self.prev[k] = v
            self.live, self.off = [], 0

        def alloc(self, name, fshape, dt=F32, parts=128):
            n = 1
            for s in fshape:
                n *= s
            nbytes = n * (4 if dt == F32 else 2)
            nbytes = (nbytes + 63) // 64 * 64
            assert self.off + nbytes <= ARENA_BYTES, ("arena overflow", name, self.off, nbytes)
            v = arena_t[0:parts, self.off // 2:(self.off + n * (4 if dt == F32 else 2)) // 2]
            if dt == F32:
                v = v.bitcast(F32)
            if len(fshape) == 2:
                v = v.rearrange("p (a b) -> p a b", a=fshape[0])
            elif len(fshape) == 3:
                v = v.rearrange("p (a b c) -> p a b c", a=fshape[0], b=fshape[1])
            self.off += nbytes
            b = P.buf(name)
            b.r = dict(self.prev)
            self.live.append(b)
            return v, b

    AR = Arena()
    dbg_cnt = [0]

    def dbg(name, ap, bb, shape):
        if not c.debug:
            return
        t = nc.dram_tensor("dbg_" + name, list(shape), F32, kind="ExternalOutput").ap()
        P.dma("sp", t, ap, [bb], [P.buf("dbgd_" + name)], bb)

    psb = [ps("psb%d" % i, [128, 512]) for i in range(8)]
    B_ps = [P.buf("ps%d" % i) for i in range(8)]

    def phase_a_proj(b):
        AR.reset()
        hT, B_hT = AR.alloc("hT", [KC, T], BF16)
        gbc, B_gbc = AR.alloc("gbc", [D])
        P.dma("sp", gbc, normg_d.partition_broadcast(128), [], [B_gbc], B_gbc)
        xt, B_xt, xn, B_xn, st1, B_st1 = [], [], [], [], [], []
        for i in range(2):
            v, bb = AR.alloc("xt%d" % i, [D]); xt.append(v); B_xt.append(bb)
            v, bb = AR.alloc("xn%d" % i, [D], BF16); xn.append(v); B_xn.append(bb)
            v, bb = AR.alloc("st1_%d" % i, [4]); st1.append(v); B_st1.append(bb)
        junk, B_junk = AR.alloc("junk", [D], BF16)
        for it in range(NT):
            i = it % 2
            P.dma("sp", xt[i], x_d[b, it * 128:(it + 1) * 128, :], [], [B_xt[i]], B_xt[i])
            P.op("act", lambda e, i=i: e.activation(out=junk, in_=xt[i], func=AF.Square, accum_out=st1[i][:, 0:1]),
                 [B_xt[i]], [B_junk, B_st1[i]])
            P.op("act", lambda e, i=i: e.activation(out=st1[i][:, 1:2], in_=st1[i][:, 0:1], func=AF.Sqrt,
                                                     scale=1.0 / D, bias=NORM_EPS), [B_st1[i]], [B_st1[i]])
            P.op("dve", lambda e, i=i: e.reciprocal(out=st1[i][:, 2:3], in_=st1[i][:, 1:2]), [B_st1[i]], [B_st1[i]])
            P.op("dve", lambda e, i=i: e.scalar_tensor_tensor(out=xn[i], in0=xt[i], scalar=st1[i][:, 2:3], in1=gbc,
                                                               op0=ALU.mult, op1=ALU.mult),
                 [B_xt[i], B_st1[i], B_gbc], [B_xn[i]])
            for k0 in range(0, KC, 4):
                nk = min(4, KC - k0)
                pi = (k0 // 4) % 2
                pt = psb[pi][:, :].bitcast(BF16)
                for kk in range(nk):
                    P.op("pe", lambda e, i=i, kk=kk, k0=k0, pt=pt: e.transpose(
                        out=pt[:, kk * 128:(kk + 1) * 128], in_=xn[i][:, (k0 + kk) * 128:(k0 + kk + 1) * 128],
                        identity=ident_b[:, :]), [B_xn[i], B_identb], [B_ps[pi]], inc=(kk == nk - 1))
                src = pt[:, 0:nk * 128].rearrange("p (k t) -> p k t", k=nk)
                dst = hT[:, k0:k0 + nk, it * 128:(it + 1) * 128]
                if (k0 // 4) % 2 == 0:
                    P.op("act", lambda e, src=src, dst=dst: e.activation(out=dst, in_=src, func=AF.Copy),
                         [B_ps[pi]], [B_hT])
                else:
                    P.op("dve", lambda e, src=src, dst=dst: e.tensor_copy(out=dst, in_=src), [B_ps[pi]], [B_hT])
        for k0 in range(0, KC, KG):
            k1 = min(KC, k0 + KG)
            P.dma("sp", hT_s[b][k0 * 128:k1 * 128, :].rearrange("(kc kp) t -> kp kc t", kp=128), hT[:, k0:k1, :],
                  [B_hT], [B_hTs[b]], B_hT)

        wt, B_wt, pdst, B_pdst = [], [], [], []
        for i in range(2):
            v, bb = AR.alloc("wt%d" % i, [KC, 128], BF16); wt.append(v); B_wt.append(bb)
            v, bb = AR.alloc("pdst%d" % i, [T]); pdst.append(v); B_pdst.append(bb)
        ypad, B_ypad = AR.alloc("ypad", [T + 2])
        P.op("dve", lambda e: e.memset(ypad[:, 0:1], 0.0), [], [B_ypad])
        P.op("dve", lambda e: e.memset(ypad[:, T + 1:T + 2], 0.0), [], [B_ypad])
        ct_wd = c.O_WD // 128
        silu_tiles = set(range(c.O_ZR // 128, c.O_Q // 128)) | set(range(c.O_ZA // 128, c.O_GR // 128))
        for ct in range(NPT):
            i = ct % 2
            P.dma("sp", wt[i], wq_in[ct], [B_wqg[ct // WQG]], [B_wt[i]], B_wt[i])
            shifted = ct < NST
            dst = pdst[i]
            for tc in range(NTC):
                pb = (ct * NTC + tc) % 2
                for kc in range(KC):
                    P.op("pe", lambda e, i=i, kc=kc, tc=tc, pb=pb: e.matmul(
                        psb[pb][:, 0:TC], lhsT=wt[i][:, kc, :], rhs=hT[:, kc, tc * TC:(tc + 1) * TC],
                        start=(kc == 0), stop=(kc == KC - 1)), [B_wt[i], B_hT], [B_ps[pb]], inc=(kc == KC - 1))
                if shifted:
                    P.op("act", lambda e, tc=tc, pb=pb: e.activation(out=ypad[:, 1 + tc * TC:1 + (tc + 1) * TC],
                                                                     in_=psb[pb][:, 0:TC], func=AF.Copy),
                         [B_ps[pb]], [B_ypad])
                else:
                    fn = AF.Silu if ct in silu_tiles else AF.Copy
                    P.op("act", lambda e, tc=tc, pb=pb, dst=dst, fn=fn: e.activation(
                        out=dst[:, tc * TC:(tc + 1) * TC], in_=psb[pb][:, 0:TC], func=fn), [B_ps[pb]], [B_pdst[i]])
            if shifted:
                P.op("dve", lambda e, ct=ct, dst=dst: e.tensor_scalar(out=dst, in0=ypad[:, 1:T + 1],
                                                                      scalar1=ccT[:, ct:ct + 1], scalar2=None,
                                                                      op0=ALU.mult), [B_ypad, B_cc], [B_pdst[i]])
                P.op("dve", lambda e, ct=ct, dst=dst: e.scalar_tensor_tensor(
                    out=dst, in0=ypad[:, 0:T], scalar=pc[:, 2 * ct:2 * ct + 1], in1=dst, op0=ALU.mult, op1=ALU.add),
                    [B_ypad, B_pc, B_pdst[i]], [B_pdst[i]])
                P.op("dve", lambda e, ct=ct, dst=dst: e.scalar_tensor_tensor(
                    out=dst, in0=ypad[:, 2:T + 2], scalar=pc[:, 2 * ct + 1:2 * ct + 2], in1=dst, op0=ALU.mult,
                    op1=ALU.add), [B_ypad, B_pc, B_pdst[i]], [B_pdst[i]])
                if ct == ct_wd:
                    P.op("act", lambda e, dst=dst: e.activation(out=dst, in_=dst, func=AF.Tanh),
                         [B_pdst[i]], [B_pdst[i]])
            P.dma("sp", proj_s[ct], dst, [B_pdst[i]], [B_proj[ct]], B_pdst[i])

    stop = getattr(c, "stop", None)
    NCH = c.NCH
    G = 4
    NG = NCH // G

    def rwkv_phase(b):
        AR.reset()
        A = AR.alloc
        wdT, B_wd = A("wdT", [T]); adT, B_ad = A("adT", [T])
        rT, B_r = A("rT", [T]); kT, B_k = A("kT", [T]); vT, B_v = A("vT", [T]); kkT, B_kk = A("kkT", [T])
        vrT, B_vr = A("vrT", [T])
        W_, B_W = A("W_", [T]); A_, B_A = A("A_", [T]); KE_, B_KE = A("KE_", [T]); G0_, B_G0 = A("G0_", [T])
        GM_, B_GM = A("GM_", [T + 1]); CS_, B_CS = A("CS_", [T])
        oT = []; B_oT = []
        for d in range(2):
            v, bb = A("oT%d" % d, [T]); oT.append(v); B_oT.append(bb)
        m0, B_m0 = A("m0", [T]); m1, B_m1 = A("m1", [T])
        wup, B_wup = A("wup", [RW]); aup, B_aup = A("aup", [RW])
        gL, B_gL = A("gL", [NCH])
        S_, B_S = A("S_", [64]); Sg_, B_Sg = A("Sg_", [64]); Sbd, B_Sbd = A("Sbd", [128])
        BTt, B_BTt, KTt, B_KTt, Vt, B_Vt, CTm, B_CTm, DTm, B_DTm, Nm, B_Nm, BVs, B_BVs = ([] for _ in range(14))
        for i in range(2):
            for lst, bl, nm, shp in ((BTt, B_BTt, "BTt", [G, 128]), (KTt, B_KTt, "KTt", [G, 128]),
                                     (Vt, B_Vt, "Vt", [G, 128]), (CTm, B_CTm, "CTm", [512]),
                                     (DTm, B_DTm, "DTm", [512]), (Nm, B_Nm, "Nm", [512]), (BVs, B_BVs, "BVs", [512])):
                v, bb = A("%s%d" % (nm, i), shp, F32, 64); lst.append(v); bl.append(bb)
        ATm, B_ATm = A("ATm", [512], F32, 64); Am, B_Am = A("Am", [512], F32, 64)
        BTm, B_BTm = A("BTm", [512], F32, 64)
        Pq, B_Pq, PTq, B_PTq = [], [], [], []
        for i in range(2):
            v, bb = A("Pq%d" % i, [512], F32, 64); Pq.append(v); B_Pq.append(bb)
            v, bb = A("PTq%d" % i, [512], F32, 64); PTq.append(v); B_PTq.append(bb)
        RHSs, B_RHS = A("RHSs", [128], F32, 64); Us, B_Us = A("Us", [128], F32, 64)
        ob, B_ob = A("ob", [T], BF16)

        P.dma("sp", m0, cm0_d[:, :], [], [B_m0], B_m0)
        P.op("dve", lambda e: e.tensor_scalar(out=m1, in0=m0, scalar1=-1.0, scalar2=1.0, op0=ALU.mult, op1=ALU.add),
             [B_m0], [B_m1])
        P.dma("sp", wup, wup_d[:, :], [], [B_wup], B_wup)
        P.dma("sp", aup, aup_d[:, :], [], [B_aup], B_aup)
        P.dma("sp", wdT, proj_s[c.O_WD // 128], [B_proj[c.O_WD // 128]], [B_wd], B_wd)
        P.dma("sp", adT, proj_s[c.O_AD // 128], [B_proj[c.O_AD // 128]], [B_ad], B_ad)
        P.op("dve", lambda e: e.memset(GM_[:, 0:1], 1.0), [], [B_GM])

        def dv(fn, reads, writes):
            P.op("dve", fn, reads, writes)

        for hp in range(NHP):
            hs = [slice(0, 64), slice(64, 128)]
            for buf, bb, off in ((rT, B_r, c.O_R), (kT, B_k, c.O_K), (vT, B_v, c.O_V)):
                ctl = off // 128 + hp
                P.dma("sp", buf, proj_s[ctl], [B_proj[ctl]], [bb], bb)
            dv(lambda e, hp=hp: e.tensor_scalar(out=kkT, in0=kT, scalar1=pcc(hp, 4), scalar2=None, op0=ALU.mult),
               [B_k, B_pc], [B_kk])
            dv(lambda e: e.tensor_tensor(out=W_, in0=kkT, in1=kkT, op=ALU.mult), [B_kk], [B_W])
            for tc in range(NTC):
                sl = slice(tc * TC, (tc + 1) * TC)
                P.op("pe", lambda e, sl=sl: e.matmul(psb[0][:, 0:TC], lhsT=bdones[:, :], rhs=W_[:, sl], start=True,
                                                     stop=True), [B_bd, B_W], [B_ps[0]])
                dv(lambda e, sl=sl: e.tensor_scalar_max(out=A_[:, sl], in0=psb[0][:, 0:TC], scalar1=1e-24),
                   [B_ps[0]], [B_A])
            P.op("act", lambda e: e.activation(out=A_, in_=A_, func=AF.Sqrt), [B_A], [B_A])
            dv(lambda e: e.reciprocal(out=A_, in_=A_), [B_A], [B_A])
            dv(lambda e: e.tensor_tensor(out=kkT, in0=kkT, in1=A_, op=ALU.mult), [B_kk, B_A], [B_kk])
            dv(lambda e: e.tensor_copy(out=vrT, in_=rev_ap(vT)), [B_v], [B_vr])

            for d in range(2):
                dsl = slice(d * 64, (d + 1) * 64)
                rsrc = (lambda ap: ap) if d == 0 else rev_ap
                vsrc = vT if d == 0 else vrT
                B_vsrc = B_v if d == 0 else B_vr
                for tc in range(NTC):
                    sl = slice(tc * TC, (tc + 1) * TC)
                    osl = sl if d == 0 else slice(T - (tc + 1) * TC, T - tc * TC)
                    P.op("pe", lambda e, sl=sl, hp=hp, dsl=dsl: e.matmul(
                        psb[0][:, 0:TC], lhsT=wup[dsl, hp * 128:(hp + 1) * 128], rhs=wdT[dsl, sl], start=True,
                        stop=True), [B_wup, B_wd], [B_ps[0]])
                    P.op("act", lambda e, osl=osl, hp=hp, d=d: e.activation(
                        out=W_[:, osl], in_=rsrc(psb[0][:, 0:TC]), func=AF.Sigmoid, bias=pcc(hp, 0 + d), scale=1.0),
                        [B_ps[0], B_pc], [B_W])
                    P.op("pe", lambda e, sl=sl, hp=hp, dsl=dsl: e.matmul(
                        psb[1][:, 0:TC], lhsT=aup[dsl, hp * 128:(hp + 1) * 128], rhs=adT[dsl, sl], start=True,
                        stop=True), [B_aup, B_ad], [B_ps[1]])
                    P.op("act", lambda e, osl=osl, hp=hp, d=d: e.activation(
                        out=A_[:, osl], in_=rsrc(psb[1][:, 0:TC]), func=AF.Sigmoid, bias=pcc(hp, 2 + d), scale=1.0),
                        [B_ps[1], B_pc], [B_A])
                P.op("act", lambda e: e.activation(out=W_, in_=W_, func=AF.Exp, scale=-math.exp(-0.5)), [B_W], [B_W])
                if hp == 0 and b == 0:
                    dbg("w%d" % d, W_, B_W, [128, T]); dbg("a%d" % d, A_, B_A, [128, T])
                dv(lambda e, hp=hp: e.tensor_scalar(out=KE_, in0=A_, scalar1=pcc(hp, 5), scalar2=omka[:, hp:hp + 1],
                                                    op0=ALU.mult, op1=ALU.add), [B_A, B_pc, B_omka], [B_KE])
                dv(lambda e: e.tensor_tensor(out=KE_, in0=KE_, in1=rsrc(kT), op=ALU.mult), [B_KE, B_k], [B_KE])
                dv(lambda e, hp=hp: e.scalar_tensor_tensor(out=G0_, in0=KE_, scalar=pcc(hp, 6), in1=rsrc(rT),
                                                           op0=ALU.mult, op1=ALU.mult), [B_KE, B_pc, B_r], [B_G0])
                if d == 0:
                    dv(lambda e: e.tensor_copy(out=CS_, in_=G0_), [B_G0], [B_CS])
                else:
                    dv(lambda e: e.tensor_tensor(out=CS_, in0=CS_, in1=rev_ap(G0_), op=ALU.add), [B_CS, B_G0], [B_CS])
                dv(lambda e: e.tensor_tensor(out=A_, in0=A_, in1=rsrc(kkT), op=ALU.mult), [B_A, B_kk], [B_A])
                dv(lambda e: e.tensor_tensor(out=G0_, in0=W_, in1=m0, op=ALU.mult), [B_W, B_m0], [B_G0])
                dv(lambda e: e.tensor_tensor(out=W_, in0=W_, in1=G0_, op=ALU.subtract), [B_W, B_G0], [B_W])
                dv(lambda e: e.tensor_tensor_scan(out=GM_[:, 1:T + 1], data0=G0_, data1=W_, initial=0.0,
                                                  op0=ALU.mult, op1=ALU.add), [B_G0, B_W], [B_GM])
                dv(lambda e: e.tensor_tensor(out=G0_, in0=GM_[:, 0:T], in1=m0, op=ALU.mult), [B_GM, B_m0], [B_G0])
                dv(lambda e: e.tensor_tensor(out=G0_, in0=G0_, in1=m1, op=ALU.add), [B_G0, B_m1], [B_G0])
                dv(lambda e: e.scalar_tensor_tensor(out=G0_, in0=rsrc(kkT), scalar=-1.0, in1=G0_, op0=ALU.mult,
                                                    op1=ALU.mult), [B_kk, B_G0], [B_G0])
                dv(lambda e: e.tensor_tensor(out=W_, in0=rsrc(rT), in1=GM_[:, 1:T + 1], op=ALU.mult),
                   [B_r, B_GM], [B_W])
                dv(lambda e: e.tensor_copy(out=gL, in_=GM_[:, 1:T + 1].rearrange("p (n l) -> p n l", l=CL)[:, :, CL - 1]),
                   [B_GM], [B_gL])
                dv(lambda e: e.reciprocal(out=GM_[:, 1:T + 1], in_=GM_[:, 1:T + 1]), [B_GM], [B_GM])
                dv(lambda e: e.tensor_tensor(out=A_, in0=A_, in1=GM_[:, 1:T + 1], op=ALU.mult), [B_A, B_GM], [B_A])
                dv(lambda e: e.tensor_tensor(out=KE_, in0=KE_, in1=GM_[:, 1:T + 1], op=ALU.mult), [B_KE, B_GM], [B_KE])
                at, bt, kt, rt = G0_, A_, KE_, W_
                B_at, B_bt, B_kt, B_rt = B_G0, B_A, B_KE, B_W
                dv(lambda e: e.memset(S_, 0.0), [], [B_S])
                dv(lambda e: e.memset(Sbd, 0.0), [], [B_Sbd])

                def precompute(g, s):
                    t0 = g * G * CL
                    for src, B_src, dstl, B_dstl, pb in ((bt, B_bt, BTt, B_BTt, 3), (kt, B_kt, KTt, B_KTt, 4),
                                                         (vsrc, B_vsrc, Vt, B_Vt, 3)):
                        for j in range(G):
                            P.op("pe", lambda e, src=src, j=j, pb=pb: e.matmul(
                                psb[pb][0:64, j * 128:(j + 1) * 128], lhsT=src[:, t0 + j * CL:t0 + (j + 1) * CL],
                                rhs=ident_f[:, :], start=True, stop=True), [B_src, B_identf], [B_ps[pb]],
                                inc=(j == G - 1))
                        P.op("act", lambda e, dstl=dstl, pb=pb: e.activation(
                            out=dstl[s], in_=psb[pb][0:64, :].rearrange("p (g c) -> p g c", g=G), func=AF.Copy),
                            [B_ps[pb]], [B_dstl[s]])
                        yield
                    if stop == "P1":
                        return
                    specs = ((0, bt, B_bt, at, B_at), (1, at, B_at, bt, B_bt), (2, kt, B_kt, at, B_at),
                             (3, bt, B_bt, rt, B_rt), (4, kt, B_kt, rt, B_rt))
                    for h in range(2):
                        for pb, L, B_L, R, B_R in specs:
                            for j in range(G):
                                tsl = slice(t0 + j * CL, t0 + (j + 1) * CL)
                                col = (j * 2 + h) * 64
                                P.op("pe", lambda e, pb=pb, L=L, R=R, tsl=tsl, h=h, col=col: e.matmul(
                                    psb[pb][0:64, col:col + 64], lhsT=L[hs[h], tsl], rhs=R[hs[h], tsl], start=True,
                                    stop=True), [B_L, B_R], [B_ps[pb]], inc=(j == G - 1 and h == 1))
                    yield
                    dv(lambda e: e.tensor_tensor(out=ATm, in0=psb[0][0:64, :], in1=masks[:, 0, :], op=ALU.mult),
                       [B_ps[0], B_mask], [B_ATm])
                    dv(lambda e: e.tensor_tensor(out=Am, in0=psb[1][0:64, :], in1=masks[:, 1, :], op=ALU.mult),
                       [B_ps[1], B_mask], [B_Am])
                    dv(lambda e: e.tensor_tensor(out=BTm, in0=psb[2][0:64, :], in1=masks[:, 0, :], op=ALU.mult),
                       [B_ps[2], B_mask], [B_BTm])
                    dv(lambda e: e.tensor_tensor(out=CTm[s], in0=psb[3][0:64, :], in1=masks[:, 2, :], op=ALU.mult),
                       [B_ps[3], B_mask], [B_CTm[s]])
                    dv(lambda e: e.tensor_tensor(out=DTm[s], in0=psb[4][0:64, :], in1=masks[:, 2, :], op=ALU.mult),
                       [B_ps[4], B_mask], [B_DTm[s]])
                    yield
                    dv(lambda e: e.tensor_tensor(out=Nm[s], in0=ATm, in1=I8[:, :], op=ALU.add), [B_ATm, B_I8], [B_Nm[s]])
                    if stop == "P2":
                        return
                    for j in range(G):
                        for h in range(2):
                            col = (j * 2 + h) * 64
                            P.op("pe", lambda e, j=j, h=h, col=col: e.matmul(
                                psb[2][0:64, col:col + 64], lhsT=BTm[:, col:col + 64],
                                rhs=Vt[s][:, j, h * 64:(h + 1) * 64], start=True, stop=True),
                                [B_BTm, B_Vt[s]], [B_ps[2]], inc=(j == G - 1 and h == 1))
                    P.op("act", lambda e: e.activation(out=BVs[s], in_=psb[2][0:64, :], func=AF.Copy),
                         [B_ps[2]], [B_BVs[s]])
                    yield
                    if stop == "P3":
                        return
                    Pc, B_Pc, PTc, B_PTc = Am, B_Am, ATm, B_ATm
                    for lev in range(5):
                        q = lev % 2
                        last = lev == 4
                        for p8 in range(2 * G):
                            col = p8 * 64
                            P.op("pe", lambda e, Pc=Pc, PTc=PTc, col=col: e.matmul(
                                psb[0][0:64, col:col + 64], lhsT=PTc[:, col:col + 64], rhs=Pc[:, col:col + 64],
                                start=True, stop=True), [B_Pc, B_PTc], [B_ps[0]], inc=(p8 == 2 * G - 1))
                        if not last:
                            for p8 in range(2 * G):
                                col = p8 * 64
                                P.op("pe", lambda e, Pc=Pc, PTc=PTc, col=col: e.matmul(
                                    psb[1][0:64, col:col + 64], lhsT=Pc[:, col:col + 64], rhs=PTc[:, col:col + 64],
                                    start=True, stop=True), [B_Pc, B_PTc], [B_ps[1]], inc=(p8 == 2 * G - 1))
                        yield
                        dv(lambda e, q=q: e.tensor_copy(out=Pq[q], in_=psb[0][0:64, :]), [B_ps[0]], [B_Pq[q]])
                        if not last:
                            P.op("act", lambda e, q=q: e.activation(out=PTq[q], in_=psb[1][0:64, :], func=AF.Copy),
                                 [B_ps[1]], [B_PTq[q]])
                        for p8 in range(2 * G):
                            col = p8 * 64
                            P.op("pe", lambda e, q=q, col=col: e.matmul(
                                psb[4][0:64, col:col + 64], lhsT=Pq[q][:, col:col + 64], rhs=Nm[s][:, col:col + 64],
                                start=True, stop=True), [B_Pq[q], B_Nm[s]], [B_ps[4]], inc=(p8 == 2 * G - 1))
                        yield
                        dv(lambda e: e.tensor_tensor(out=Nm[s], in0=Nm[s], in1=psb[4][0:64, :], op=ALU.add),
                           [B_Nm[s], B_ps[4]], [B_Nm[s]])
                        Pc, B_Pc, PTc, B_PTc = Pq[q], B_Pq[q], PTq[q], B_PTq[q]

                def chain(g, s):
                    for j in range(G):
                        ch = g * G + j
                        tsl = slice(ch * CL, (ch + 1) * CL)
                        P.op("pe", lambda e, tsl=tsl: e.matmul(
                            psb[5][0:64, 0:128], lhsT=at[:, tsl], rhs=Sbd, start=True, stop=True),
                            [B_at, B_Sbd], [B_ps[5]])
                        yield
                        dv(lambda e, j=j: e.tensor_tensor(out=RHSs, in0=psb[5][0:64, 0:128],
                                                          in1=BVs[s][:, j * 128:(j + 1) * 128], op=ALU.add),
                           [B_ps[5], B_BVs[s]], [B_RHS])
                        for h in range(2):
                            col = (j * 2 + h) * 64
                            P.op("pe", lambda e, h=h, col=col: e.matmul(
                                psb[5][0:64, 128 + h * 64:128 + (h + 1) * 64], lhsT=Nm[s][:, col:col + 64],
                                rhs=RHSs[:, h * 64:(h + 1) * 64], start=True, stop=True),
                                [B_Nm[s], B_RHS], [B_ps[5]], inc=(h == 1))
                        yield
                        P.op("act", lambda e: e.activation(out=Us, in_=psb[5][0:64, 128:256], func=AF.Copy),
                             [B_ps[5]], [B_Us])
                        dv(lambda e, ch=ch: e.tensor_scalar(out=Sg_, in0=S_, scalar1=gL[:, ch:ch + 1], scalar2=None,
                                                            op0=ALU.mult), [B_S, B_gL], [B_Sg])
                        for h in range(2):
                            hc = slice(h * 64, (h + 1) * 64)
                            P.op("pe", lambda e, h=h, hc=hc, j=j: e.matmul(
                                psb[6][hs[h], 0:64], lhsT=KTt[s][:, j, hc], rhs=Vt[s][:, j, hc], start=True,
                                stop=False, tile_position=(0, h * 64)), [B_KTt[s], B_Vt[s]], [B_ps[6]], inc=False)
                            P.op("pe", lambda e, h=h, hc=hc, j=j: e.matmul(
                                psb[6][hs[h], 0:64], lhsT=BTt[s][:, j, hc], rhs=Us[:, hc], start=False, stop=True,
                                tile_position=(0, h * 64)), [B_BTt[s], B_Us], [B_ps[6]], inc=(h == 1))
                        oc = (ch % 8) * 64
                        for h in range(2):
                            hc = slice(h * 64, (h + 1) * 64)
                            col = (j * 2 + h) * 64
                            P.op("pe", lambda e, h=h, hc=hc, tsl=tsl, oc=oc: e.matmul(
                                psb[7][hs[h], oc:oc + 64], lhsT=Sbd[:, hc], rhs=rt[:, tsl], start=True,
                                stop=False, tile_position=(0, h * 64)), [B_Sbd, B_rt], [B_ps[7]], inc=False)
                            P.op("pe", lambda e, h=h, hc=hc, col=col, oc=oc: e.matmul(
                                psb[7][hs[h], oc:oc + 64], lhsT=Us[:, hc], rhs=CTm[s][:, col:col + 64], start=False,
                                stop=False, tile_position=(0, h * 64)), [B_Us, B_CTm[s]], [B_ps[7]], inc=False)
                            P.op("pe", lambda e, h=h, hc=hc, col=col, oc=oc, j=j: e.matmul(
                                psb[7][hs[h], oc:oc + 64], lhsT=Vt[s][:, j, hc], rhs=DTm[s][:, col:col + 64],
                                start=False, stop=True, tile_position=(0, h * 64)), [B_Vt[s], B_DTm[s]], [B_ps[7]],
                                inc=(h == 1))
                        yield
                        dv(lambda e, ch=ch: e.scalar_tensor_tensor(out=S_, in0=psb[6][:, 0:64], scalar=gL[:, ch:ch + 1],
                                                                   in1=Sg_, op0=ALU.mult, op1=ALU.add),
                           [B_ps[6], B_gL, B_Sg], [B_S])
                        for h in range(2):
                            P.op("act", lambda e, h=h: e.activation(out=Sbd[hs[h], h * 64:(h + 1) * 64], in_=S_[hs[h], :],
                                                                    func=AF.Copy), [B_S], [B_Sbd])
                        if ch % 8 == 7 or ch == NCH - 1:
                            nch8 = ch % 8 + 1
                            c0 = (ch - nch8 + 1) * CL
                            dst = oT[d][:, c0:c0 + nch8 * CL]
                            if d == 1:
                                dst = rev_ap(oT[d][:, T - c0 - nch8 * CL:T - c0])
                            P.op("act", lambda e, dst=dst, nch8=nch8: e.activation(
                                out=dst, in_=psb[7][:, 0:nch8 * CL], func=AF.Copy), [B_ps[7]], [B_oT[d]])

                if hp == 0 and b == 0 and d == 0:
                    dbg("at", at, B_at, [128, T]); dbg("bt", bt, B_bt, [128, T]); dbg("kt", kt, B_kt, [128, T])
                    dbg("rt", rt, B_rt, [128, T]); dbg("v", vT, B_v, [128, T]); dbg("gL", gL, B_gL, [128, NCH])
                if stop == "R1":
                    break
                for _ in precompute(0, 0):
                    pass
                for g in range(NG):
                    if stop in ("R2", "P1", "P2", "P3"):
                        break
                    ch_it = chain(g, g % 2)
                    pre_it = precompute(g + 1, (g + 1) % 2) if g + 1 < NG else iter(())
                    ch_done = pre_done = False
                    while not (ch_done and pre_done):
                        if not ch_done:
                            try:
                                next(ch_it)
                            except StopIteration:
                                ch_done = True
                        for _ in range(2):
                            if not pre_done:
                                try:
                                    next(pre_it)
                                except StopIteration:
                                    pre_done = True

            if stop in ("R1", "R2", "P1", "P2", "P3"):
                break
            if hp == 0 and b == 0:
                dbg("o0", oT[0], B_oT[0], [128, T]); dbg("o1", oT[1], B_oT[1], [128, T])
                dbg("kk", kkT, B_kk, [128, T]); dbg("r", rT, B_r, [128, T]); dbg("cs", CS_, B_CS, [128, T])
            zr, B_zr = W_, B_W
            ctl = c.O_ZR // 128 + hp
            P.dma("sp", zr, proj_s[ctl], [B_proj[ctl]], [B_zr], B_zr)
            wkv, B_wkv = oT[0], B_oT[0]
            dv(lambda e: e.tensor_tensor(out=wkv, in0=oT[0], in1=oT[1], op=ALU.add), [B_oT[0], B_oT[1]], [B_wkv])
            cen, B_cen = A_, B_A
            for tc in range(NTC):
                sl = slice(tc * TC, (tc + 1) * TC)
                P.op("pe", lambda e, sl=sl: e.matmul(psb[0][:, 0:TC], lhsT=bdones[:, :], rhs=wkv[:, sl], start=True,
                                                     stop=True), [B_bd, B_wkv], [B_ps[0]])
                dv(lambda e, sl=sl: e.scalar_tensor_tensor(out=cen[:, sl], in0=psb[0][:, 0:TC], scalar=-1.0 / HD,
                                                           in1=wkv[:, sl], op0=ALU.mult, op1=ALU.add),
                   [B_ps[0], B_wkv], [B_cen])
            sq, B_sq = KE_, B_KE
            dv(lambda e: e.tensor_tensor(out=sq, in0=cen, in1=cen, op=ALU.mult), [B_cen], [B_sq])
            rs, B_rs = G0_, B_G0
            for tc in range(NTC):
                sl = slice(tc * TC, (tc + 1) * TC)
                P.op("pe", lambda e, sl=sl: e.matmul(psb[1][:, 0:TC], lhsT=bdones[:, :], rhs=sq[:, sl], start=True,
                                                     stop=True), [B_bd, B_sq], [B_ps[1]])
                P.op("act", lambda e, sl=sl: e.activation(out=rs[:, sl], in_=psb[1][:, 0:TC], func=AF.Sqrt,
                                                           scale=1.0 / HD, bias=GN_EPS), [B_ps[1]], [B_rs])
            dv(lambda e: e.reciprocal(out=rs, in_=rs), [B_rs], [B_rs])
            dv(lambda e: e.tensor_tensor(out=cen, in0=cen, in1=rs, op=ALU.mult), [B_cen, B_rs], [B_cen])
            dv(lambda e, hp=hp: e.tensor_scalar(out=cen, in0=cen, scalar1=pcc(hp, 7), scalar2=pcc(hp, 8),
                                                op0=ALU.mult, op1=ALU.add), [B_cen, B_pc], [B_cen])
            for tc in range(NTC):
                sl = slice(tc * TC, (tc + 1) * TC)
                P.op("pe", lambda e, sl=sl: e.matmul(psb[0][:, 0:TC], lhsT=bdones[:, :], rhs=CS_[:, sl], start=True,
                                                     stop=True), [B_bd, B_CS], [B_ps[0]])
                dv(lambda e, sl=sl: e.tensor_tensor(out=sq[:, sl], in0=psb[0][:, 0:TC], in1=vT[:, sl], op=ALU.mult),
                   [B_ps[0], B_v], [B_sq])
            dv(lambda e: e.tensor_tensor(out=cen, in0=cen, in1=sq, op=ALU.add), [B_cen, B_sq], [B_cen])
            dv(lambda e: e.tensor_tensor(out=ob, in0=cen, in1=zr, op=ALU.mult), [B_cen, B_zr], [B_ob])
            P.dma("sp", orT_s[b][hp * 128:(hp + 1) * 128, :], ob, [B_ob], [B_orTs[b]], B_ob)

    NKP = c.KVW // 128
    NQP = c.AW // 128
    SCALE = float(HD) ** -0.5

    def attn_phase(b):
        AR.reset()
        A = AR.alloc
        cosT, B_cos = A("cosT", [T]); sinT, B_sin = A("sinT", [T])
        P.dma("sp", cosT, ccos_d[:, :], [], [B_cos], B_cos)
        P.dma("sp", sinT, csin_d[:, :], [], [B_sin], B_sin)
        src, B_src = A("asrc", [T]); t1, B_t1 = A("at1", [T]); t2, B_t2 = A("at2", [T])
        za, B_za = A("za", [T])
        nb, B_nb = A("anb", [T], BF16)
        og, B_og = A("og", [T], BF16)
        qz, B_qz = [], []
        for p in range(2):
            v, bb = A("qz%d" % p, [T], BF16); qz.append(v); B_qz.append(bb)
        kd, B_kd = [], []
        for kvh in range(c.KVH):
            v, bb = A("kd%d" % kvh, [T], BF16); kd.append(v); B_kd.append(bb)
        Va, B_Va = [], []
        for kvh in range(c.KVH):
            row, brow = [], []
            for p in range(2):
                v, bb = A("Va%d_%d" % (kvh, p), [NT, 128], BF16); row.append(v); brow.append(bb)
            Va.append(row); B_Va.append(brow)
        pT, B_pT = [], []
        for i in range(3):
            v, bb = A("pT%d" % i, [TC], BF16); pT.append(v); B_pT.append(bb)
        rc, B_rc = A("rc", [TC]); on, B_on = A("on", [TC])

        def dv(fn, reads, writes):
            P.op("dve", fn, reads, writes)

        def qk_prep(ct, gcol):
            P.dma("sp", src, proj_s[ct], [B_proj[ct]], [B_src], B_src)
            dv(lambda e: e.tensor_tensor(out=t1, in0=src, in1=src, op=ALU.mult), [B_src], [B_t1])
            for tc in range(NTC):
                sl = slice(tc * TC, (tc + 1) * TC)
                P.op("pe", lambda e, sl=sl: e.matmul(psb[5][:, 0:TC], lhsT=bdones[:, :], rhs=t1[:, sl], start=True,
                                                     stop=True), [B_bd, B_t1], [B_ps[5]])
                P.op("act", lambda e, sl=sl: e.activation(out=t2[:, sl], in_=psb[5][:, 0:TC], func=AF.Sqrt,
                                                           scale=1.0 / HD, bias=NORM_EPS), [B_ps[5]], [B_t2])
            dv(lambda e: e.reciprocal(out=t2, in_=t2), [B_t2], [B_t2])
            dv(lambda e: e.scalar_tensor_tensor(out=t2, in0=src, scalar=pc[:, gcol:gcol + 1], in1=t2, op0=ALU.mult,
                                                op1=ALU.mult), [B_src, B_pc, B_t2], [B_t2])
            for tc in range(NTC):
                sl = slice(tc * TC, (tc + 1) * TC)
                P.op("pe", lambda e, sl=sl: e.matmul(psb[6][:, 0:TC], lhsT=rotm[:, :], rhs=t2[:, sl], start=True,
                                                     stop=True), [B_rot, B_t2], [B_ps[6]])
                dv(lambda e, sl=sl: e.tensor_tensor(out=t1[:, sl], in0=psb[6][:, 0:TC], in1=sinT[:, sl], op=ALU.mult),
                   [B_ps[6], B_sin], [B_t1])
            dv(lambda e: e.tensor_tensor(out=t2, in0=t2, in1=cosT, op=ALU.mult), [B_t2, B_cos], [B_t2])
            dv(lambda e: e.tensor_tensor(out=t2, in0=t2, in1=t1, op=ALU.add), [B_t2, B_t1], [B_t2])
            dv(lambda e: e.tensor_copy(out=nb, in_=t2), [B_t2], [B_nb])

        GQ = PCB + 9 * NHP
        for kp in range(NKP):
            qk_prep(c.O_AK // 128 + kp, GQ + 1)
            for h2 in range(2):
                kvh = kp * 2 + h2
                hsl = slice(h2 * 64, (h2 + 1) * 64)
                osl = slice((1 - h2) * 64, (2 - h2) * 64)
                dv(lambda e, kvh=kvh, hsl=hsl: e.tensor_copy(out=kd[kvh][hsl, :], in_=t2[hsl, :]), [B_t2], [B_kd[kvh]])
                dv(lambda e, hsl=hsl, osl=osl: e.tensor_copy(out=t1[osl, :], in_=t2[hsl, :]), [B_t2], [B_t1])
                dv(lambda e, kvh=kvh, osl=osl: e.tensor_copy(out=kd[kvh][osl, :], in_=t1[osl, :]), [B_t1], [B_kd[kvh]])
        if stop == "T1":
            return
        for kvh in range(c.KVH):
            for p in range(2):
                dv(lambda e, kvh=kvh, p=p: e.memset(Va[kvh][p].rearrange("p a b -> p (a b)"), 1.0), [], [B_Va[kvh][p]])
        if stop == "V1":
            return
        for kp in range(NKP):
            ct = c.O_AV // 128 + kp
            P.dma("sp", src, proj_s[ct], [B_proj[ct]], [B_src], B_src)
            if stop == "V2":
                return
            for it in range(NT):
                P.op("pe", lambda e, it=it: e.matmul(psb[7][:, 0:128], lhsT=src[:, it * 128:(it + 1) * 128],
                                                     rhs=ident_f[:, :], start=True, stop=True),
                     [B_src, B_identf], [B_ps[7]])
                if stop == "V3":
                    continue
                for h2 in range(2):
                    kvh = kp * 2 + h2
                    cs_ = slice(h2 * 64, (h2 + 1) * 64)
                    dv(lambda e, kvh=kvh, it=it, cs_=cs_: e.tensor_copy(out=Va[kvh][0][:, it, 0:64], in_=psb[7][:, cs_]),
                       [B_ps[7]], [B_Va[kvh][0]])
                    dv(lambda e, kvh=kvh, it=it, cs_=cs_: e.tensor_copy(out=Va[kvh][1][:, it, 64:128], in_=psb[7][:, cs_]),
                       [B_ps[7]], [B_Va[kvh][1]])
        if stop == "T2":
            return
        cnt = 0
        for qp in range(NQP):
            qk_prep(c.O_Q // 128 + qp, GQ)
            ctz = c.O_ZA // 128 + qp
            P.dma("sp", za, proj_s[ctz], [B_proj[ctz]], [B_za], B_za)
            for p in range(2):
                dv(lambda e, p=p: e.memset(qz[p], 0.0), [], [B_qz[p]])
                dv(lambda e, p=p: e.tensor_copy(out=qz[p][p * 64:(p + 1) * 64, :], in_=nb[p * 64:(p + 1) * 64, :]),
                   [B_nb], [B_qz[p]])
            for p in range(2):
                qh = qp * 2 + p
                kvh = qh // c.GROUP
                qsl = slice(p * 64, (p + 1) * 64)
                ssl = slice((1 - p) * 64, (2 - p) * 64)
                for qc in range(NTC):
                    qs = slice(qc * TC, (qc + 1) * TC)
                    acc = 3 + (cnt % 2)
                    def emit_s(kt):
                        sb_ = (cnt * NT + kt) % 3
                        P.op("pe", lambda e, sb_=sb_, kt=kt, qs=qs, kvh=kvh, p=p: e.matmul(
                            psb[sb_][:, 0:TC], lhsT=kd[kvh][:, kt * 128:(kt + 1) * 128], rhs=qz[p][:, qs], start=True,
                            stop=True), [B_kd[kvh], B_qz[p]], [B_ps[sb_]])

                    emit_s(0)
                    for kt in range(NT):
                        sb_ = (cnt * NT + kt) % 3
                        if kt + 1 < NT:
                            emit_s(kt + 1)
                        P.op("act", lambda e, sb_=sb_: e.activation(out=pT[sb_], in_=psb[sb_][:, 0:TC], func=AF.Exp,
                                                                    scale=SCALE), [B_ps[sb_]], [B_pT[sb_]])
                        P.op("pe", lambda e, sb_=sb_, kt=kt, acc=acc, kvh=kvh: e.matmul(
                            psb[acc][:, 0:TC], lhsT=Va[kvh][p][:, kt, :], rhs=pT[sb_], start=(kt == 0),
                            stop=(kt == NT - 1)), [B_Va[kvh][p], B_pT[sb_]], [B_ps[acc]], inc=(kt == NT - 1))
                    dv(lambda e, acc=acc: e.reciprocal(out=rc[ssl, :], in_=psb[acc][ssl, 0:TC]), [B_ps[acc]], [B_rc])
                    dv(lambda e: e.tensor_copy(out=rc[qsl, :], in_=rc[ssl, :]), [B_rc], [B_rc])
                    dv(lambda e, acc=acc: e.tensor_tensor(out=on[qsl, :], in0=psb[acc][qsl, 0:TC], in1=rc[qsl, :],
                                                          op=ALU.mult), [B_ps[acc], B_rc], [B_on])
                    dv(lambda e, qs=qs: e.tensor_tensor(out=og[qsl, qs], in0=on[qsl, :], in1=za[qsl, qs], op=ALU.mult),
                       [B_on, B_za], [B_og])
                    cnt += 1
            P.dma("sp", oaT_s[b][qp * 128:(qp + 1) * 128, :], og, [B_og], [B_oaTs[b]], B_og)

    CG = min(512, D)
    NCG = D // CG
    KR = RW // 128
    KA = c.AW // 128

    def phase_d_all():
        AR.reset()
        A = AR.alloc
        wout, B_wout = A("wout", [KC, D], BF16)
        fbc, B_fbc = A("fbc", [D])
        for k0 in range(0, KC, KG):
            k1 = min(KC, k0 + KG)
            P.dma("sp", wout[:, k0:k1, :], wq_out[k0 * 128:k1 * 128, :].rearrange("(kc kp) n -> kp kc n", kp=128),
                  [B_wq], [B_wout], B_wout)
        P.dma("sp", fbc, fing_d.partition_broadcast(128), [], [B_fbc], B_fbc)
        hTc, B_hTc = A("hTc", [KC, TC], BF16)
        orc, B_orc = A("orc", [KR, TC], BF16); oac, B_oac = A("oac", [KA, TC], BF16)
        mg, B_mg = A("mg", [KC, TC], BF16)
        wgr, B_wgr, wga, B_wga, wbr, B_wbr, wba, B_wba = ([] for _ in range(8))
        for i in range(2):
            v, bb = A("wgr%d" % i, [KC, 128], BF16); wgr.append(v); B_wgr.append(bb)
            v, bb = A("wga%d" % i, [KC, 128], BF16); wga.append(v); B_wga.append(bb)
            v, bb = A("wbr%d" % i, [KR, 128], BF16); wbr.append(v); B_wbr.append(bb)
            v, bb = A("wba%d" % i, [KA, 128], BF16); wba.append(v); B_wba.append(bb)
        s1, B_s1 = A("s1", [TC]); s2, B_s2 = A("s2", [TC]); u1, B_u1 = A("u1", [TC]); u2, B_u2 = A("u2", [TC])
        xt, B_xt = A("dxt", [D]); yp, B_yp = A("yp", [D]); junk, B_junk = A("djunk", [D], BF16)
        st2, B_st2 = A("st2", [4])

        def dv(fn, reads, writes):
            P.op("dve", fn, reads, writes)

        for b in range(c.BPC):
            for tcn in range(NTC):
                ts = slice(tcn * TC, (tcn + 1) * TC)
                for dst_, src_, nk, B_s, B_d in ((hTc, hT_s, KC, B_hTs[b], B_hTc), (orc, orT_s, KR, B_orTs[b], B_orc),
                                                 (oac, oaT_s, KA, B_oaTs[b], B_oac)):
                    for k0 in range(0, nk, KG):
                        k1 = min(nk, k0 + KG)
                        P.dma("sp", dst_[:, k0:k1, :],
                              src_[b][k0 * 128:k1 * 128, ts].rearrange("(kc kp) t -> kp kc t", kp=128),
                              [B_s], [B_d], B_d)
                for j in range(KC):
                    i = j % 2
                    P.dma("sp", wgr[i], wq_in[c.O_GR // 128 + j], [B_wqg[(c.O_GR // 128 + j) // WQG]], [B_wgr[i]], B_wgr[i])
                    P.dma("sp", wga[i], wq_in[c.O_GA // 128 + j], [B_wqg[(c.O_GA // 128 + j) // WQG]], [B_wga[i]], B_wga[i])
                    P.dma("sp", wbr[i], wq_br[:, j * 128:(j + 1) * 128].rearrange("(kc kp) n -> kp kc n", kp=128),
                          [B_wq], [B_wbr[i]], B_wbr[i])
                    P.dma("sp", wba[i], wq_ba[:, j * 128:(j + 1) * 128].rearrange("(kc kp) n -> kp kc n", kp=128),
                          [B_wq], [B_wba[i]], B_wba[i])
                    pb0 = 4 * i
                    for kc in range(KC):
                        P.op("pe", lambda e, kc=kc, i=i, pb0=pb0: e.matmul(
                            psb[pb0][:, 0:TC], lhsT=wgr[i][:, kc, :], rhs=hTc[:, kc, :], start=(kc == 0),
                            stop=(kc == KC - 1)), [B_wgr[i], B_hTc], [B_ps[pb0]], inc=(kc == KC - 1))
                    for kc in range(KC):
                        P.op("pe", lambda e, kc=kc, i=i, pb0=pb0: e.matmul(
                            psb[pb0 + 1][:, 0:TC], lhsT=wga[i][:, kc, :], rhs=hTc[:, kc, :], start=(kc == 0),
                            stop=(kc == KC - 1)), [B_wga[i], B_hTc], [B_ps[pb0 + 1]], inc=(kc == KC - 1))
                    for kc in range(KR):
                        P.op("pe", lambda e, kc=kc, i=i, pb0=pb0: e.matmul(
                            psb[pb0 + 2][:, 0:TC], lhsT=wbr[i][:, kc, :], rhs=orc[:, kc, :], start=(kc == 0),
                            stop=(kc == KR - 1)), [B_wbr[i], B_orc], [B_ps[pb0 + 2]], inc=(kc == KR - 1))
                    for kc in range(KA):
                        P.op("pe", lambda e, kc=kc, i=i, pb0=pb0: e.matmul(
                            psb[pb0 + 3][:, 0:TC], lhsT=wba[i][:, kc, :], rhs=oac[:, kc, :], start=(kc == 0),
                            stop=(kc == KA - 1)), [B_wba[i], B_oac], [B_ps[pb0 + 3]], inc=(kc == KA - 1))
                    P.op("act", lambda e, pb0=pb0: e.activation(out=s1, in_=psb[pb0][:, 0:TC], func=AF.Sigmoid),
                         [B_ps[pb0]], [B_s1])
                    P.op("act", lambda e, pb0=pb0: e.activation(out=s2, in_=psb[pb0 + 1][:, 0:TC], func=AF.Sigmoid),
                         [B_ps[pb0 + 1]], [B_s2])
                    dv(lambda e, pb0=pb0: e.tensor_tensor(out=u1, in0=psb[pb0 + 2][:, 0:TC], in1=s1, op=ALU.mult),
                       [B_ps[pb0 + 2], B_s1], [B_u1])
                    dv(lambda e, pb0=pb0: e.tensor_tensor(out=u2, in0=psb[pb0 + 3][:, 0:TC], in1=s2, op=ALU.mult),
                       [B_ps[pb0 + 3], B_s2], [B_u2])
                    dv(lambda e, j=j: e.tensor_tensor(out=mg[:, j, :], in0=u1, in1=u2, op=ALU.add), [B_u1, B_u2], [B_mg])
                for it in range(TC // 128):
                    t0 = tcn * TC + it * 128
                    P.dma("sp", xt, x_d[b, t0:t0 + 128, :], [], [B_xt], B_xt)
                    for cg in range(NCG):
                        pb = (it * NCG + cg) % 2
                        for kc in range(KC):
                            P.op("pe", lambda e, kc=kc, it=it, cg=cg, pb=pb: e.matmul(
                                psb[pb][:, 0:CG], lhsT=mg[:, kc, it * 128:(it + 1) * 128],
                                rhs=wout[:, kc, cg * CG:(cg + 1) * CG], start=(kc == 0), stop=(kc == KC - 1)),
                                [B_mg, B_wout], [B_ps[pb]], inc=(kc == KC - 1))
                        dv(lambda e, cg=cg, pb=pb: e.tensor_tensor(out=yp[:, cg * CG:(cg + 1) * CG], in0=psb[pb][:, 0:CG],
                                                                   in1=xt[:, cg * CG:(cg + 1) * CG], op=ALU.add),
                           [B_ps[pb], B_xt], [B_yp])
                    P.op("act", lambda e: e.activation(out=junk, in_=yp, func=AF.Square, accum_out=st2[:, 0:1]),
                         [B_yp], [B_junk, B_st2])
                    P.op("act", lambda e: e.activation(out=st2[:, 1:2], in_=st2[:, 0:1], func=AF.Sqrt, scale=1.0 / D,
                                                       bias=NORM_EPS), [B_st2], [B_st2])
                    dv(lambda e: e.reciprocal(out=st2[:, 2:3], in_=st2[:, 1:2]), [B_st2], [B_st2])
                    dv(lambda e: e.scalar_tensor_tensor(out=yp, in0=yp, scalar=st2[:, 2:3], in1=fbc, op0=ALU.mult,
                                                        op1=ALU.mult), [B_yp, B_st2, B_fbc], [B_yp])
                    P.dma("sp", y_d[b, t0:t0 + 128, :], yp, [B_yp], [B_y], B_yp)

    stop = getattr(c, "stop", None)
    for b in range(c.BPC):
        phase_a_proj(b)
        if stop == "A":
            continue
        rwkv_phase(b)
        if stop in ("R", "R1", "R2", "P1", "P2", "P3"):
            continue
        attn_phase(b)
    if stop is None or stop == "D":
        phase_d_all()

    P.finish()
    P.emit()
    return nc


def host_consts(cfg):
    c = cfg
    T = c.T
    ident = np.eye(128, dtype=np.float32)
    j64 = np.eye(64, dtype=np.float32)[::-1].copy()
    rot = np.zeros((128, 128), np.float32)
    for i in range(64):
        rot[2 * i + 1, 2 * i] = -1.0
        rot[2 * i, 2 * i + 1] = 1.0
    rows = T // c.GRID_W
    row = np.repeat(np.arange(rows, dtype=np.float32), c.GRID_W)
    col = np.tile(np.arange(c.GRID_W, dtype=np.float32), rows)
    axis_dim = HD // 2
    freqs = (10000.0 ** (-np.arange(0, axis_dim, 2, dtype=np.float32) / axis_dim)).astype(np.float32)
    ang = np.concatenate([row[:, None] * freqs, col[:, None] * freqs], axis=-1).astype(np.float32)
    cosT = np.repeat(np.cos(ang).T, 2, axis=0)
    sinT = np.repeat(np.sin(ang).T, 2, axis=0)
    cos2 = np.concatenate([cosT, cosT], 0).astype(np.float32)
    sin2 = np.concatenate([sinT, sinT], 0).astype(np.float32)
    bd = np.zeros((128, 128), np.float32)
    bd[:64, :64] = 1.0
    bd[64:, 64:] = 1.0
    m0 = np.ones((128, T), np.float32)
    m0[:, ::CL] = 0.0
    i64 = np.arange(64)
    strict_st = (i64[None, :] > i64[:, None]).astype(np.float32)
    strict_ts = (i64[None, :] < i64[:, None]).astype(np.float32)
    incl_st = (i64[None, :] >= i64[:, None]).astype(np.float32)
    masks = np.stack([np.tile(m, (1, 8)) for m in (strict_st, strict_ts, incl_st)], axis=1)
    return dict(c_ident=ident, c_j64=j64, c_rot=rot, c_cos=cos2, c_sin=sin2, c_bdones=bd, c_m0=m0,
                c_masks=np.ascontiguousarray(masks.astype(np.float32)))


def make_in_maps(cfg, inputs, n_cores):
    c = cfg
    consts = host_consts(c)
    f = lambda a: np.ascontiguousarray(np.asarray(a, dtype=np.float32))
    NST, NHP = c.SHIFT_W // 128, c.RW // 128
    smu = f(inputs["shift_mu"][0])
    cols = [smu.reshape(2, NST, 128).transpose(2, 1, 0).reshape(128, 2 * NST)]
    per = [f(inputs["w0"][0])[0], f(inputs["w0"][0])[1], f(inputs["a0"][0])[0], f(inputs["a0"][0])[1],
           f(inputs["k_k"][0]), f(inputs["k_a"][0]), f(inputs["r_k"][0]), f(inputs["gn_w"][0]), f(inputs["gn_b"][0])]
    per = np.stack([p.reshape(NHP, 128) for p in per], axis=-1)
    cols.append(per.transpose(1, 0, 2).reshape(128, 9 * NHP))
    qg = np.tile(f(inputs["q_norm_g"][0]), 2)[:, None]
    kg = np.tile(f(inputs["k_norm_g"][0]), 2)[:, None]
    pc = np.ascontiguousarray(np.concatenate(cols + [qg, kg], axis=1).astype(np.float32))
    shared = dict(
        w_in=f(inputs["w_in"][0]), w_br=f(inputs["w_branch_rwkv"][0]), w_ba=f(inputs["w_branch_attn"][0]),
        w_out=f(inputs["w_out"][0]), norm_g=f(inputs["norm_g"][0]), final_norm_g=f(inputs["final_norm_g"]),
        pc=pc, w_up=f(inputs["w_up"][0]).reshape(2 * LORA, c.RW), a_up=f(inputs["a_up"][0]).reshape(2 * LORA, c.RW),
        **consts)
    x = np.asarray(inputs["x"], dtype=np.float32)
    maps = []
    for i in range(n_cores):
        m = dict(shared)
        m["x"] = np.ascontiguousarray(x[i * c.BPC:(i + 1) * c.BPC])
        maps.append(m)
    return maps


def kernel(**inputs):
    cfg = Cfg()
    n = 8
    nc = build(cfg)
    in_maps = make_in_maps(cfg, inputs, n)
    res = run_bass_kernel_spmd(nc, in_maps, core_ids=list(range(n)))
    return np.concatenate([r["y"] for r in res.results], axis=0)
```

```python
import math
from contextlib import ExitStack
import numpy as np
import ml_dtypes
import concourse.bass as bass
import concourse.mybir as mybir
from concourse.bass_utils import run_bass_kernel_spmd

F32 = mybir.dt.float32
BF16 = mybir.dt.bfloat16
ALU = mybir.AluOpType
AF = mybir.ActivationFunctionType
AX = mybir.AxisListType

NORM_EPS = 1e-6
GN_EPS = 64e-5
HD = 64
LORA = 64
CL = 64


class Cfg:
    def __init__(self, T=2048, D=2048, RH=16, QH=16, KVH=4, BPC=2, GRID_W=64, debug=False):
        self.T, self.D, self.RH, self.QH, self.KVH, self.BPC, self.GRID_W = T, D, RH, QH, KVH, BPC, GRID_W
        self.debug = debug
        self.RW = RH * HD
        self.AW = QH * HD
        self.KVW = KVH * HD
        self.GROUP = QH // KVH
        self.SHIFT_W = 3 * self.RW + 4 * LORA
        self.O_R, self.O_K, self.O_V = 0, self.RW, 2 * self.RW
        self.O_WD, self.O_AD = 3 * self.RW, 3 * self.RW + 2 * LORA
        self.O_ZR = self.SHIFT_W
        self.O_Q = self.O_ZR + self.RW
        self.O_AK = self.O_Q + self.AW
        self.O_AV = self.O_AK + self.KVW
        self.O_ZA = self.O_AV + self.KVW
        self.O_GR = self.O_ZA + self.AW
        self.O_GA = self.O_GR + D
        self.D_IN = self.O_GA + D
        self.KC = D // 128
        self.NCT = self.D_IN // 128
        self.TC = min(512, T)
        self.NTC = T // self.TC
        self.NT = T // 128
        self.NCH = T // CL


class Buf:
    __slots__ = ("name", "w", "r", "chan")

    def __init__(self, name):
        self.name, self.w, self.r, self.chan = name, None, {}, None


class Chan:
    __slots__ = ("sem", "cnt", "key")


class _Rec:
    def __getattr__(self, name):
        def f(*a, **k):
            self.call = (name, a, k)
            return self
        return f


class Prog:
    ENG = ("pe", "dve", "act", "pool", "sp")

    def __init__(self, nc, stack):
        self.nc, self.stack = nc, stack
        self.ops = {e: [] for e in self.ENG}
        self.sem = {e: stack.enter_context(nc.semaphore("s_" + e)) for e in self.ENG}
        self.cnt = {e: 0 for e in self.ENG}
        self.pending = {e: False for e in self.ENG}
        self.known = {e: {} for e in self.ENG}
        self.chans = {}
        self.chan_of = {}
        self.NCHAN = 16
        self.nbuf = 0

    def buf(self, name=None):
        self.nbuf += 1
        return Buf(name or "b%d" % self.nbuf)

    def _chan(self, b, dedicated=False):
        if b.chan is None and dedicated:
            c = Chan()
            c.key = "cd_" + b.name
            c.sem = self.stack.enter_context(self.nc.semaphore(c.key))
            c.cnt = 0
            self.chans[c.key] = c
            b.chan = c
        if b.chan is None:
            if b.name not in self.chan_of:
                idx = len(self.chan_of) % self.NCHAN
                key = "c_%d" % idx
                if key not in self.chans:
                    c = Chan()
                    c.key = key
                    c.sem = self.stack.enter_context(self.nc.semaphore(key))
                    c.cnt = 0
                    self.chans[key] = c
                self.chan_of[b.name] = key
            b.chan = self.chans[self.chan_of[b.name]]
        return b.chan

    def _waits(self, eng, reads, writes):
        need = {}

        def add(ev, raw):
            if ev is None:
                return
            key, val = ev
            if key == eng and eng == "pe":
                return
            if key in self.chans:
                val = self.chans[key].cnt * 16
            if need.get(key, 0) < val:
                need[key] = val

        for b in reads:
            add(b.w, True)
        for b in writes:
            add(b.w, False)
            for k, v in b.r.items():
                add((k, v), False)
        out = []
        kn = self.known[eng]
        for key, val in need.items():
            if kn.get(key, 0) >= val:
                continue
            kn[key] = val
            sem = self.chans[key].sem if key in self.chans else self.sem[key]
            out.append((sem, val))
        return out

    def _mark(self, ev, reads, writes):
        k, v = ev
        for b in reads:
            if b.r.get(k, 0) < v:
                b.r[k] = v
        for b in writes:
            b.w = ev
            b.r = {}

    def op(self, eng, fn, reads=(), writes=(), inc=True):
        rec = _Rec()
        fn(rec)
        name_, a_, k_ = rec.call
        fn = lambda e, name_=name_, a_=a_, k_=k_: getattr(e, name_)(*a_, **k_)
        waits = self._waits(eng, reads, writes)
        if inc:
            self.cnt[eng] += 1
            ev = (eng, self.cnt[eng])
            self.pending[eng] = False
            self.ops[eng].append((waits, fn, (self.sem[eng], 1)))
        else:
            ev = (eng, self.cnt[eng] + 1)
            self.pending[eng] = True
            self.ops[eng].append((waits, fn, None))
        self._mark(ev, reads, writes)

    def dma(self, q, out_ap, in_ap, reads, writes, chanbuf):
        waits = self._waits(q, reads, writes)
        c = self._chan(chanbuf, dedicated=(q == "pool"))
        c.cnt += 1
        ev = (c.key, c.cnt * 16)
        self.ops[q].append((waits, lambda e: e.dma_start(out=out_ap, in_=in_ap), (c.sem, 16)))
        self._mark(ev, reads, writes)

    def finish(self):
        for e in self.ENG:
            if self.pending[e]:
                raise RuntimeError("engine %s ends with a non-incrementing op" % e)
        waits = []
        for e in self.ENG:
            if e != "sp" and self.cnt[e] > 0:
                waits.append((self.sem[e], self.cnt[e]))
        for c in self.chans.values():
            if c.cnt:
                waits.append((c.sem, c.cnt * 16))
        self.ops["sp"].append((waits, None, None))

    def emit(self):
        nc = self.nc
        engmap = {"pe": "tensor", "dve": "vector", "act": "scalar", "pool": "gpsimd", "sp": "sync"}
        with nc.Block() as block:
            for e in self.ENG:
                ops = self.ops[e]

                def body(eng, ops=ops):
                    for waits, fn, inc in ops:
                        for sem, val in waits:
                            eng.wait_ge(sem, val)
                        if fn is None:
                            continue
                        try:
                            ins = fn(eng)
                        except Exception:
                            print("FAILED OP:", getattr(fn, "__defaults__", None))
                            raise
                        if inc is not None:
                            ins.then_inc(inc[0], inc[1])

                getattr(block, engmap[e])(body)


def rev_ap(ap):
    pat = [list(p) for p in ap.ap]
    step, n = pat[-1]
    pat[-1] = [-step, n]
    return bass.AP(tensor=ap.tensor, offset=ap.offset + step * (n - 1), ap=pat)


def build(cfg):
    c = cfg
    T, D, KC, TC, NTC, NT, RW = c.T, c.D, c.KC, c.TC, c.NTC, c.NT, c.RW
    nc = bass.Bass("TRN2", target_bir_lowering=False)
    st = ExitStack()
    P = Prog(nc, st)

    def din(name, shape, dt=F32):
        return nc.dram_tensor(name, list(shape), dt, kind="ExternalInput").ap()

    def dscr(name, shape, dt, dbg=False):
        kind = "ExternalOutput" if (dbg and c.debug) else "Internal"
        return nc.dram_tensor(name, list(shape), dt, kind=kind).ap()

    x_d = din("x", [c.BPC, T, D])
    w_in_d = din("w_in", [D, c.D_IN])
    w_br_d = din("w_br", [RW, D])
    w_ba_d = din("w_ba", [c.AW, D])
    w_out_d = din("w_out", [D, D])
    normg_d = din("norm_g", [D])
    fing_d = din("final_norm_g", [D])
    NST = c.SHIFT_W // 128
    NHP = RW // 128
    NPC = 2 * NST + 9 * NHP + 2
    pc_d = din("pc", [128, NPC])
    wup_d = din("w_up", [2 * LORA, RW])
    aup_d = din("a_up", [2 * LORA, RW])
    cmask_d = din("c_masks", [64, 3, 512])
    cident_d = din("c_ident", [128, 128])
    cj_d = din("c_j64", [64, 64])
    crot_d = din("c_rot", [128, 128])
    ccos_d = din("c_cos", [128, T])
    csin_d = din("c_sin", [128, T])
    cbd_d = din("c_bdones", [128, 128])
    cm0_d = din("c_m0", [128, T])
    y_d = nc.dram_tensor("y", [c.BPC, T, D], F32, kind="ExternalOutput").ap()

    wq_in = dscr("wq_in", [c.NCT, 128, KC, 128], BF16)
    wq_br = dscr("wq_br", [RW, D], BF16)
    wq_ba = dscr("wq_ba", [c.AW, D], BF16)
    wq_out = dscr("wq_out", [D, D], BF16)
    NPT = c.O_GR // 128
    proj_s = dscr("proj_s", [NPT, 128, T], F32)
    hT_s = dscr("hT_s", [c.BPC, D, T], BF16, dbg=True)
    orT_s = dscr("orT_s", [c.BPC, RW, T], BF16, dbg=True)
    oaT_s = dscr("oaT_s", [c.BPC, c.AW, T], BF16, dbg=True)

    def sb(name, shape, dt=F32):
        return st.enter_context(nc.sbuf_tensor(name, list(shape), dt))

    def ps(name, shape, dt=F32):
        return st.enter_context(nc.psum_tensor(name, list(shape), dt))

    B_wq = P.buf("wq")
    WQG = 12
    B_wqg = [P.buf("wqg%d" % g) for g in range((c.NCT + WQG - 1) // WQG)]
    B_hTs = [P.buf("hTs%d" % b) for b in range(c.BPC)]
    B_orTs = [P.buf("orTs%d" % b) for b in range(c.BPC)]
    B_oaTs = [P.buf("oaTs%d" % b) for b in range(c.BPC)]
    B_proj = [P.buf("proj%d" % i) for i in range(NPT)]
    B_y = P.buf("y")

    ident_f = sb("ident_f", [128, 128]); B_identf = P.buf("identf")
    ident_b = sb("ident_b", [128, 128], BF16); B_identb = P.buf("identb")
    pc = sb("pc_sb", [128, NPC]); B_pc = P.buf("pc")
    ccT = sb("ccT", [128, NST]); B_cc = P.buf("cc")
    omka = sb("omka", [128, NHP]); B_omka = P.buf("omka")
    bdones = sb("bdones", [128, 128]); B_bd = P.buf("bd")
    rotm = sb("rotm", [128, 128]); B_rot = P.buf("rot")
    masks = sb("masks", [64, 3, 512]); B_mask = P.buf("mask")
    I8 = sb("I8", [64, 512]); B_I8 = P.buf("I8")
    j64 = sb("j64", [64, 64]); B_j64 = P.buf("j64")
    P.dma("sp", ident_f[:, :], cident_d[:, :], [], [B_identf], B_identf)
    P.dma("sp", pc[:, :], pc_d[:, :], [], [B_pc], B_pc)
    P.dma("sp", bdones[:, :], cbd_d[:, :], [], [B_bd], B_bd)
    P.dma("sp", rotm[:, :], crot_d[:, :], [], [B_rot], B_rot)
    P.dma("sp", masks[:, :, :], cmask_d[:, :, :], [], [B_mask], B_mask)
    P.dma("sp", j64[:, :], cj_d[:, :], [], [B_j64], B_j64)
    P.op("dve", lambda e: e.tensor_copy(out=ident_b[:, :], in_=ident_f[:, :]), [B_identf], [B_identb])
    for g8 in range(8):
        P.op("dve", lambda e, g8=g8: e.tensor_copy(out=I8[:, g8 * 64:(g8 + 1) * 64], in_=ident_f[0:64, 0:64]),
             [B_identf], [B_I8])
    pcv = pc[:, 0:2 * NST].rearrange("p (n j) -> p n j", j=2)
    P.op("dve", lambda e: e.tensor_tensor(out=ccT[:, :], in0=pcv[:, :, 0], in1=pcv[:, :, 1], op=ALU.add),
         [B_pc], [B_cc])
    P.op("dve", lambda e: e.tensor_scalar(out=ccT[:, :], in0=ccT[:, :], scalar1=-1.0, scalar2=1.0,
                                          op0=ALU.mult, op1=ALU.add), [B_cc], [B_cc])
    PCB = 2 * NST

    def pcc(hp, j):
        return pc[:, PCB + 9 * hp + j:PCB + 9 * hp + j + 1]

    for hp in range(NHP):
        P.op("dve", lambda e, hp=hp: e.tensor_scalar(out=omka[:, hp:hp + 1], in0=pcc(hp, 5), scalar1=-1.0,
                                                     scalar2=1.0, op0=ALU.mult, op1=ALU.add), [B_pc], [B_omka])

    KG = 4
    for ct in range(c.NCT):
        for k0 in range(0, KC, KG):
            k1 = min(KC, k0 + KG)
            P.dma("pool", wq_in[ct][:, k0:k1, :],
                  w_in_d[k0 * 128:k1 * 128, ct * 128:(ct + 1) * 128].rearrange("(kc kp) c -> kp kc c", kp=128),
                  [], [B_wqg[ct // WQG]], B_wqg[ct // WQG])
    for dst_, src_, rows in ((wq_br, w_br_d, RW), (wq_ba, w_ba_d, c.AW), (wq_out, w_out_d, D)):
        for r0 in range(0, rows, 256):
            r1 = min(rows, r0 + 256)
            P.dma("pool", dst_[r0:r1, :], src_[r0:r1, :], [], [B_wq], B_wq)

    ARENA_BYTES = 196 * 1024
    arena_t = sb("arena", [128, ARENA_BYTES // 2], BF16)

    class Arena:
        def __init__(self):
            self.off, self.prev, self.live = 0, {}, []

        def reset(self):
            for b in self.live:
                evs = list(b.r.items())
                if b.w is not None:
                    evs.append(b.w)
                for k, v in evs:
                    if self.prev.get(k, 0) < v:
                        self.prev[k] = v
            self.live, self.off = [], 0

        def alloc(self, name, fshape, dt=F32, parts=128):
            n = 1
            for s in fshape:
                n *= s
            nbytes = n * (4 if dt == F32 else 2)
            nbytes = (nbytes + 63) // 64 * 64
            assert self.off + nbytes <= ARENA_BYTES, ("arena overflow", name, self.off, nbytes)
            v = arena_t[0:parts, self.off // 2:(self.off + n * (4 if dt == F32 else 2)) // 2]
            if dt == F32:
                v = v.bitcast(F32)
            if len(fshape) == 2:
                v = v.rearrange("p (a b) -> p a b", a=fshape[0])
            elif len(fshape) == 3:
                v = v.rearrange("p (a b c) -> p a b c", a=fshape[0], b=fshape[1])
            self.off += nbytes
            b = P.buf(name)
            b.r = dict(self.prev)
            self.live.append(b)
            return v, b

    AR = Arena()
    dbg_cnt = [0]

    def dbg(name, ap, bb, shape):
        if not c.debug:
            return
        t = nc.dram_tensor("dbg_" + name, list(shape), F32, kind="ExternalOutput").ap()
        P.dma("sp", t, ap, [bb], [P.buf("dbgd_" + name)], bb)

    psb = [ps("psb%d" % i, [128, 512]) for i in range(8)]
    B_ps = [P.buf("ps%d" % i) for i in range(8)]

    def phase_a_proj(b):
        AR.reset()
        hT, B_hT = AR.alloc("hT", [KC, T], BF16)
        gbc, B_gbc = AR.alloc("gbc", [D])
        P.dma("sp", gbc, normg_d.partition_broadcast(128), [], [B_gbc], B_gbc)
        xt, B_xt, xn, B_xn, st1, B_st1 = [], [], [], [], [], []
        for i in range(2):
            v, bb = AR.alloc("xt%d" % i, [D]); xt.append(v); B_xt.append(bb)
            v, bb = AR.alloc("xn%d" % i, [D], BF16); xn.append(v); B_xn.append(bb)
            v, bb = AR.alloc("st1_%d" % i, [4]); st1.append(v); B_st1.append(bb)
        junk, B_junk = AR.alloc("junk", [D], BF16)
        for it in range(NT):
            i = it % 2
            P.dma("sp", xt[i], x_d[b, it * 128:(it + 1) * 128, :], [], [B_xt[i]], B_xt[i])
            P.op("act", lambda e, i=i: e.activation(out=junk, in_=xt[i], func=AF.Square, accum_out=st1[i][:, 0:1]),
                 [B_xt[i]], [B_junk, B_st1[i]])
            P.op("act", lambda e, i=i: e.activation(out=st1[i][:, 1:2], in_=st1[i][:, 0:1], func=AF.Sqrt,
                                                     scale=1.0 / D, bias=NORM_EPS), [B_st1[i]], [B_st1[i]])
            P.op("dve", lambda e, i=i: e.reciprocal(out=st1[i][:, 2:3], in_=st1[i][:, 1:2]), [B_st1[i]], [B_st1[i]])
            P.op("dve", lambda e, i=i: e.scalar_tensor_tensor(out=xn[i], in0=xt[i], scalar=st1[i][:, 2:3], in1=gbc,
                                                               op0=ALU.mult, op1=ALU.mult),
                 [B_xt[i], B_st1[i], B_gbc], [B_xn[i]])
            for k0 in range(0, KC, 4):
                nk = min(4, KC - k0)
                pi = (k0 // 4) % 2
                pt = psb[pi][:, :].bitcast(BF16)
                for kk in range(nk):
                    P.op("pe", lambda e, i=i, kk=kk, k0=k0, pt=pt: e.transpose(
                        out=pt[:, kk * 128:(kk + 1) * 128], in_=xn[i][:, (k0 + kk) * 128:(k0 + kk + 1) * 128],
                        identity=ident_b[:, :]), [B_xn[i], B_identb], [B_ps[pi]], inc=(kk == nk - 1))
                src = pt[:, 0:nk * 128].rearrange("p (k t) -> p k t", k=nk)
                dst = hT[:, k0:k0 + nk, it * 128:(it + 1) * 128]
                if (k0 // 4) % 2 == 0:
                    P.op("act", lambda e, src=src, dst=dst: e.activation(out=dst, in_=src, func=AF.Copy),
                         [B_ps[pi]], [B_hT])
                else:
                    P.op("dve", lambda e, src=src, dst=dst: e.tensor_copy(out=dst, in_=src), [B_ps[pi]], [B_hT])
        for k0 in range(0, KC, KG):
            k1 = min(KC, k0 + KG)
            P.dma("sp", hT_s[b][k0 * 128:k1 * 128, :].rearrange("(kc kp) t -> kp kc t", kp=128), hT[:, k0:k1, :],
                  [B_hT], [B_hTs[b]], B_hT)

        wt, B_wt, pdst, B_pdst = [], [], [], []
        for i in range(2):
            v, bb = AR.alloc("wt%d" % i, [KC, 128], BF16); wt.append(v); B_wt.append(bb)
            v, bb = AR.alloc("pdst%d" % i, [T]); pdst.append(v); B_pdst.append(bb)
        ypad, B_ypad = AR.alloc("ypad", [T + 2])
        P.op("dve", lambda e: e.memset(ypad[:, 0:1], 0.0), [], [B_ypad])
        P.op("dve", lambda e: e.memset(ypad[:, T + 1:T + 2], 0.0), [], [B_ypad])
        ct_wd = c.O_WD // 128
        silu_tiles = set(range(c.O_ZR // 128, c.O_Q // 128)) | set(range(c.O_ZA // 128, c.O_GR // 128))
        for ct in range(NPT):
            i = ct % 2
            P.dma("sp", wt[i], wq_in[ct], [B_wqg[ct // WQG]], [B_wt[i]], B_wt[i])
            shifted = ct < NST
            dst = pdst[i]
            for tc in range(NTC):
                pb = (ct * NTC + tc) % 2
                for kc in range(KC):
                    P.op("pe", lambda e, i=i, kc=kc, tc=tc, pb=pb: e.matmul(
                        psb[pb][:, 0:TC], lhsT=wt[i][:, kc, :], rhs=hT[:, kc, tc * TC:(tc + 1) * TC],
                        start=(kc == 0), stop=(kc == KC - 1)), [B_wt[i], B_hT], [B_ps[pb]], inc=(kc == KC - 1))
                if shifted:
                    P.op("act", lambda e, tc=tc, pb=pb: e.activation(out=ypad[:, 1 + tc * TC:1 + (tc + 1) * TC],
                                                                     in_=psb[pb][:, 0:TC], func=AF.Copy),
                         [B_ps[pb]], [B_ypad])
                else:
                    fn = AF.Silu if ct in silu_tiles else AF.Copy
                    P.op("act", lambda e, tc=tc, pb=pb, dst=dst, fn=fn: e.activation(
                        out=dst[:, tc * TC:(tc + 1) * TC], in_=psb[pb][:, 0:TC], func=fn), [B_ps[pb]], [B_pdst[i]])
            if shifted:
                P.op("dve", lambda e, ct=ct, dst=dst: e.tensor_scalar(out=dst, in0=ypad[:, 1:T + 1],
                                                                      scalar1=ccT[:, ct:ct + 1], scalar2=None,
                                                                      op0=ALU.mult), [B_ypad, B_cc], [B_pdst[i]])
                P.op("dve", lambda e, ct=ct, dst=dst: e.scalar_tensor_tensor(
                    out=dst, in0=ypad[:, 0:T], scalar=pc[:, 2 * ct:2 * ct + 1], in1=dst, op0=ALU.mult, op1=ALU.add),
                    [B_ypad, B_pc, B_pdst[i]], [B_pdst[i]])
                P.op("dve", lambda e, ct=ct, dst=dst: e.scalar_tensor_tensor(
                    out=dst, in0=ypad[:, 2:T + 2], scalar=pc[:, 2 * ct + 1:2 * ct + 2], in1=dst, op0=ALU.mult,
                    op1=ALU.add), [B_ypad, B_pc, B_pdst[i]], [B_pdst[i]])
                if ct == ct_wd:
                    P.op("act", lambda e, dst=dst: e.activation(out=dst, in_=dst, func=AF.Tanh),
                         [B_pdst[i]], [B_pdst[i]])
            P.dma("sp", proj_s[ct], dst, [B_pdst[i]], [B_proj[ct]], B_pdst[i])

    stop = getattr(c, "stop", None)
    NCH = c.NCH
    G = 4
    NG = NCH // G

    def rwkv_phase(b):
        AR.reset()
        A = AR.alloc
        wdT, B_wd = A("wdT", [T]); adT, B_ad = A("adT", [T])
        rT, B_r = A("rT", [T]); kT, B_k = A("kT", [T]); vT, B_v = A("vT", [T]); kkT, B_kk = A("kkT", [T])
        vrT, B_vr = A("vrT", [T])
        W_, B_W = A("W_", [T]); A_, B_A = A("A_", [T]); KE_, B_KE = A("KE_", [T]); G0_, B_G0 = A("G0_", [T])
        GM_, B_GM = A("GM_", [T + 1]); CS_, B_CS = A("CS_", [T])
        oT = []; B_oT = []
        for d in range(2):
            v, bb = A("oT%d" % d, [T]); oT.append(v); B_oT.append(bb)
        m0, B_m0 = A("m0", [T]); m1, B_m1 = A("m1", [T])
        wup, B_wup = A("wup", [RW]); aup, B_aup = A("aup", [RW])
        gL, B_gL = A("gL", [NCH])
        S_, B_S = A("S_", [64]); Sg_, B_Sg = A("Sg_", [64]); Sbd, B_Sbd = A("Sbd", [128])
        BTt, B_BTt, KTt, B_KTt, Vt, B_Vt, CTm, B_CTm, DTm, B_DTm, Nm, B_Nm, BVs, B_BVs = ([] for _ in range(14))
        for i in range(2):
            for lst, bl, nm, shp in ((BTt, B_BTt, "BTt", [G, 128]), (KTt, B_KTt, "KTt", [G, 128]),
                                     (Vt, B_Vt, "Vt", [G, 128]), (CTm, B_CTm, "CTm", [512]),
                                     (DTm, B_DTm, "DTm", [512]), (Nm, B_Nm, "Nm", [512]), (BVs, B_BVs, "BVs", [512])):
                v, bb = A("%s%d" % (nm, i), shp, F32, 64); lst.append(v); bl.append(bb)
        ATm, B_ATm = A("ATm", [512], F32, 64); Am, B_Am = A("Am", [512], F32, 64)
        BTm, B_BTm = A("BTm", [512], F32, 64)
        Pq, B_Pq, PTq, B_PTq = [], [], [], []
        for i in range(2):
            v, bb = A("Pq%d" % i, [512], F32, 64); Pq.append(v); B_Pq.append(bb)
            v, bb = A("PTq%d" % i, [512], F32, 64); PTq.append(v); B_PTq.append(bb)
        RHSs, B_RHS = A("RHSs", [128], F32, 64); Us, B_Us = A("Us", [128], F32, 64)
        ob, B_ob = A("ob", [T], BF16)

        P.dma("sp", m0, cm0_d[:, :], [], [B_m0], B_m0)
        P.op("dve", lambda e: e.tensor_scalar(out=m1, in0=m0, scalar1=-1.0, scalar2=1.0, op0=ALU.mult, op1=ALU.add),
             [B_m0], [B_m1])
        P.dma("sp", wup, wup_d[:, :], [], [B_wup], B_wup)
        P.dma("sp", aup, aup_d[:, :], [], [B_aup], B_aup)
        P.dma("sp", wdT, proj_s[c.O_WD // 128], [B_proj[c.O_WD // 128]], [B_wd], B_wd)
        P.dma("sp", adT, proj_s[c.O_AD // 128], [B_proj[c.O_AD // 128]], [B_ad], B_ad)
        P.op("dve", lambda e: e.memset(GM_[:, 0:1], 1.0), [], [B_GM])

        def dv(fn, reads, writes):
            P.op("dve", fn, reads, writes)

        for hp in range(NHP):
            hs = [slice(0, 64), slice(64, 128)]
            for buf, bb, off in ((rT, B_r, c.O_R), (kT, B_k, c.O_K), (vT, B_v, c.O_V)):
                ctl = off // 128 + hp
                P.dma("sp", buf, proj_s[ctl], [B_proj[ctl]], [bb], bb)
            dv(lambda e, hp=hp: e.tensor_scalar(out=kkT, in0=kT, scalar1=pcc(hp, 4), scalar2=None, op0=ALU.mult),
               [B_k, B_pc], [B_kk])
            dv(lambda e: e.tensor_tensor(out=W_, in0=kkT, in1=kkT, op=ALU.mult), [B_kk], [B_W])
            for tc in range(NTC):
                sl = slice(tc * TC, (tc + 1) * TC)
                P.op("pe", lambda e, sl=sl: e.matmul(psb[0][:, 0:TC], lhsT=bdones[:, :], rhs=W_[:, sl], start=True,
                                                     stop=True), [B_bd, B_W], [B_ps[0]])
                dv(lambda e, sl=sl: e.tensor_scalar_max(out=A_[:, sl], in0=psb[0][:, 0:TC], scalar1=1e-24),
                   [B_ps[0]], [B_A])
            P.op("act", lambda e: e.activation(out=A_, in_=A_, func=AF.Sqrt), [B_A], [B_A])
            dv(lambda e: e.reciprocal(out=A_, in_=A_), [B_A], [B_A])
            dv(lambda e: e.tensor_tensor(out=kkT, in0=kkT, in1=A_, op=ALU.mult), [B_kk, B_A], [B_kk])
            dv(lambda e: e.tensor_copy(out=vrT, in_=rev_ap(vT)), [B_v], [B_vr])

            for d in range(2):
                dsl = slice(d * 64, (d + 1) * 64)
                rsrc = (lambda ap: ap) if d == 0 else rev_ap
                vsrc = vT if d == 0 else vrT
                B_vsrc = B_v if d == 0 else B_vr
                for tc in range(NTC):
                    sl = slice(tc * TC, (tc + 1) * TC)
                    osl = sl if d == 0 else slice(T - (tc + 1) * TC, T - tc * TC)
                    P.op("pe", lambda e, sl=sl, hp=hp, dsl=dsl: e.matmul(
                        psb[0][:, 0:TC], lhsT=wup[dsl, hp * 128:(hp + 1) * 128], rhs=wdT[dsl, sl], start=True,
                        stop=True), [B_wup, B_wd], [B_ps[0]])
                    P.op("act", lambda e, osl=osl, hp=hp, d=d: e.activation(
                        out=W_[:, osl], in_=rsrc(psb[0][:, 0:TC]), func=AF.Sigmoid, bias=pcc(hp, 0 + d), scale=1.0),
                        [B_ps[0], B_pc], [B_W])
                    P.op("pe", lambda e, sl=sl, hp=hp, dsl=dsl: e.matmul(
                        psb[1][:, 0:TC], lhsT=aup[dsl, hp * 128:(hp + 1) * 128], rhs=adT[dsl, sl], start=True,
                        stop=True), [B_aup, B_ad], [B_ps[1]])
                    P.op("act", lambda e, osl=osl, hp=hp, d=d: e.activation(
                        out=A_[:, osl], in_=rsrc(psb[1][:, 0:TC]), func=AF.Sigmoid, bias=pcc(hp, 2 + d), scale=1.0),
                        [B_ps[1], B_pc], [B_A])
                P.op("act", lambda e: e.activation(out=W_, in_=W_, func=AF.Exp, scale=-math.exp(-0.5)), [B_W], [B_W])
                if hp == 0 and b == 0:
                    dbg("w%d" % d, W_, B_W, [128, T]); dbg("a%d" % d, A_, B_A, [128, T])
                dv(lambda e, hp=hp: e.tensor_scalar(out=KE_, in0=A_, scalar1=pcc(hp, 5), scalar2=omka[:, hp:hp + 1],
                                                    op0=ALU.mult, op1=ALU.add), [B_A, B_pc, B_omka], [B_KE])
                dv(lambda e: e.tensor_tensor(out=KE_, in0=KE_, in1=rsrc(kT), op=ALU.mult), [B_KE, B_k], [B_KE])
                dv(lambda e, hp=hp: e.scalar_tensor_tensor(out=G0_, in0=KE_, scalar=pcc(hp, 6), in1=rsrc(rT),
                                                           op0=ALU.mult, op1=ALU.mult), [B_KE, B_pc, B_r], [B_G0])
                if d == 0:
                    dv(lambda e: e.tensor_copy(out=CS_, in_=G0_), [B_G0], [B_CS])
                else:
                    dv(lambda e: e.tensor_tensor(out=CS_, in0=CS_, in1=rev_ap(G0_), op=ALU.add), [B_CS, B_G0], [B_CS])
                dv(lambda e: e.tensor_tensor(out=A_, in0=A_, in1=rsrc(kkT), op=ALU.mult), [B_A, B_kk], [B_A])
                dv(lambda e: e.tensor_tensor(out=G0_, in0=W_, in1=m0, op=ALU.mult), [B_W, B_m0], [B_G0])
                dv(lambda e: e.tensor_tensor(out=W_, in0=W_, in1=G0_, op=ALU.subtract), [B_W, B_G0], [B_W])
                dv(lambda e: e.tensor_tensor_scan(out=GM_[:, 1:T + 1], data0=G0_, data1=W_, initial=0.0,
                                                  op0=ALU.mult, op1=ALU.add), [B_G0, B_W], [B_GM])
                dv(lambda e: e.tensor_tensor(out=G0_, in0=GM_[:, 0:T], in1=m0, op=ALU.mult), [B_GM, B_m0], [B_G0])
                dv(lambda e: e.tensor_tensor(out=G0_, in0=G0_, in1=m1, op=ALU.add), [B_G0, B_m1], [B_G0])
                dv(lambda e: e.scalar_tensor_tensor(out=G0_, in0=rsrc(kkT), scalar=-1.0, in1=G0_, op0=ALU.mult,
                                                    op1=ALU.mult), [B_kk, B_G0], [B_G0])
                dv(lambda e: e.tensor_tensor(out=W_, in0=rsrc(rT), in1=GM_[:, 1:T + 1], op=ALU.mult),
                   [B_r, B_GM], [B_W])
                dv(lambda e: e.tensor_copy(out=gL, in_=GM_[:, 1:T + 1].rearrange("p (n l) -> p n l", l=CL)[:, :, CL - 1]),
                   [B_GM], [B_gL])
                dv(lambda e: e.reciprocal(out=GM_[:, 1:T + 1], in_=GM_[:, 1:T + 1]), [B_GM], [B_GM])
                dv(lambda e: e.tensor_tensor(out=A_, in0=A_, in1=GM_[:, 1:T + 1], op=ALU.mult), [B_A, B_GM], [B_A])
                dv(lambda e: e.tensor_tensor(out=KE_, in0=KE_, in1=GM_[:, 1:T + 1], op=ALU.mult), [B_KE, B_GM], [B_KE])
                at, bt, kt, rt = G0_, A_, KE_, W_
                B_at, B_bt, B_kt, B_rt = B_G0, B_A, B_KE, B_W
                dv(lambda e: e.memset(S_, 0.0), [], [B_S])
                dv(lambda e: e.memset(Sbd, 0.0), [], [B_Sbd])

                def precompute(g, s):
                    t0 = g * G * CL
                    for src, B_src, dstl, B_dstl, pb in ((bt, B_bt, BTt, B_BTt, 3), (kt, B_kt, KTt, B_KTt, 4),
                                                         (vsrc, B_vsrc, Vt, B_Vt, 3)):
                        for j in range(G):
                            P.op("pe", lambda e, src=src, j=j, pb=pb: e.matmul(
                                psb[pb][0:64, j * 128:(j + 1) * 128], lhsT=src[:, t0 + j * CL:t0 + (j + 1) * CL],
                                rhs=ident_f[:, :], start=True, stop=True), [B_src, B_identf], [B_ps[pb]],
                                inc=(j == G - 1))
                        P.op("act", lambda e, dstl=dstl, pb=pb: e.activation(
                            out=dstl[s], in_=psb[pb][0:64, :].rearrange("p (g c) -> p g c", g=G), func=AF.Copy),
                            [B_ps[pb]], [B_dstl[s]])
                        yield
                    if stop == "P1":
                        return
                    specs = ((0, bt, B_bt, at, B_at), (1, at, B_at, bt, B_bt), (2, kt, B_kt, at, B_at),
                             (3, bt, B_bt, rt, B_rt), (4, kt, B_kt, rt, B_rt))
                    for h in range(2):
                        for pb, L, B_L, R, B_R in specs:
                            for j in range(G):
                                tsl = slice(t0 + j * CL, t0 + (j + 1) * CL)
                                col = (j * 2 + h) * 64
                                P.op("pe", lambda e, pb=pb, L=L, R=R, tsl=tsl, h=h, col=col: e.matmul(
                                    psb[pb][0:64, col:col + 64], lhsT=L[hs[h], tsl], rhs=R[hs[h], tsl], start=True,
                                    stop=True), [B_L, B_R], [B_ps[pb]], inc=(j == G - 1 and h == 1))
                    yield
                    dv(lambda e: e.tensor_tensor(out=ATm, in0=psb[0][0:64, :], in1=masks[:, 0, :], op=ALU.mult),
                       [B_ps[0], B_mask], [B_ATm])
                    dv(lambda e: e.tensor_tensor(out=Am, in0=psb[1][0:64, :], in1=masks[:, 1, :], op=ALU.mult),
                       [B_ps[1], B_mask], [B_Am])
                    dv(lambda e: e.tensor_tensor(out=BTm, in0=psb[2][0:64, :], in1=masks[:, 0, :], op=ALU.mult),
                       [B_ps[2], B_mask], [B_BTm])
                    dv(lambda e: e.tensor_tensor(out=CTm[s], in0=psb[3][0:64, :], in1=masks[:, 2, :], op=ALU.mult),
                       [B_ps[3], B_mask], [B_CTm[s]])
                    dv(lambda e: e.tensor_tensor(out=DTm[s], in0=psb[4][0:64, :], in1=masks[:, 2, :], op=ALU.mult),
                       [B_ps[4], B_mask], [B_DTm[s]])
                    yield
                    dv(lambda e: e.tensor_tensor(out=Nm[s], in0=ATm, in1=I8[:, :], op=ALU.add), [B_ATm, B_I8], [B_Nm[s]])
                    if stop == "P2":
                        return
                    for j in range(G):
                        for h in range(2):
                            col = (j * 2 + h) * 64
                            P.op("pe", lambda e, j=j, h=h, col=col: e.matmul(
                                psb[2][0:64, col:col + 64], lhsT=BTm[:, col:col + 64],
                                rhs=Vt[s][:, j, h * 64:(h + 1) * 64], start=True, stop=True),
                                [B_BTm, B_Vt[s]], [B_ps[2]], inc=(j == G - 1 and h == 1))
                    P.op("act", lambda e: e.activation(out=BVs[s], in_=psb[2][0:64, :], func=AF.Copy),
                         [B_ps[2]], [B_BVs[s]])
                    yield
                    if stop == "P3":
                        return
                    Pc, B_Pc, PTc, B_PTc = Am, B_Am, ATm, B_ATm
                    for lev in range(5):
                        q = lev % 2
                        last = lev == 4
                        for p8 in range(2 * G):
                            col = p8 * 64
                            P.op("pe", lambda e, Pc=Pc, PTc=PTc, col=col: e.matmul(
                                psb[0][0:64, col:col + 64], lhsT=PTc[:, col:col + 64], rhs=Pc[:, col:col + 64],
                                start=True, stop=True), [B_Pc, B_PTc], [B_ps[0]], inc=(p8 == 2 * G - 1))
                        if not last:
                            for p8 in range(2 * G):
                                col = p8 * 64
                                P.op("pe", lambda e, Pc=Pc, PTc=PTc, col=col: e.matmul(
                                    psb[1][0:64, col:col + 64], lhsT=Pc[:, col:col + 64], rhs=PTc[:, col:col + 64],
                                    start=True, stop=True), [B_Pc, B_PTc], [B_ps[1]], inc=(p8 == 2 * G - 1))
                        yield
                        dv(lambda e, q=q: e.tensor_copy(out=Pq[q], in_=psb[0][0:64, :]), [B_ps[0]], [B_Pq[q]])
                        if not last:
                            P.op("act", lambda e, q=q: e.activation(out=PTq[q], in_=psb[1][0:64, :], func=AF.Copy),
                                 [B_ps[1]], [B_PTq[q]])
                        for p8 in range(2 * G):
                            col = p8 * 64
                            P.op("pe", lambda e, q=q, col=col: e.matmul(
                                psb[4][0:64, col:col + 64], lhsT=Pq[q][:, col:col + 64], rhs=Nm[s][:, col:col + 64],
                                start=True, stop=True), [B_Pq[q], B_Nm[s]], [B_ps[4]], inc=(p8 == 2 * G - 1))
                        yield
                        dv(lambda e: e.tensor_tensor(out=Nm[s], in0=Nm[s], in1=psb[4][0:64, :], op=ALU.add),
                           [B_Nm[s], B_ps[4]], [B_Nm[s]])
                        Pc, B_Pc, PTc, B_PTc = Pq[q], B_Pq[q], PTq[q], B_PTq[q]

                def chain(g, s):
                    for j in range(G):
                        ch = g * G + j
                        tsl = slice(ch * CL, (ch + 1) * CL)
                        P.op("pe", lambda e, tsl=tsl: e.matmul(
                            psb[5][0:64, 0:128], lhsT=at[:, tsl], rhs=Sbd, start=True, stop=True),
                            [B_at, B_Sbd], [B_ps[5]])
                        yield
                        dv(lambda e, j=j: e.tensor_tensor(out=RHSs, in0=psb[5][0:64, 0:128],
                                                          in1=BVs[s][:, j * 128:(j + 1) * 128], op=ALU.add),
                           [B_ps[5], B_BVs[s]], [B_RHS])
                        for h in range(2):
                            col = (j * 2 + h) * 64
                            P.op("pe", lambda e, h=h, col=col: e.matmul(
                                psb[5][0:64, 128 + h * 64:128 + (h + 1) * 64], lhsT=Nm[s][:, col:col + 64],
                                rhs=RHSs[:, h * 64:(h + 1) * 64], start=True, stop=True),
                                [B_Nm[s], B_RHS], [B_ps[5]], inc=(h == 1))
                        yield
                        P.op("act", lambda e: e.activation(out=Us, in_=psb[5][0:64, 128:256], func=AF.Copy),
                             [B_ps[5]], [B_Us])
                        dv(lambda e, ch=ch: e.tensor_scalar(out=Sg_, in0=S_, scalar1=gL[:, ch:ch + 1], scalar2=None,
                                                            op0=ALU.mult), [B_S, B_gL], [B_Sg])
                        for h in range(2):
                            hc = slice(h * 64, (h + 1) * 64)
                            P.op("pe", lambda e, h=h, hc=hc, j=j: e.matmul(
                                psb[6][hs[h], 0:64], lhsT=KTt[s][:, j, hc], rhs=Vt[s][:, j, hc], start=True,
                                stop=False, tile_position=(0, h * 64)), [B_KTt[s], B_Vt[s]], [B_ps[6]], inc=False)
                            P.op("pe", lambda e, h=h, hc=hc, j=j: e.matmul(
                                psb[6][hs[h], 0:64], lhsT=BTt[s][:, j, hc], rhs=Us[:, hc], start=False, stop=True,
                                tile_position=(0, h * 64)), [B_BTt[s], B_Us], [B_ps[6]], inc=(h == 1))
                        oc = (ch % 8) * 64
                        for h in range(2):
                            hc = slice(h * 64, (h + 1) * 64)
                            col = (j * 2 + h) * 64
                            P.op("pe", lambda e, h=h, hc=hc, tsl=tsl, oc=oc: e.matmul(
                                psb[7][hs[h], oc:oc + 64], lhsT=Sbd[:, hc], rhs=rt[:, tsl], start=True,
                                stop=False, tile_position=(0, h * 64)), [B_Sbd, B_rt], [B_ps[7]], inc=False)
                            P.op("pe", lambda e, h=h, hc=hc, col=col, oc=oc: e.matmul(
                                psb[7][hs[h], oc:oc + 64], lhsT=Us[:, hc], rhs=CTm[s][:, col:col + 64], start=False,
                                stop=False, tile_position=(0, h * 64)), [B_Us, B_CTm[s]], [B_ps[7]], inc=False)
                            P.op("pe", lambda e, h=h, hc=hc, col=col, oc=oc, j=j: e.matmul(
                                psb[7][hs[h], oc:oc + 64], lhsT=Vt[s][:, j, hc], rhs=DTm[s][:, col:col + 64],
                                start=False, stop=True, tile_position=(0, h * 64)), [B_Vt[s], B_DTm[s]], [B_ps[7]],
                                inc=(h == 1))
                        yield
                        dv(lambda e, ch=ch: e.scalar_tensor_tensor(out=S_, in0=psb[6][:, 0:64], scalar=gL[:, ch:ch + 1],
                                                                   in1=Sg_, op0=ALU.mult, op1=ALU.add),
                           [B_ps[6], B_gL, B_Sg], [B_S])
                        for h in range(2):
                            P.op("act", lambda e, h=h: e.activation(out=Sbd[hs[h], h * 64:(h + 1) * 64], in_=S_[hs[h], :],
                                                                    func=AF.Copy), [B_S], [B_Sbd])
                        if ch % 8 == 7 or ch == NCH - 1:
                            nch8 = ch % 8 + 1
                            c0 = (ch - nch8 + 1) * CL
                            dst = oT[d][:, c0:c0 + nch8 * CL]
                            if d == 1:
                                dst = rev_ap(oT[d][:, T - c0 - nch8 * CL:T - c0])
                            P.op("act", lambda e, dst=dst, nch8=nch8: e.activation(
                                out=dst, in_=psb[7][:, 0:nch8 * CL], func=AF.Copy), [B_ps[7]], [B_oT[d]])

                if hp == 0 and b == 0 and d == 0:
                    dbg("at", at, B_at, [128, T]); dbg("bt", bt, B_bt, [128, T]); dbg("kt", kt, B_kt, [128, T])
                    dbg("rt", rt, B_rt, [128, T]); dbg("v", vT, B_v, [128, T]); dbg("gL", gL, B_gL, [128, NCH])
                if stop == "R1":
                    break
                for _ in precompute(0, 0):
                    pass
                for g in range(NG):
                    if stop in ("R2", "P1", "P2", "P3"):
                        break
                    ch_it = chain(g, g % 2)
                    pre_it = precompute(g + 1, (g + 1) % 2) if g + 1 < NG else iter(())
                    ch_done = pre_done = False
                    while not (ch_done and pre_done):
                        if not ch_done:
                            try:
                                next(ch_it)
                            except StopIteration:
                                ch_done = True
                        for _ in range(2):
                            if not pre_done:
                                try:
                                    next(pre_it)
                                except StopIteration:
                                    pre_done = True

            if stop in ("R1", "R2", "P1", "P2", "P3"):
                break
            if hp == 0 and b == 0:
                dbg("o0", oT[0], B_oT[0], [128, T]); dbg("o1", oT[1], B_oT[1], [128, T])
                dbg("kk", kkT, B_kk, [128, T]); dbg("r", rT, B_r, [128, T]); dbg("cs", CS_, B_CS, [128, T])
            zr, B_zr = W_, B_W
            ctl = c.O_ZR // 128 + hp
            P.dma("sp", zr, proj_s[ctl], [B_proj[ctl]], [B_zr], B_zr)
            wkv, B_wkv = oT[0], B_oT[0]
            dv(lambda e: e.tensor_tensor(out=wkv, in0=oT[0], in1=oT[1], op=ALU.add), [B_oT[0], B_oT[1]], [B_wkv])
            cen, B_cen = A_, B_A
            for tc in range(NTC):
                sl = slice(tc * TC, (tc + 1) * TC)
                P.op("pe", lambda e, sl=sl: e.matmul(psb[0][:, 0:TC], lhsT=bdones[:, :], rhs=wkv[:, sl], start=True,
                                                     stop=True), [B_bd, B_wkv], [B_ps[0]])
                dv(lambda e, sl=sl: e.scalar_tensor_tensor(out=cen[:, sl], in0=psb[0][:, 0:TC], scalar=-1.0 / HD,
                                                           in1=wkv[:, sl], op0=ALU.mult, op1=ALU.add),
                   [B_ps[0], B_wkv], [B_cen])
            sq, B_sq = KE_, B_KE
            dv(lambda e: e.tensor_tensor(out=sq, in0=cen, in1=cen, op=ALU.mult), [B_cen], [B_sq])
            rs, B_rs = G0_, B_G0
            for tc in range(NTC):
                sl = slice(tc * TC, (tc + 1) * TC)
                P.op("pe", lambda e, sl=sl: e.matmul(psb[1][:, 0:TC], lhsT=bdones[:, :], rhs=sq[:, sl], start=True,
                                                     stop=True), [B_bd, B_sq], [B_ps[1]])
                P.op("act", lambda e, sl=sl: e.activation(out=rs[:, sl], in_=psb[1][:, 0:TC], func=AF.Sqrt,
                                                           scale=1.0 / HD, bias=GN_EPS), [B_ps[1]], [B_rs])
            dv(lambda e: e.reciprocal(out=rs, in_=rs), [B_rs], [B_rs])
            dv(lambda e: e.tensor_tensor(out=cen, in0=cen, in1=rs, op=ALU.mult), [B_cen, B_rs], [B_cen])
            dv(lambda e, hp=hp: e.tensor_scalar(out=cen, in0=cen, scalar1=pcc(hp, 7), scalar2=pcc(hp, 8),
                                                op0=ALU.mult, op1=ALU.add), [B_cen, B_pc], [B_cen])
            for tc in range(NTC):
                sl = slice(tc * TC, (tc + 1) * TC)
                P.op("pe", lambda e, sl=sl: e.matmul(psb[0][:, 0:TC], lhsT=bdones[:, :], rhs=CS_[:, sl], start=True,
                                                     stop=True), [B_bd, B_CS], [B_ps[0]])
                dv(lambda e, sl=sl: e.tensor_tensor(out=sq[:, sl], in0=psb[0][:, 0:TC], in1=vT[:, sl], op=ALU.mult),
                   [B_ps[0], B_v], [B_sq])
            dv(lambda e: e.tensor_tensor(out=cen, in0=cen, in1=sq, op=ALU.add), [B_cen, B_sq], [B_cen])
            dv(lambda e: e.tensor_tensor(out=ob, in0=cen, in1=zr, op=ALU.mult), [B_cen, B_zr], [B_ob])
            P.dma("sp", orT_s[b][hp * 128:(hp + 1) * 128, :], ob, [B_ob], [B_orTs[b]], B_ob)

    NKP = c.KVW // 128
    NQP = c.AW // 128
    SCALE = float(HD) ** -0.5

    def attn_phase(b):
        AR.reset()
        A = AR.alloc
        cosT, B_cos = A("cosT", [T]); sinT, B_sin = A("sinT", [T])
        P.dma("sp", cosT, ccos_d[:, :], [], [B_cos], B_cos)
        P.dma("sp", sinT, csin_d[:, :], [], [B_sin], B_sin)
        src, B_src = A("asrc", [T]); t1, B_t1 = A("at1", [T]); t2, B_t2 = A("at2", [T])
        za, B_za = A("za", [T])
        nb, B_nb = A("anb", [T], BF16)
        og, B_og = A("og", [T], BF16)
        qz, B_qz = [], []
        for p in range(2):
            v, bb = A("qz%d" % p, [T], BF16); qz.append(v); B_qz.append(bb)
        kd, B_kd = [], []
        for kvh in range(c.KVH):
            v, bb = A("kd%d" % kvh, [T], BF16); kd.append(v); B_kd.append(bb)
        Va, B_Va = [], []
        for kvh in range(c.KVH):
            row, brow = [], []
            for p in range(2):
                v, bb = A("Va%d_%d" % (kvh, p), [NT, 128], BF16); row.append(v); brow.append(bb)
            Va.append(row); B_Va.append(brow)
        pT, B_pT = [], []
        for i in range(3):
            v, bb = A("pT%d" % i, [TC], BF16); pT.append(v); B_pT.append(bb)
        rc, B_rc = A("rc", [TC]); on, B_on = A("on", [TC])

        def dv(fn, reads, writes):
            P.op("dve", fn, reads, writes)

        def qk_prep(ct, gcol):
            P.dma("sp", src, proj_s[ct], [B_proj[ct]], [B_src], B_src)
            dv(lambda e: e.tensor_tensor(out=t1, in0=src, in1=src, op=ALU.mult), [B_src], [B_t1])
            for tc in range(NTC):
                sl = slice(tc * TC, (tc + 1) * TC)
                P.op("pe", lambda e, sl=sl: e.matmul(psb[5][:, 0:TC], lhsT=bdones[:, :], rhs=t1[:, sl], start=True,
                                                     stop=True), [B_bd, B_t1], [B_ps[5]])
                P.op("act", lambda e, sl=sl: e.activation(out=t2[:, sl], in_=psb[5][:, 0:TC], func=AF.Sqrt,
                                                           scale=1.0 / HD, bias=NORM_EPS), [B_ps[5]], [B_t2])
            dv(lambda e: e.reciprocal(out=t2, in_=t2), [B_t2], [B_t2])
            dv(lambda e: e.scalar_tensor_tensor(out=t2, in0=src, scalar=pc[:, gcol:gcol + 1], in1=t2, op0=ALU.mult,
                                                op1=ALU.mult), [B_src, B_pc, B_t2], [B_t2])
            for tc in range(NTC):
                sl = slice(tc * TC, (tc + 1) * TC)
                P.op("pe", lambda e, sl=sl: e.matmul(psb[6][:, 0:TC], lhsT=rotm[:, :], rhs=t2[:, sl], start=True,
                                                     stop=True), [B_rot, B_t2], [B_ps[6]])
                dv(lambda e, sl=sl: e.tensor_tensor(out=t1[:, sl], in0=psb[6][:, 0:TC], in1=sinT[:, sl], op=ALU.mult),
                   [B_ps[6], B_sin], [B_t1])
            dv(lambda e: e.tensor_tensor(out=t2, in0=t2, in1=cosT, op=ALU.mult), [B_t2, B_cos], [B_t2])
            dv(lambda e: e.tensor_tensor(out=t2, in0=t2, in1=t1, op=ALU.add), [B_t2, B_t1], [B_t2])
            dv(lambda e: e.tensor_copy(out=nb, in_=t2), [B_t2], [B_nb])

        GQ = PCB + 9 * NHP
        for kp in range(NKP):
            qk_prep(c.O_AK // 128 + kp, GQ + 1)
            for h2 in range(2):
                kvh = kp * 2 + h2
                hsl = slice(h2 * 64, (h2 + 1) * 64)
                osl = slice((1 - h2) * 64, (2 - h2) * 64)
                dv(lambda e, kvh=kvh, hsl=hsl: e.tensor_copy(out=kd[kvh][hsl, :], in_=t2[hsl, :]), [B_t2], [B_kd[kvh]])
                dv(lambda e, hsl=hsl, osl=osl: e.tensor_copy(out=t1[osl, :], in_=t2[hsl, :]), [B_t2], [B_t1])
                dv(lambda e, kvh=kvh, osl=osl: e.tensor_copy(out=kd[kvh][osl, :], in_=t1[osl, :]), [B_t1], [B_kd[kvh]])
        if stop == "T1":
            return
        for kvh in range(c.KVH):
            for p in range(2):
                dv(lambda e, kvh=kvh, p=p: e.memset(Va[kvh][p].rearrange("p a b -> p (a b)"), 1.0), [], [B_Va[kvh][p]])
        if stop == "V1":
            return
        for kp in range(NKP):
            ct = c.O_AV // 128 + kp
            P.dma("sp", src, proj_s[ct], [B_proj[ct]], [B_src], B_src)
            if stop == "V2":
                return
            for it in range(NT):
                P.op("pe", lambda e, it=it: e.matmul(psb[7][:, 0:128], lhsT=src[:, it * 128:(it + 1) * 128],
                                                     rhs=ident_f[:, :], start=True, stop=True),
                     [B_src, B_identf], [B_ps[7]])
                if stop == "V3":
                    continue
                for h2 in range(2):
                    kvh = kp * 2 + h2
                    cs_ = slice(h2 * 64, (h2 + 1) * 64)
                    dv(lambda e, kvh=kvh, it=it, cs_=cs_: e.tensor_copy(out=Va[kvh][0][:, it, 0:64], in_=psb[7][:, cs_]),
                       [B_ps[7]], [B_Va[kvh][0]])
                    dv(lambda e, kvh=kvh, it=it, cs_=cs_: e.tensor_copy(out=Va[kvh][1][:, it, 64:128], in_=psb[7][:, cs_]),
                       [B_ps[7]], [B_Va[kvh][1]])
        if stop == "T2":
            return
        cnt = 0
        for qp in range(NQP):
            qk_prep(c.O_Q // 128 + qp, GQ)
            ctz = c.O_ZA // 128 + qp
            P.dma("sp", za, proj_s[ctz], [B_proj[ctz]], [B_za], B_za)
            for p in range(2):
                dv(lambda e, p=p: e.memset(qz[p], 0.0), [], [B_qz[p]])
                dv(lambda e, p=p: e.tensor_copy(out=qz[p][p * 64:(p + 1) * 64, :], in_=nb[p * 64:(p + 1) * 64, :]),
                   [B_nb], [B_qz[p]])
            for p in range(2):
                qh = qp * 2 + p
                kvh = qh // c.GROUP
                qsl = slice(p * 64, (p + 1) * 64)
                ssl = slice((1 - p) * 64, (2 - p) * 64)
                for qc in range(NTC):
                    qs = slice(qc * TC, (qc + 1) * TC)
                    acc = 3 + (cnt % 2)
                    def emit_s(kt):
                        sb_ = (cnt * NT + kt) % 3
                        P.op("pe", lambda e, sb_=sb_, kt=kt, qs=qs, kvh=kvh, p=p: e.matmul(
                            psb[sb_][:, 0:TC], lhsT=kd[kvh][:, kt * 128:(kt + 1) * 128], rhs=qz[p][:, qs], start=True,
                            stop=True), [B_kd[kvh], B_qz[p]], [B_ps[sb_]])

                    emit_s(0)
                    for kt in range(NT):
                        sb_ = (cnt * NT + kt) % 3
                        if kt + 1 < NT:
                            emit_s(kt + 1)
                        P.op("act", lambda e, sb_=sb_: e.activation(out=pT[sb_], in_=psb[sb_][:, 0:TC], func=AF.Exp,
                                                                    scale=SCALE), [B_ps[sb_]], [B_pT[sb_]])
                        P.op("pe", lambda e, sb_=sb_, kt=kt, acc=acc, kvh=kvh: e.matmul(
                            psb[acc][:, 0:TC], lhsT=Va[kvh][p][:, kt, :], rhs=pT[sb_], start=(kt == 0),
                            stop=(kt == NT - 1)), [B_Va[kvh][p], B_pT[sb_]], [B_ps[acc]], inc=(kt == NT - 1))
                    dv(lambda e, acc=acc: e.reciprocal(out=rc[ssl, :], in_=psb[acc][ssl, 0:TC]), [B_ps[acc]], [B_rc])
                    dv(lambda e: e.tensor_copy(out=rc[qsl, :], in_=rc[ssl, :]), [B_rc], [B_rc])
                    dv(lambda e, acc=acc: e.tensor_tensor(out=on[qsl, :], in0=psb[acc][qsl, 0:TC], in1=rc[qsl, :],
                                                          op=ALU.mult), [B_ps[acc], B_rc], [B_on])
                    dv(lambda e, qs=qs: e.tensor_tensor(out=og[qsl, qs], in0=on[qsl, :], in1=za[qsl, qs], op=ALU.mult),
                       [B_on, B_za], [B_og])
                    cnt += 1
            P.dma("sp", oaT_s[b][qp * 128:(qp + 1) * 128, :], og, [B_og], [B_oaTs[b]], B_og)

    CG = min(512, D)
    NCG = D // CG
    KR = RW // 128
    KA = c.AW // 128

    def phase_d_all():
        AR.reset()
        A = AR.alloc
        wout, B_wout = A("wout", [KC, D], BF16)
        fbc, B_fbc = A("fbc", [D])
        for k0 in range(0, KC, KG):
            k1 = min(KC, k0 + KG)
            P.dma("sp", wout[:, k0:k1, :], wq_out[k0 * 128:k1 * 128, :].rearrange("(kc kp) n -> kp kc n", kp=128),
                  [B_wq], [B_wout], B_wout)
        P.dma("sp", fbc, fing_d.partition_broadcast(128), [], [B_fbc], B_fbc)
        hTc, B_hTc = A("hTc", [KC, TC], BF16)
        orc, B_orc = A("orc", [KR, TC], BF16); oac, B_oac = A("oac", [KA, TC], BF16)
        mg, B_mg = A("mg", [KC, TC], BF16)
        wgr, B_wgr, wga, B_wga, wbr, B_wbr, wba, B_wba = ([] for _ in range(8))
        for i in range(2):
            v, bb = A("wgr%d" % i, [KC, 128], BF16); wgr.append(v); B_wgr.append(bb)
            v, bb = A("wga%d" % i, [KC, 128], BF16); wga.append(v); B_wga.append(bb)
            v, bb = A("wbr%d" % i, [KR, 128], BF16); wbr.append(v); B_wbr.append(bb)
            v, bb = A("wba%d" % i, [KA, 128], BF16); wba.append(v); B_wba.append(bb)
        s1, B_s1 = A("s1", [TC]); s2, B_s2 = A("s2", [TC]); u1, B_u1 = A("u1", [TC]); u2, B_u2 = A("u2", [TC])
        xts, B_xts, yps, B_yps, st2s, B_st2s = ([] for _ in range(6))
        for i in range(2):
            v, bb = A("dxt%d" % i, [D]); xts.append(v); B_xts.append(bb)
            v, bb = A("yp%d" % i, [D]); yps.append(v); B_yps.append(bb)
            v, bb = A("st2_%d" % i, [4]); st2s.append(v); B_st2s.append(bb)
        junk, B_junk = A("djunk", [D], BF16)

        def dv(fn, reads, writes):
            P.op("dve", fn, reads, writes)

        for b in range(c.BPC):
            for tcn in range(NTC):
                ts = slice(tcn * TC, (tcn + 1) * TC)
                for dst_, src_, nk, B_s, B_d in ((hTc, hT_s, KC, B_hTs[b], B_hTc), (orc, orT_s, KR, B_orTs[b], B_orc),
                                                 (oac, oaT_s, KA, B_oaTs[b], B_oac)):
                    for k0 in range(0, nk, KG):
                        k1 = min(nk, k0 + KG)
                        P.dma("sp", dst_[:, k0:k1, :],
                              src_[b][k0 * 128:k1 * 128, ts].rearrange("(kc kp) t -> kp kc t", kp=128),
                              [B_s], [B_d], B_d)
                for j in range(KC):
                    i = j % 2
                    P.dma("sp", wgr[i], wq_in[c.O_GR // 128 + j], [B_wqg[(c.O_GR // 128 + j) // WQG]], [B_wgr[i]], B_wgr[i])
                    P.dma("sp", wga[i], wq_in[c.O_GA // 128 + j], [B_wqg[(c.O_GA // 128 + j) // WQG]], [B_wga[i]], B_wga[i])
                    P.dma("sp", wbr[i], wq_br[:, j * 128:(j + 1) * 128].rearrange("(kc kp) n -> kp kc n", kp=128),
                          [B_wq], [B_wbr[i]], B_wbr[i])
                    P.dma("sp", wba[i], wq_ba[:, j * 128:(j + 1) * 128].rearrange("(kc kp) n -> kp kc n", kp=128),
                          [B_wq], [B_wba[i]], B_wba[i])
                    pb0 = 4 * i
                    for kc in range(KC):
                        P.op("pe", lambda e, kc=kc, i=i, pb0=pb0: e.matmul(
                            psb[pb0][:, 0:TC], lhsT=wgr[i][:, kc, :], rhs=hTc[:, kc, :], start=(kc == 0),
                            stop=(kc == KC - 1)), [B_wgr[i], B_hTc], [B_ps[pb0]], inc=(kc == KC - 1))
                    for kc in range(KC):
                        P.op("pe", lambda e, kc=kc, i=i, pb0=pb0: e.matmul(
                            psb[pb0 + 1][:, 0:TC], lhsT=wga[i][:, kc, :], rhs=hTc[:, kc, :], start=(kc == 0),
                            stop=(kc == KC - 1)), [B_wga[i], B_hTc], [B_ps[pb0 + 1]], inc=(kc == KC - 1))
                    for kc in range(KR):
                        P.op("pe", lambda e, kc=kc, i=i, pb0=pb0: e.matmul(
                            psb[pb0 + 2][:, 0:TC], lhsT=wbr[i][:, kc, :], rhs=orc[:, kc, :], start=(kc == 0),
                            stop=(kc == KR - 1)), [B_wbr[i], B_orc], [B_ps[pb0 + 2]], inc=(kc == KR - 1))
                    for kc in range(KA):
                        P.op("pe", lambda e, kc=kc, i=i, pb0=pb0: e.matmul(
                            psb[pb0 + 3][:, 0:TC], lhsT=wba[i][:, kc, :], rhs=oac[:, kc, :], start=(kc == 0),
                            stop=(kc == KA - 1)), [B_wba[i], B_oac], [B_ps[pb0 + 3]], inc=(kc == KA - 1))
                    P.op("act", lambda e, pb0=pb0: e.activation(out=s1, in_=psb[pb0][:, 0:TC], func=AF.Sigmoid),
                         [B_ps[pb0]], [B_s1])
                    P.op("act", lambda e, pb0=pb0: e.activation(out=s2, in_=psb[pb0 + 1][:, 0:TC], func=AF.Sigmoid),
                         [B_ps[pb0 + 1]], [B_s2])
                    dv(lambda e, pb0=pb0: e.tensor_tensor(out=u1, in0=psb[pb0 + 2][:, 0:TC], in1=s1, op=ALU.mult),
                       [B_ps[pb0 + 2], B_s1], [B_u1])
                    dv(lambda e, pb0=pb0: e.tensor_tensor(out=u2, in0=psb[pb0 + 3][:, 0:TC], in1=s2, op=ALU.mult),
                       [B_ps[pb0 + 3], B_s2], [B_u2])
                    dv(lambda e, j=j: e.tensor_tensor(out=mg[:, j, :], in0=u1, in1=u2, op=ALU.add), [B_u1, B_u2], [B_mg])
                for it in range(TC // 128):
                    t0 = tcn * TC + it * 128
                    xt, B_xt, yp, B_yp, st2, B_st2 = (xts[it % 2], B_xts[it % 2], yps[it % 2], B_yps[it % 2],
                                                      st2s[it % 2], B_st2s[it % 2])
                    P.dma("sp", xt, x_d[b, t0:t0 + 128, :], [], [B_xt], B_xt)
                    for cg in range(NCG):
                        pb = (it * NCG + cg) % 2
                        for kc in range(KC):
                            P.op("pe", lambda e, kc=kc, it=it, cg=cg, pb=pb: e.matmul(
                                psb[pb][:, 0:CG], lhsT=mg[:, kc, it * 128:(it + 1) * 128],
                                rhs=wout[:, kc, cg * CG:(cg + 1) * CG], start=(kc == 0), stop=(kc == KC - 1)),
                                [B_mg, B_wout], [B_ps[pb]], inc=(kc == KC - 1))
                        dv(lambda e, cg=cg, pb=pb: e.tensor_tensor(out=yp[:, cg * CG:(cg + 1) * CG], in0=psb[pb][:, 0:CG],
                                                                   in1=xt[:, cg * CG:(cg + 1) * CG], op=ALU.add),
                           [B_ps[pb], B_xt], [B_yp])
                    P.op("act", lambda e: e.activation(out=junk, in_=yp, func=AF.Square, accum_out=st2[:, 0:1]),
                         [B_yp], [B_junk, B_st2])
                    P.op("act", lambda e: e.activation(out=st2[:, 1:2], in_=st2[:, 0:1], func=AF.Sqrt, scale=1.0 / D,
                                                       bias=NORM_EPS), [B_st2], [B_st2])
                    dv(lambda e: e.reciprocal(out=st2[:, 2:3], in_=st2[:, 1:2]), [B_st2], [B_st2])
                    dv(lambda e: e.scalar_tensor_tensor(out=yp, in0=yp, scalar=st2[:, 2:3], in1=fbc, op0=ALU.mult,
                                                        op1=ALU.mult), [B_yp, B_st2, B_fbc], [B_yp])
                    P.dma("sp", y_d[b, t0:t0 + 128, :], yp, [B_yp], [B_y], B_yp)

    stop = getattr(c, "stop", None)
    for b in range(c.BPC):
        phase_a_proj(b)
        if stop == "A":
            continue
        rwkv_phase(b)
        if stop in ("R", "R1", "R2", "P1", "P2", "P3"):
            continue
        attn_phase(b)
    if stop is None or stop == "D":
        phase_d_all()

    P.finish()
    P.emit()
    return nc


def host_consts(cfg):
    c = cfg
    T = c.T
    ident = np.eye(128, dtype=np.float32)
    j64 = np.eye(64, dtype=np.float32)[::-1].copy()
    rot = np.zeros((128, 128), np.float32)
    for i in range(64):
        rot[2 * i + 1, 2 * i] = -1.0
        rot[2 * i, 2 * i + 1] = 1.0
    rows = T // c.GRID_W
    row = np.repeat(np.arange(rows, dtype=np.float32), c.GRID_W)
    col = np.tile(np.arange(c.GRID_W, dtype=np.float32), rows)
    axis_dim = HD // 2
    freqs = (10000.0 ** (-np.arange(0, axis_dim, 2, dtype=np.float32) / axis_dim)).astype(np.float32)
    ang = np.concatenate([row[:, None] * freqs, col[:, None] * freqs], axis=-1).astype(np.float32)
    cosT = np.repeat(np.cos(ang).T, 2, axis=0)
    sinT = np.repeat(np.sin(ang).T, 2, axis=0)
    cos2 = np.concatenate([cosT, cosT], 0).astype(np.float32)
    sin2 = np.concatenate([sinT, sinT], 0).astype(np.float32)
    bd = np.zeros((128, 128), np.float32)
    bd[:64, :64] = 1.0
    bd[64:, 64:] = 1.0
    m0 = np.ones((128, T), np.float32)
    m0[:, ::CL] = 0.0
    i64 = np.arange(64)
    strict_st = (i64[None, :] > i64[:, None]).astype(np.float32)
    strict_ts = (i64[None, :] < i64[:, None]).astype(np.float32)
    incl_st = (i64[None, :] >= i64[:, None]).astype(np.float32)
    masks = np.stack([np.tile(m, (1, 8)) for m in (strict_st, strict_ts, incl_st)], axis=1)
    return dict(c_ident=ident, c_j64=j64, c_rot=rot, c_cos=cos2, c_sin=sin2, c_bdones=bd, c_m0=m0,
                c_masks=np.ascontiguousarray(masks.astype(np.float32)))


def make_in_maps(cfg, inputs, n_cores):
    c = cfg
    consts = host_consts(c)
    f = lambda a: np.ascontiguousarray(np.asarray(a, dtype=np.float32))
    NST, NHP = c.SHIFT_W // 128, c.RW // 128
    smu = f(inputs["shift_mu"][0])
    cols = [smu.reshape(2, NST, 128).transpose(2, 1, 0).reshape(128, 2 * NST)]
    per = [f(inputs["w0"][0])[0], f(inputs["w0"][0])[1], f(inputs["a0"][0])[0], f(inputs["a0"][0])[1],
           f(inputs["k_k"][0]), f(inputs["k_a"][0]), f(inputs["r_k"][0]), f(inputs["gn_w"][0]), f(inputs["gn_b"][0])]
    per = np.stack([p.reshape(NHP, 128) for p in per], axis=-1)
    cols.append(per.transpose(1, 0, 2).reshape(128, 9 * NHP))
    qg = np.tile(f(inputs["q_norm_g"][0]), 2)[:, None]
    kg = np.tile(f(inputs["k_norm_g"][0]), 2)[:, None]
    pc = np.ascontiguousarray(np.concatenate(cols + [qg, kg], axis=1).astype(np.float32))
    shared = dict(
        w_in=f(inputs["w_in"][0]), w_br=f(inputs["w_branch_rwkv"][0]), w_ba=f(inputs["w_branch_attn"][0]),
        w_out=f(inputs["w_out"][0]), norm_g=f(inputs["norm_g"][0]), final_norm_g=f(inputs["final_norm_g"]),
        pc=pc, w_up=f(inputs["w_up"][0]).reshape(2 * LORA, c.RW), a_up=f(inputs["a_up"][0]).reshape(2 * LORA, c.RW),
        **consts)
    x = np.asarray(inputs["x"], dtype=np.float32)
    maps = []
    for i in range(n_cores):
        m = dict(shared)
        m["x"] = np.ascontiguousarray(x[i * c.BPC:(i + 1) * c.BPC])
        maps.append(m)
    return maps


def kernel(**inputs):
    cfg = Cfg()
    n = 8
    nc = build(cfg)
    in_maps = make_in_maps(cfg, inputs, n)
    res = run_bass_kernel_spmd(nc, in_maps, core_ids=list(range(n)))
    return np.concatenate([r["y"] for r in res.results], axis=0)
```

```python
import math
from contextlib import ExitStack
import numpy as np
import ml_dtypes
import concourse.bass as bass
import concourse.mybir as mybir
from concourse.bass_utils import run_bass_kernel_spmd

F32 = mybir.dt.float32
BF16 = mybir.dt.bfloat16
ALU = mybir.AluOpType
AF = mybir.ActivationFunctionType
AX = mybir.AxisListType

NORM_EPS = 1e-6
GN_EPS = 64e-5
HD = 64
LORA = 64
CL = 64


class Cfg:
    def __init__(self, T=2048, D=2048, RH=16, QH=16, KVH=4, BPC=2, GRID_W=64, debug=False):
        self.T, self.D, self.RH, self.QH, self.KVH, self.BPC, self.GRID_W = T, D, RH, QH, KVH, BPC, GRID_W
        self.debug = debug
        self.RW = RH * HD
        self.AW = QH * HD
        self.KVW = KVH * HD
        self.GROUP = QH // KVH
        self.SHIFT_W = 3 * self.RW + 4 * LORA
        self.O_R, self.O_K, self.O_V = 0, self.RW, 2 * self.RW
        self.O_WD, self.O_AD = 3 * self.RW, 3 * self.RW + 2 * LORA
        self.O_ZR = self.SHIFT_W
        self.O_Q = self.O_ZR + self.RW
        self.O_AK = self.O_Q + self.AW
        self.O_AV = self.O_AK + self.KVW
        self.O_ZA = self.O_AV + self.KVW
        self.O_GR = self.O_ZA + self.AW
        self.O_GA = self.O_GR + D
        self.D_IN = self.O_GA + D
        self.KC = D // 128
        self.NCT = self.D_IN // 128
        self.TC = min(512, T)
        self.NTC = T // self.TC
        self.NT = T // 128
        self.NCH = T // CL


class Buf:
    __slots__ = ("name", "w", "r", "chan")

    def __init__(self, name):
        self.name, self.w, self.r, self.chan = name, None, {}, None


class Chan:
    __slots__ = ("sem", "cnt", "key")


class _Rec:
    def __getattr__(self, name):
        def f(*a, **k):
            self.call = (name, a, k)
            return self
        return f


class Prog:
    ENG = ("pe", "dve", "act", "pool", "sp")

    def __init__(self, nc, stack):
        self.nc, self.stack = nc, stack
        self.ops = {e: [] for e in self.ENG}
        self.sem = {e: stack.enter_context(nc.semaphore("s_" + e)) for e in self.ENG}
        self.cnt = {e: 0 for e in self.ENG}
        self.pending = {e: False for e in self.ENG}
        self.known = {e: {} for e in self.ENG}
        self.chans = {}
        self.chan_of = {}
        self.NCHAN = 16
        self.nbuf = 0

    def buf(self, name=None):
        self.nbuf += 1
        return Buf(name or "b%d" % self.nbuf)

    def _chan(self, b, dedicated=False):
        if b.chan is None and dedicated:
            c = Chan()
            c.key = "cd_" + b.name
            c.sem = self.stack.enter_context(self.nc.semaphore(c.key))
            c.cnt = 0
            self.chans[c.key] = c
            b.chan = c
        if b.chan is None:
            if b.name not in self.chan_of:
                idx = len(self.chan_of) % self.NCHAN
                key = "c_%d" % idx
                if key not in self.chans:
                    c = Chan()
                    c.key = key
                    c.sem = self.stack.enter_context(self.nc.semaphore(key))
                    c.cnt = 0
                    self.chans[key] = c
                self.chan_of[b.name] = key
            b.chan = self.chans[self.chan_of[b.name]]
        return b.chan

    def _waits(self, eng, reads, writes):
        need = {}

        def add(ev, raw):
            if ev is None:
                return
            key, val = ev
            if key == eng and eng == "pe":
                return
            if key in self.chans:
                val = self.chans[key].cnt * 16
            if need.get(key, 0) < val:
                need[key] = val

        for b in reads:
            add(b.w, True)
        for b in writes:
            add(b.w, False)
            for k, v in b.r.items():
                add((k, v), False)
        out = []
        kn = self.known[eng]
        for key, val in need.items():
            if kn.get(key, 0) >= val:
                continue
            kn[key] = val
            sem = self.chans[key].sem if key in self.chans else self.sem[key]
            out.append((sem, val))
        return out

    def _mark(self, ev, reads, writes):
        k, v = ev
        for b in reads:
            if b.r.get(k, 0) < v:
                b.r[k] = v
        for b in writes:
            b.w = ev
            b.r = {}

    def op(self, eng, fn, reads=(), writes=(), inc=True):
        rec = _Rec()
        fn(rec)
        name_, a_, k_ = rec.call
        fn = lambda e, name_=name_, a_=a_, k_=k_: getattr(e, name_)(*a_, **k_)
        waits = self._waits(eng, reads, writes)
        if inc:
            self.cnt[eng] += 1
            ev = (eng, self.cnt[eng])
            self.pending[eng] = False
            self.ops[eng].append((waits, fn, (self.sem[eng], 1)))
        else:
            ev = (eng, self.cnt[eng] + 1)
            self.pending[eng] = True
            self.ops[eng].append((waits, fn, None))
        self._mark(ev, reads, writes)

    def dma(self, q, out_ap, in_ap, reads, writes, chanbuf):
        waits = self._waits(q, reads, writes)
        c = self._chan(chanbuf, dedicated=(q == "pool"))
        c.cnt += 1
        ev = (c.key, c.cnt * 16)
        self.ops[q].append((waits, lambda e: e.dma_start(out=out_ap, in_=in_ap), (c.sem, 16)))
        self._mark(ev, reads, writes)

    def finish(self):
        for e in self.ENG:
            if self.pending[e]:
                raise RuntimeError("engine %s ends with a non-incrementing op" % e)
        waits = []
        for e in self.ENG:
            if e != "sp" and self.cnt[e] > 0:
                waits.append((self.sem[e], self.cnt[e]))
        for c in self.chans.values():
            if c.cnt:
                waits.append((c.sem, c.cnt * 16))
        self.ops["sp"].append((waits, None, None))

    def emit(self):
        nc = self.nc
        engmap = {"pe": "tensor", "dve": "vector", "act": "scalar", "pool": "gpsimd", "sp": "sync"}
        with nc.Block() as block:
            for e in self.ENG:
                ops = self.ops[e]

                def body(eng, ops=ops):
                    for waits, fn, inc in ops:
                        for sem, val in waits:
                            eng.wait_ge(sem, val)
                        if fn is None:
                            continue
                        try:
                            ins = fn(eng)
                        except Exception:
                            print("FAILED OP:", getattr(fn, "__defaults__", None))
                            raise
                        if inc is not None:
                            ins.then_inc(inc[0], inc[1])

                getattr(block, engmap[e])(body)


def rev_ap(ap):
    pat = [list(p) for p in ap.ap]
    step, n = pat[-1]
    pat[-1] = [-step, n]
    return bass.AP(tensor=ap.tensor, offset=ap.offset + step * (n - 1), ap=pat)


def build(cfg):
    c = cfg
    T, D, KC, TC, NTC, NT, RW = c.T, c.D, c.KC, c.TC, c.NTC, c.NT, c.RW
    nc = bass.Bass("TRN2", target_bir_lowering=False)
    st = ExitStack()
    P = Prog(nc, st)

    def din(name, shape, dt=F32):
        return nc.dram_tensor(name, list(shape), dt, kind="ExternalInput").ap()

    def dscr(name, shape, dt, dbg=False):
        kind = "ExternalOutput" if (dbg and c.debug) else "Internal"
        return nc.dram_tensor(name, list(shape), dt, kind=kind).ap()

    x_d = din("x", [c.BPC, T, D])
    w_in_d = din("w_in", [D, c.D_IN])
    w_br_d = din("w_br", [RW, D])
    w_ba_d = din("w_ba", [c.AW, D])
    w_out_d = din("w_out", [D, D])
    normg_d = din("norm_g", [D])
    fing_d = din("final_norm_g", [D])
    NST = c.SHIFT_W // 128
    NHP = RW // 128
    NPC = 2 * NST + 9 * NHP + 2
    pc_d = din("pc", [128, NPC])
    wup_d = din("w_up", [2 * LORA, RW])
    aup_d = din("a_up", [2 * LORA, RW])
    cmask_d = din("c_masks", [64, 3, 512])
    cident_d = din("c_ident", [128, 128])
    cj_d = din("c_j64", [64, 64])
    crot_d = din("c_rot", [128, 128])
    ccos_d = din("c_cos", [128, T])
    csin_d = din("c_sin", [128, T])
    cbd_d = din("c_bdones", [128, 128])
    cm0_d = din("c_m0", [128, T])
    y_d = nc.dram_tensor("y", [c.BPC, T, D], F32, kind="ExternalOutput").ap()

    wq_in = dscr("wq_in", [c.NCT, 128, KC, 128], BF16)
    wq_br = dscr("wq_br", [RW, D], BF16)
    wq_ba = dscr("wq_ba", [c.AW, D], BF16)
    wq_out = dscr("wq_out", [D, D], BF16)
    NPT = c.O_GR // 128
    proj_s = dscr("proj_s", [NPT, 128, T], F32)
    hT_s = dscr("hT_s", [c.BPC, D, T], BF16, dbg=True)
    orT_s = dscr("orT_s", [c.BPC, RW, T], BF16, dbg=True)
    oaT_s = dscr("oaT_s", [c.BPC, c.AW, T], BF16, dbg=True)

    def sb(name, shape, dt=F32):
        return st.enter_context(nc.sbuf_tensor(name, list(shape), dt))

    def ps(name, shape, dt=F32):
        return st.enter_context(nc.psum_tensor(name, list(shape), dt))

    B_wq = P.buf("wq")
    WQG = 12
    B_wqg = [P.buf("wqg%d" % g) for g in range((c.NCT + WQG - 1) // WQG)]
    B_hTs = [P.buf("hTs%d" % b) for b in range(c.BPC)]
    B_orTs = [P.buf("orTs%d" % b) for b in range(c.BPC)]
    B_oaTs = [P.buf("oaTs%d" % b) for b in range(c.BPC)]
    B_proj = [P.buf("proj%d" % i) for i in range(NPT)]
    B_y = P.buf("y")

    ident_f = sb("ident_f", [128, 128]); B_identf = P.buf("identf")
    ident_b = sb("ident_b", [128, 128], BF16); B_identb = P.buf("identb")
    pc = sb("pc_sb", [128, NPC]); B_pc = P.buf("pc")
    ccT = sb("ccT", [128, NST]); B_cc = P.buf("cc")
    omka = sb("omka", [128, NHP]); B_omka = P.buf("omka")
    bdones = sb("bdones", [128, 128]); B_bd = P.buf("bd")
    rotm = sb("rotm", [128, 128]); B_rot = P.buf("rot")
    masks = sb("masks", [64, 3, 512]); B_mask = P.buf("mask")
    I8 = sb("I8", [64, 512]); B_I8 = P.buf("I8")
    j64 = sb("j64", [64, 64]); B_j64 = P.buf("j64")
    P.dma("sp", ident_f[:, :], cident_d[:, :], [], [B_identf], B_identf)
    P.dma("sp", pc[:, :], pc_d[:, :], [], [B_pc], B_pc)
    P.dma("sp", bdones[:, :], cbd_d[:, :], [], [B_bd], B_bd)
    P.dma("sp", rotm[:, :], crot_d[:, :], [], [B_rot], B_rot)
    P.dma("sp", masks[:, :, :], cmask_d[:, :, :], [], [B_mask], B_mask)
    P.dma("sp", j64[:, :], cj_d[:, :], [], [B_j64], B_j64)
    P.op("dve", lambda e: e.tensor_copy(out=ident_b[:, :], in_=ident_f[:, :]), [B_identf], [B_identb])
    for g8 in range(8):
        P.op("dve", lambda e, g8=g8: e.tensor_copy(out=I8[:, g8 * 64:(g8 + 1) * 64], in_=ident_f[0:64, 0:64]),
             [B_identf], [B_I8])
    pcv = pc[:, 0:2 * NST].rearrange("p (n j) -> p n j", j=2)
    P.op("dve", lambda e: e.tensor_tensor(out=ccT[:, :], in0=pcv[:, :, 0], in1=pcv[:, :, 1], op=ALU.add),
         [B_pc], [B_cc])
    P.op("dve", lambda e: e.tensor_scalar(out=ccT[:, :], in0=ccT[:, :], scalar1=-1.0, scalar2=1.0,
                                          op0=ALU.mult, op1=ALU.add), [B_cc], [B_cc])
    PCB = 2 * NST

    def pcc(hp, j):
        return pc[:, PCB + 9 * hp + j:PCB + 9 * hp + j + 1]

    for hp in range(NHP):
        P.op("dve", lambda e, hp=hp: e.tensor_scalar(out=omka[:, hp:hp + 1], in0=pcc(hp, 5), scalar1=-1.0,
                                                     scalar2=1.0, op0=ALU.mult, op1=ALU.add), [B_pc], [B_omka])

    KG = 4
    for ct in range(c.NCT):
        for k0 in range(0, KC, KG):
            k1 = min(KC, k0 + KG)
            P.dma("pool", wq_in[ct][:, k0:k1, :],
                  w_in_d[k0 * 128:k1 * 128, ct * 128:(ct + 1) * 128].rearrange("(kc kp) c -> kp kc c", kp=128),
                  [], [B_wqg[ct // WQG]], B_wqg[ct // WQG])
    for dst_, src_, rows in ((wq_br, w_br_d, RW), (wq_ba, w_ba_d, c.AW), (wq_out, w_out_d, D)):
        for r0 in range(0, rows, 256):
            r1 = min(rows, r0 + 256)
            P.dma("pool", dst_[r0:r1, :], src_[r0:r1, :], [], [B_wq], B_wq)

    ARENA_BYTES = 196 * 1024
    arena_t = sb("arena", [128, ARENA_BYTES // 2], BF16)

    class Arena:
        def __init__(self):
            self.off, self.prev, self.live = 0, {}, []

        def reset(self):
            for b in self.live:
                evs = list(b.r.items())
                if b.w is not None:
                    evs.append(b.w)
                for k, v in evs:
                    if self.prev.get(k, 0) < v:
                        self.prev[k] = v
            self.live, self.off = [], 0

        def alloc(self, name, fshape, dt=F32, parts=128):
            n = 1
            for s in fshape:
                n *= s
            nbytes = n * (4 if dt == F32 else 2)
            nbytes = (nbytes + 63) // 64 * 64
            assert self.off + nbytes <= ARENA_BYTES, ("arena overflow", name, self.off, nbytes)
            v = arena_t[0:parts, self.off // 2:(self.off + n * (4 if dt == F32 else 2)) // 2]
            if dt == F32:
                v = v.bitcast(F32)
            if len(fshape) == 2:
                v = v.rearrange("p (a b) -> p a b", a=fshape[0])
            elif len(fshape) == 3:
                v = v.rearrange("p (a b c) -> p a b c", a=fshape[0], b=fshape[1])
            self.off += nbytes
            b = P.buf(name)
            b.r = dict(self.prev)
            self.live.append(b)
            return v, b

    AR = Arena()
    dbg_cnt = [0]

    def dbg(name, ap, bb, shape):
        if not c.debug:
            return
        t = nc.dram_tensor("dbg_" + name, list(shape), F32, kind="ExternalOutput").ap()
        P.dma("sp", t, ap, [bb], [P.buf("dbgd_" + name)], bb)

    psb = [ps("psb%d" % i, [128, 512]) for i in range(8)]
    B_ps = [P.buf("ps%d" % i) for i in range(8)]

    def phase_a_proj(b):
        AR.reset()
        hT, B_hT = AR.alloc("hT", [KC, T], BF16)
        gbc, B_gbc = AR.alloc("gbc", [D])
        P.dma("sp", gbc, normg_d.partition_broadcast(128), [], [B_gbc], B_gbc)
        xt, B_xt, xn, B_xn, st1, B_st1 = [], [], [], [], [], []
        for i in range(2):
            v, bb = AR.alloc("xt%d" % i, [D]); xt.append(v); B_xt.append(bb)
            v, bb = AR.alloc("xn%d" % i, [D], BF16); xn.append(v); B_xn.append(bb)
            v, bb = AR.alloc("st1_%d" % i, [4]); st1.append(v); B_st1.append(bb)
        junk, B_junk = AR.alloc("junk", [D], BF16)
        for it in range(NT):
            i = it % 2
            P.dma("sp", xt[i], x_d[b, it * 128:(it + 1) * 128, :], [], [B_xt[i]], B_xt[i])
            P.op("act", lambda e, i=i: e.activation(out=junk, in_=xt[i], func=AF.Square, accum_out=st1[i][:, 0:1]),
                 [B_xt[i]], [B_junk, B_st1[i]])
            P.op("act", lambda e, i=i: e.activation(out=st1[i][:, 1:2], in_=st1[i][:, 0:1], func=AF.Sqrt,
                                                     scale=1.0 / D, bias=NORM_EPS), [B_st1[i]], [B_st1[i]])
            P.op("dve", lambda e, i=i: e.reciprocal(out=st1[i][:, 2:3], in_=st1[i][:, 1:2]), [B_st1[i]], [B_st1[i]])
            P.op("dve", lambda e, i=i: e.scalar_tensor_tensor(out=xn[i], in0=xt[i], scalar=st1[i][:, 2:3], in1=gbc,
                                                               op0=ALU.mult, op1=ALU.mult),
                 [B_xt[i], B_st1[i], B_gbc], [B_xn[i]])
            for k0 in range(0, KC, 4):
                nk = min(4, KC - k0)
                pi = (k0 // 4) % 2
                pt = psb[pi][:, :].bitcast(BF16)
                for kk in range(nk):
                    P.op("pe", lambda e, i=i, kk=kk, k0=k0, pt=pt: e.transpose(
                        out=pt[:, kk * 128:(kk + 1) * 128], in_=xn[i][:, (k0 + kk) * 128:(k0 + kk + 1) * 128],
                        identity=ident_b[:, :]), [B_xn[i], B_identb], [B_ps[pi]], inc=(kk == nk - 1))
                src = pt[:, 0:nk * 128].rearrange("p (k t) -> p k t", k=nk)
                dst = hT[:, k0:k0 + nk, it * 128:(it + 1) * 128]
                if (k0 // 4) % 2 == 0:
                    P.op("act", lambda e, src=src, dst=dst: e.activation(out=dst, in_=src, func=AF.Copy),
                         [B_ps[pi]], [B_hT])
                else:
                    P.op("dve", lambda e, src=src, dst=dst: e.tensor_copy(out=dst, in_=src), [B_ps[pi]], [B_hT])
        for k0 in range(0, KC, KG):
            k1 = min(KC, k0 + KG)
            P.dma("sp", hT_s[b][k0 * 128:k1 * 128, :].rearrange("(kc kp) t -> kp kc t", kp=128), hT[:, k0:k1, :],
                  [B_hT], [B_hTs[b]], B_hT)

        wt, B_wt, pdst, B_pdst = [], [], [], []
        for i in range(2):
            v, bb = AR.alloc("wt%d" % i, [KC, 128], BF16); wt.append(v); B_wt.append(bb)
            v, bb = AR.alloc("pdst%d" % i, [T]); pdst.append(v); B_pdst.append(bb)
        ypad, B_ypad = AR.alloc("ypad", [T + 2])
        P.op("dve", lambda e: e.memset(ypad[:, 0:1], 0.0), [], [B_ypad])
        P.op("dve", lambda e: e.memset(ypad[:, T + 1:T + 2], 0.0), [], [B_ypad])
        ct_wd = c.O_WD // 128
        silu_tiles = set(range(c.O_ZR // 128, c.O_Q // 128)) | set(range(c.O_ZA // 128, c.O_GR // 128))
        for ct in range(NPT):
            i = ct % 2
            P.dma("sp", wt[i], wq_in[ct], [B_wqg[ct // WQG]], [B_wt[i]], B_wt[i])
            shifted = ct < NST
            dst = pdst[i]
            for tc in range(NTC):
                pb = (ct * NTC + tc) % 4
                for kc in range(KC):
                    P.op("pe", lambda e, i=i, kc=kc, tc=tc, pb=pb: e.matmul(
                        psb[pb][:, 0:TC], lhsT=wt[i][:, kc, :], rhs=hT[:, kc, tc * TC:(tc + 1) * TC],
                        start=(kc == 0), stop=(kc == KC - 1)), [B_wt[i], B_hT], [B_ps[pb]], inc=(kc == KC - 1))
                if shifted:
                    P.op("act", lambda e, tc=tc, pb=pb: e.activation(out=ypad[:, 1 + tc * TC:1 + (tc + 1) * TC],
                                                                     in_=psb[pb][:, 0:TC], func=AF.Copy),
                         [B_ps[pb]], [B_ypad])
                else:
                    fn = AF.Silu if ct in silu_tiles else AF.Copy
                    P.op("act", lambda e, tc=tc, pb=pb, dst=dst, fn=fn: e.activation(
                        out=dst[:, tc * TC:(tc + 1) * TC], in_=psb[pb][:, 0:TC], func=fn), [B_ps[pb]], [B_pdst[i]])
            if shifted:
                P.op("dve", lambda e, ct=ct, dst=dst: e.tensor_scalar(out=dst, in0=ypad[:, 1:T + 1],
                                                                      scalar1=ccT[:, ct:ct + 1], scalar2=None,
                                                                      op0=ALU.mult), [B_ypad, B_cc], [B_pdst[i]])
                P.op("dve", lambda e, ct=ct, dst=dst: e.scalar_tensor_tensor(
                    out=dst, in0=ypad[:, 0:T], scalar=pc[:, 2 * ct:2 * ct + 1], in1=dst, op0=ALU.mult, op1=ALU.add),
                    [B_ypad, B_pc, B_pdst[i]], [B_pdst[i]])
                P.op("dve", lambda e, ct=ct, dst=dst: e.scalar_tensor_tensor(
                    out=dst, in0=ypad[:, 2:T + 2], scalar=pc[:, 2 * ct + 1:2 * ct + 2], in1=dst, op0=ALU.mult,
                    op1=ALU.add), [B_ypad, B_pc, B_pdst[i]], [B_pdst[i]])
                if ct == ct_wd:
                    P.op("act", lambda e, dst=dst: e.activation(out=dst, in_=dst, func=AF.Tanh),
                         [B_pdst[i]], [B_pdst[i]])
            P.dma("sp", proj_s[ct], dst, [B_pdst[i]], [B_proj[ct]], B_pdst[i])

    stop = getattr(c, "stop", None)
    NCH = c.NCH
    G = 4
    NG = NCH // G

    def rwkv_phase(b):
        AR.reset()
        A = AR.alloc
        wdT, B_wd = A("wdT", [T]); adT, B_ad = A("adT", [T])
        rT, B_r = A("rT", [T]); kT, B_k = A("kT", [T]); vT, B_v = A("vT", [T]); kkT, B_kk = A("kkT", [T])
        vrT, B_vr = A("vrT", [T])
        W_, B_W = A("W_", [T]); A_, B_A = A("A_", [T]); KE_, B_KE = A("KE_", [T]); G0_, B_G0 = A("G0_", [T])
        GM_, B_GM = A("GM_", [T + 1]); CS_, B_CS = A("CS_", [T])
        oT = []; B_oT = []
        for d in range(2):
            v, bb = A("oT%d" % d, [T]); oT.append(v); B_oT.append(bb)
        m0, B_m0 = A("m0", [T]); m1, B_m1 = A("m1", [T])
        wup, B_wup = A("wup", [RW]); aup, B_aup = A("aup", [RW])
        gL, B_gL = A("gL", [NCH])
        S_, B_S = A("S_", [64]); Sg_, B_Sg = A("Sg_", [64]); Sbd, B_Sbd = A("Sbd", [128])
        BTt, B_BTt, KTt, B_KTt, Vt, B_Vt, CTm, B_CTm, DTm, B_DTm, Nm, B_Nm, BVs, B_BVs = ([] for _ in range(14))
        for i in range(2):
            for lst, bl, nm, shp in ((BTt, B_BTt, "BTt", [G, 128]), (KTt, B_KTt, "KTt", [G, 128]),
                                     (Vt, B_Vt, "Vt", [G, 128]), (CTm, B_CTm, "CTm", [512]),
                                     (DTm, B_DTm, "DTm", [512]), (Nm, B_Nm, "Nm", [512]), (BVs, B_BVs, "BVs", [512])):
                v, bb = A("%s%d" % (nm, i), shp, F32, 64); lst.append(v); bl.append(bb)
        ATm, B_ATm = A("ATm", [512], F32, 64); Am, B_Am = A("Am", [512], F32, 64)
        BTm, B_BTm = A("BTm", [512], F32, 64)
        Pq, B_Pq, PTq, B_PTq = [], [], [], []
        for i in range(2):
            v, bb = A("Pq%d" % i, [512], F32, 64); Pq.append(v); B_Pq.append(bb)
            v, bb = A("PTq%d" % i, [512], F32, 64); PTq.append(v); B_PTq.append(bb)
        RHSs, B_RHS = A("RHSs", [128], F32, 64); Us, B_Us = A("Us", [128], F32, 64)
        ob, B_ob = A("ob", [T], BF16)

        P.dma("sp", m0, cm0_d[:, :], [], [B_m0], B_m0)
        P.op("dve", lambda e: e.tensor_scalar(out=m1, in0=m0, scalar1=-1.0, scalar2=1.0, op0=ALU.mult, op1=ALU.add),
             [B_m0], [B_m1])
        P.dma("sp", wup, wup_d[:, :], [], [B_wup], B_wup)
        P.dma("sp", aup, aup_d[:, :], [], [B_aup], B_aup)
        P.dma("sp", wdT, proj_s[c.O_WD // 128], [B_proj[c.O_WD // 128]], [B_wd], B_wd)
        P.dma("sp", adT, proj_s[c.O_AD // 128], [B_proj[c.O_AD // 128]], [B_ad], B_ad)
        P.op("dve", lambda e: e.memset(GM_[:, 0:1], 1.0), [], [B_GM])

        def dv(fn, reads, writes):
            P.op("dve", fn, reads, writes)

        for hp in range(NHP):
            hs = [slice(0, 64), slice(64, 128)]
            for buf, bb, off in ((rT, B_r, c.O_R), (kT, B_k, c.O_K), (vT, B_v, c.O_V)):
                ctl = off // 128 + hp
                P.dma("sp", buf, proj_s[ctl], [B_proj[ctl]], [bb], bb)
            dv(lambda e, hp=hp: e.tensor_scalar(out=kkT, in0=kT, scalar1=pcc(hp, 4), scalar2=None, op0=ALU.mult),
               [B_k, B_pc], [B_kk])
            dv(lambda e: e.tensor_tensor(out=W_, in0=kkT, in1=kkT, op=ALU.mult), [B_kk], [B_W])
            for tc in range(NTC):
                sl = slice(tc * TC, (tc + 1) * TC)
                P.op("pe", lambda e, sl=sl: e.matmul(psb[0][:, 0:TC], lhsT=bdones[:, :], rhs=W_[:, sl], start=True,
                                                     stop=True), [B_bd, B_W], [B_ps[0]])
                dv(lambda e, sl=sl: e.tensor_scalar_max(out=A_[:, sl], in0=psb[0][:, 0:TC], scalar1=1e-24),
                   [B_ps[0]], [B_A])
            P.op("act", lambda e: e.activation(out=A_, in_=A_, func=AF.Sqrt), [B_A], [B_A])
            dv(lambda e: e.reciprocal(out=A_, in_=A_), [B_A], [B_A])
            dv(lambda e: e.tensor_tensor(out=kkT, in0=kkT, in1=A_, op=ALU.mult), [B_kk, B_A], [B_kk])
            dv(lambda e: e.tensor_copy(out=vrT, in_=rev_ap(vT)), [B_v], [B_vr])

            for d in range(2):
                dsl = slice(d * 64, (d + 1) * 64)
                rsrc = (lambda ap: ap) if d == 0 else rev_ap
                vsrc = vT if d == 0 else vrT
                B_vsrc = B_v if d == 0 else B_vr
                for tc in range(NTC):
                    sl = slice(tc * TC, (tc + 1) * TC)
                    osl = sl if d == 0 else slice(T - (tc + 1) * TC, T - tc * TC)
                    P.op("pe", lambda e, sl=sl, hp=hp, dsl=dsl: e.matmul(
                        psb[0][:, 0:TC], lhsT=wup[dsl, hp * 128:(hp + 1) * 128], rhs=wdT[dsl, sl], start=True,
                        stop=True), [B_wup, B_wd], [B_ps[0]])
                    P.op("act", lambda e, osl=osl, hp=hp, d=d: e.activation(
                        out=W_[:, osl], in_=rsrc(psb[0][:, 0:TC]), func=AF.Sigmoid, bias=pcc(hp, 0 + d), scale=1.0),
                        [B_ps[0], B_pc], [B_W])
                    P.op("pe", lambda e, sl=sl, hp=hp, dsl=dsl: e.matmul(
                        psb[1][:, 0:TC], lhsT=aup[dsl, hp * 128:(hp + 1) * 128], rhs=adT[dsl, sl], start=True,
                        stop=True), [B_aup, B_ad], [B_ps[1]])
                    P.op("act", lambda e, osl=osl, hp=hp, d=d: e.activation(
                        out=A_[:, osl], in_=rsrc(psb[1][:, 0:TC]), func=AF.Sigmoid, bias=pcc(hp, 2 + d), scale=1.0),
                        [B_ps[1], B_pc], [B_A])
                P.op("act", lambda e: e.activation(out=W_, in_=W_, func=AF.Exp, scale=-math.exp(-0.5)), [B_W], [B_W])
                if hp == 0 and b == 0:
                    dbg("w%d" % d, W_, B_W, [128, T]); dbg("a%d" % d, A_, B_A, [128, T])
                dv(lambda e, hp=hp: e.tensor_scalar(out=KE_, in0=A_, scalar1=pcc(hp, 5), scalar2=omka[:, hp:hp + 1],
                                                    op0=ALU.mult, op1=ALU.add), [B_A, B_pc, B_omka], [B_KE])
                dv(lambda e: e.tensor_tensor(out=KE_, in0=KE_, in1=rsrc(kT), op=ALU.mult), [B_KE, B_k], [B_KE])
                dv(lambda e, hp=hp: e.scalar_tensor_tensor(out=G0_, in0=KE_, scalar=pcc(hp, 6), in1=rsrc(rT),
                                                           op0=ALU.mult, op1=ALU.mult), [B_KE, B_pc, B_r], [B_G0])
                if d == 0:
                    dv(lambda e: e.tensor_copy(out=CS_, in_=G0_), [B_G0], [B_CS])
                else:
                    dv(lambda e: e.tensor_tensor(out=CS_, in0=CS_, in1=rev_ap(G0_), op=ALU.add), [B_CS, B_G0], [B_CS])
                dv(lambda e: e.tensor_tensor(out=A_, in0=A_, in1=rsrc(kkT), op=ALU.mult), [B_A, B_kk], [B_A])
                dv(lambda e: e.tensor_tensor(out=G0_, in0=W_, in1=m0, op=ALU.mult), [B_W, B_m0], [B_G0])
                dv(lambda e: e.tensor_tensor(out=W_, in0=W_, in1=G0_, op=ALU.subtract), [B_W, B_G0], [B_W])
                dv(lambda e: e.tensor_tensor_scan(out=GM_[:, 1:T + 1], data0=G0_, data1=W_, initial=0.0,
                                                  op0=ALU.mult, op1=ALU.add), [B_G0, B_W], [B_GM])
                dv(lambda e: e.tensor_tensor(out=G0_, in0=GM_[:, 0:T], in1=m0, op=ALU.mult), [B_GM, B_m0], [B_G0])
                dv(lambda e: e.tensor_tensor(out=G0_, in0=G0_, in1=m1, op=ALU.add), [B_G0, B_m1], [B_G0])
                dv(lambda e: e.scalar_tensor_tensor(out=G0_, in0=rsrc(kkT), scalar=-1.0, in1=G0_, op0=ALU.mult,
                                                    op1=ALU.mult), [B_kk, B_G0], [B_G0])
                dv(lambda e: e.tensor_tensor(out=W_, in0=rsrc(rT), in1=GM_[:, 1:T + 1], op=ALU.mult),
                   [B_r, B_GM], [B_W])
                dv(lambda e: e.tensor_copy(out=gL, in_=GM_[:, 1:T + 1].rearrange("p (n l) -> p n l", l=CL)[:, :, CL - 1]),
                   [B_GM], [B_gL])
                dv(lambda e: e.reciprocal(out=GM_[:, 1:T + 1], in_=GM_[:, 1:T + 1]), [B_GM], [B_GM])
                dv(lambda e: e.tensor_tensor(out=A_, in0=A_, in1=GM_[:, 1:T + 1], op=ALU.mult), [B_A, B_GM], [B_A])
                dv(lambda e: e.tensor_tensor(out=KE_, in0=KE_, in1=GM_[:, 1:T + 1], op=ALU.mult), [B_KE, B_GM], [B_KE])
                at, bt, kt, rt = G0_, A_, KE_, W_
                B_at, B_bt, B_kt, B_rt = B_G0, B_A, B_KE, B_W
                dv(lambda e: e.memset(S_, 0.0), [], [B_S])
                dv(lambda e: e.memset(Sbd, 0.0), [], [B_Sbd])

                def precompute(g, s):
                    t0 = g * G * CL
                    for src, B_src, dstl, B_dstl, pb in ((bt, B_bt, BTt, B_BTt, 3), (kt, B_kt, KTt, B_KTt, 4),
                                                         (vsrc, B_vsrc, Vt, B_Vt, 3)):
                        for j in range(G):
                            P.op("pe", lambda e, src=src, j=j, pb=pb: e.matmul(
                                psb[pb][0:64, j * 128:(j + 1) * 128], lhsT=src[:, t0 + j * CL:t0 + (j + 1) * CL],
                                rhs=ident_f[:, :], start=True, stop=True), [B_src, B_identf], [B_ps[pb]],
                                inc=(j == G - 1))
                        P.op("act", lambda e, dstl=dstl, pb=pb: e.activation(
                            out=dstl[s], in_=psb[pb][0:64, :].rearrange("p (g c) -> p g c", g=G), func=AF.Copy),
                            [B_ps[pb]], [B_dstl[s]])
                        yield
                    if stop == "P1":
                        return
                    specs = ((0, bt, B_bt, at, B_at), (1, at, B_at, bt, B_bt), (2, kt, B_kt, at, B_at),
                             (3, bt, B_bt, rt, B_rt), (4, kt, B_kt, rt, B_rt))
                    for h in range(2):
                        for pb, L, B_L, R, B_R in specs:
                            for j in range(G):
                                tsl = slice(t0 + j * CL, t0 + (j + 1) * CL)
                                col = (j * 2 + h) * 64
                                P.op("pe", lambda e, pb=pb, L=L, R=R, tsl=tsl, h=h, col=col: e.matmul(
                                    psb[pb][0:64, col:col + 64], lhsT=L[hs[h], tsl], rhs=R[hs[h], tsl], start=True,
                                    stop=True), [B_L, B_R], [B_ps[pb]], inc=(j == G - 1 and h == 1))
                    yield
                    dv(lambda e: e.tensor_tensor(out=ATm, in0=psb[0][0:64, :], in1=masks[:, 0, :], op=ALU.mult),
                       [B_ps[0], B_mask], [B_ATm])
                    dv(lambda e: e.tensor_tensor(out=Am, in0=psb[1][0:64, :], in1=masks[:, 1, :], op=ALU.mult),
                       [B_ps[1], B_mask], [B_Am])
                    dv(lambda e: e.tensor_tensor(out=BTm, in0=psb[2][0:64, :], in1=masks[:, 0, :], op=ALU.mult),
                       [B_ps[2], B_mask], [B_BTm])
                    dv(lambda e: e.tensor_tensor(out=CTm[s], in0=psb[3][0:64, :], in1=masks[:, 2, :], op=ALU.mult),
                       [B_ps[3], B_mask], [B_CTm[s]])
                    dv(lambda e: e.tensor_tensor(out=DTm[s], in0=psb[4][0:64, :], in1=masks[:, 2, :], op=ALU.mult),
                       [B_ps[4], B_mask], [B_DTm[s]])
                    yield
                    dv(lambda e: e.tensor_tensor(out=Nm[s], in0=ATm, in1=I8[:, :], op=ALU.add), [B_ATm, B_I8], [B_Nm[s]])
                    if stop == "P2":
                        return
                    for j in range(G):
                        for h in range(2):
                            col = (j * 2 + h) * 64
                            P.op("pe", lambda e, j=j, h=h, col=col: e.matmul(
                                psb[2][0:64, col:col + 64], lhsT=BTm[:, col:col + 64],
                                rhs=Vt[s][:, j, h * 64:(h + 1) * 64], start=True, stop=True),
                                [B_BTm, B_Vt[s]], [B_ps[2]], inc=(j == G - 1 and h == 1))
                    P.op("act", lambda e: e.activation(out=BVs[s], in_=psb[2][0:64, :], func=AF.Copy),
                         [B_ps[2]], [B_BVs[s]])
                    yield
                    if stop == "P3":
                        return
                    Pc, B_Pc, PTc, B_PTc = Am, B_Am, ATm, B_ATm
                    for lev in range(5):
                        q = lev % 2
                        last = lev == 4
                        for p8 in range(2 * G):
                            col = p8 * 64
                            P.op("pe", lambda e, Pc=Pc, PTc=PTc, col=col: e.matmul(
                                psb[0][0:64, col:col + 64], lhsT=PTc[:, col:col + 64], rhs=Pc[:, col:col + 64],
                                start=True, stop=True), [B_Pc, B_PTc], [B_ps[0]], inc=(p8 == 2 * G - 1))
                        if not last:
                            for p8 in range(2 * G):
                                col = p8 * 64
                                P.op("pe", lambda e, Pc=Pc, PTc=PTc, col=col: e.matmul(
                                    psb[1][0:64, col:col + 64], lhsT=Pc[:, col:col + 64], rhs=PTc[:, col:col + 64],
                                    start=True, stop=True), [B_Pc, B_PTc], [B_ps[1]], inc=(p8 == 2 * G - 1))
                        yield
                        dv(lambda e, q=q: e.tensor_copy(out=Pq[q], in_=psb[0][0:64, :]), [B_ps[0]], [B_Pq[q]])
                        if not last:
                            P.op("act", lambda e, q=q: e.activation(out=PTq[q], in_=psb[1][0:64, :], func=AF.Copy),
                                 [B_ps[1]], [B_PTq[q]])
                        for p8 in range(2 * G):
                            col = p8 * 64
                            P.op("pe", lambda e, q=q, col=col: e.matmul(
                                psb[4][0:64, col:col + 64], lhsT=Pq[q][:, col:col + 64], rhs=Nm[s][:, col:col + 64],
                                start=True, stop=True), [B_Pq[q], B_Nm[s]], [B_ps[4]], inc=(p8 == 2 * G - 1))
                        yield
                        dv(lambda e: e.tensor_tensor(out=Nm[s], in0=Nm[s], in1=psb[4][0:64, :], op=ALU.add),
                           [B_Nm[s], B_ps[4]], [B_Nm[s]])
                        Pc, B_Pc, PTc, B_PTc = Pq[q], B_Pq[q], PTq[q], B_PTq[q]

                def chain(g, s):
                    for j in range(G):
                        ch = g * G + j
                        tsl = slice(ch * CL, (ch + 1) * CL)
                        P.op("pe", lambda e, tsl=tsl: e.matmul(
                            psb[5][0:64, 0:128], lhsT=at[:, tsl], rhs=Sbd, start=True, stop=True),
                            [B_at, B_Sbd], [B_ps[5]])
                        yield
                        dv(lambda e, j=j: e.tensor_tensor(out=RHSs, in0=psb[5][0:64, 0:128],
                                                          in1=BVs[s][:, j * 128:(j + 1) * 128], op=ALU.add),
                           [B_ps[5], B_BVs[s]], [B_RHS])
                        for h in range(2):
                            col = (j * 2 + h) * 64
                            P.op("pe", lambda e, h=h, col=col: e.matmul(
                                psb[5][0:64, 128 + h * 64:128 + (h + 1) * 64], lhsT=Nm[s][:, col:col + 64],
                                rhs=RHSs[:, h * 64:(h + 1) * 64], start=True, stop=True),
                                [B_Nm[s], B_RHS], [B_ps[5]], inc=(h == 1))
                        yield
                        P.op("act", lambda e: e.activation(out=Us, in_=psb[5][0:64, 128:256], func=AF.Copy),
                             [B_ps[5]], [B_Us])
                        dv(lambda e, ch=ch: e.tensor_scalar(out=Sg_, in0=S_, scalar1=gL[:, ch:ch + 1], scalar2=None,
                                                            op0=ALU.mult), [B_S, B_gL], [B_Sg])
                        for h in range(2):
                            hc = slice(h * 64, (h + 1) * 64)
                            P.op("pe", lambda e, h=h, hc=hc, j=j: e.matmul(
                                psb[6][hs[h], 0:64], lhsT=KTt[s][:, j, hc], rhs=Vt[s][:, j, hc], start=True,
                                stop=False, tile_position=(0, h * 64)), [B_KTt[s], B_Vt[s]], [B_ps[6]], inc=False)
                            P.op("pe", lambda e, h=h, hc=hc, j=j: e.matmul(
                                psb[6][hs[h], 0:64], lhsT=BTt[s][:, j, hc], rhs=Us[:, hc], start=False, stop=True,
                                tile_position=(0, h * 64)), [B_BTt[s], B_Us], [B_ps[6]], inc=(h == 1))
                        oc = (ch % 8) * 64
                        for h in range(2):
                            hc = slice(h * 64, (h + 1) * 64)
                            col = (j * 2 + h) * 64
                            P.op("pe", lambda e, h=h, hc=hc, tsl=tsl, oc=oc: e.matmul(
                                psb[7][hs[h], oc:oc + 64], lhsT=Sbd[:, hc], rhs=rt[:, tsl], start=True,
                                stop=False, tile_position=(0, h * 64)), [B_Sbd, B_rt], [B_ps[7]], inc=False)
                            P.op("pe", lambda e, h=h, hc=hc, col=col, oc=oc: e.matmul(
                                psb[7][hs[h], oc:oc + 64], lhsT=Us[:, hc], rhs=CTm[s][:, col:col + 64], start=False,
                                stop=False, tile_position=(0, h * 64)), [B_Us, B_CTm[s]], [B_ps[7]], inc=False)
                            P.op("pe", lambda e, h=h, hc=hc, col=col, oc=oc, j=j: e.matmul(
                                psb[7][hs[h], oc:oc + 64], lhsT=Vt[s][:, j, hc], rhs=DTm[s][:, col:col + 64],
                                start=False, stop=True, tile_position=(0, h * 64)), [B_Vt[s], B_DTm[s]], [B_ps[7]],
                                inc=(h == 1))
                        yield
                        dv(lambda e, ch=ch: e.scalar_tensor_tensor(out=S_, in0=psb[6][:, 0:64], scalar=gL[:, ch:ch + 1],
                                                                   in1=Sg_, op0=ALU.mult, op1=ALU.add),
                           [B_ps[6], B_gL, B_Sg], [B_S])
                        for h in range(2):
                            P.op("act", lambda e, h=h: e.activation(out=Sbd[hs[h], h * 64:(h + 1) * 64], in_=S_[hs[h], :],
                                                                    func=AF.Copy), [B_S], [B_Sbd])
                        if ch % 8 == 7 or ch == NCH - 1:
                            nch8 = ch % 8 + 1
                            c0 = (ch - nch8 + 1) * CL
                            dst = oT[d][:, c0:c0 + nch8 * CL]
                            if d == 1:
                                dst = rev_ap(oT[d][:, T - c0 - nch8 * CL:T - c0])
                            P.op("act", lambda e, dst=dst, nch8=nch8: e.activation(
                                out=dst, in_=psb[7][:, 0:nch8 * CL], func=AF.Copy), [B_ps[7]], [B_oT[d]])

                if hp == 0 and b == 0 and d == 0:
                    dbg("at", at, B_at, [128, T]); dbg("bt", bt, B_bt, [128, T]); dbg("kt", kt, B_kt, [128, T])
                    dbg("rt", rt, B_rt, [128, T]); dbg("v", vT, B_v, [128, T]); dbg("gL", gL, B_gL, [128, NCH])
                if stop == "R1":
                    break
                for _ in precompute(0, 0):
                    pass
                for g in range(NG):
                    if stop in ("R2", "P1", "P2", "P3"):
                        break
                    ch_it = chain(g, g % 2)
                    pre_it = precompute(g + 1, (g + 1) % 2) if g + 1 < NG else iter(())
                    ch_done = pre_done = False
                    while not (ch_done and pre_done):
                        if not ch_done:
                            try:
                                next(ch_it)
                            except StopIteration:
                                ch_done = True
                        for _ in range(2):
                            if not pre_done:
                                try:
                                    next(pre_it)
                                except StopIteration:
                                    pre_done = True

            if stop in ("R1", "R2", "P1", "P2", "P3"):
                break
            if hp == 0 and b == 0:
                dbg("o0", oT[0], B_oT[0], [128, T]); dbg("o1", oT[1], B_oT[1], [128, T])
                dbg("kk", kkT, B_kk, [128, T]); dbg("r", rT, B_r, [128, T]); dbg("cs", CS_, B_CS, [128, T])
            zr, B_zr = W_, B_W
            ctl = c.O_ZR // 128 + hp
            P.dma("sp", zr, proj_s[ctl], [B_proj[ctl]], [B_zr], B_zr)
            wkv, B_wkv = oT[0], B_oT[0]
            dv(lambda e: e.tensor_tensor(out=wkv, in0=oT[0], in1=oT[1], op=ALU.add), [B_oT[0], B_oT[1]], [B_wkv])
            cen, B_cen = A_, B_A
            for tc in range(NTC):
                sl = slice(tc * TC, (tc + 1) * TC)
                P.op("pe", lambda e, sl=sl: e.matmul(psb[0][:, 0:TC], lhsT=bdones[:, :], rhs=wkv[:, sl], start=True,
                                                     stop=True), [B_bd, B_wkv], [B_ps[0]])
                dv(lambda e, sl=sl: e.scalar_tensor_tensor(out=cen[:, sl], in0=psb[0][:, 0:TC], scalar=-1.0 / HD,
                                                           in1=wkv[:, sl], op0=ALU.mult, op1=ALU.add),
                   [B_ps[0], B_wkv], [B_cen])
            sq, B_sq = KE_, B_KE
            dv(lambda e: e.tensor_tensor(out=sq, in0=cen, in1=cen, op=ALU.mult), [B_cen], [B_sq])
            rs, B_rs = G0_, B_G0
            for tc in range(NTC):
                sl = slice(tc * TC, (tc + 1) * TC)
                P.op("pe", lambda e, sl=sl: e.matmul(psb[1][:, 0:TC], lhsT=bdones[:, :], rhs=sq[:, sl], start=True,
                                                     stop=True), [B_bd, B_sq], [B_ps[1]])
                P.op("act", lambda e, sl=sl: e.activation(out=rs[:, sl], in_=psb[1][:, 0:TC], func=AF.Sqrt,
                                                           scale=1.0 / HD, bias=GN_EPS), [B_ps[1]], [B_rs])
            dv(lambda e: e.reciprocal(out=rs, in_=rs), [B_rs], [B_rs])
            dv(lambda e: e.tensor_tensor(out=cen, in0=cen, in1=rs, op=ALU.mult), [B_cen, B_rs], [B_cen])
            dv(lambda e, hp=hp: e.tensor_scalar(out=cen, in0=cen, scalar1=pcc(hp, 7), scalar2=pcc(hp, 8),
                                                op0=ALU.mult, op1=ALU.add), [B_cen, B_pc], [B_cen])
            for tc in range(NTC):
                sl = slice(tc * TC, (tc + 1) * TC)
                P.op("pe", lambda e, sl=sl: e.matmul(psb[0][:, 0:TC], lhsT=bdones[:, :], rhs=CS_[:, sl], start=True,
                                                     stop=True), [B_bd, B_CS], [B_ps[0]])
                dv(lambda e, sl=sl: e.tensor_tensor(out=sq[:, sl], in0=psb[0][:, 0:TC], in1=vT[:, sl], op=ALU.mult),
                   [B_ps[0], B_v], [B_sq])
            dv(lambda e: e.tensor_tensor(out=cen, in0=cen, in1=sq, op=ALU.add), [B_cen, B_sq], [B_cen])
            dv(lambda e: e.tensor_tensor(out=ob, in0=cen, in1=zr, op=ALU.mult), [B_cen, B_zr], [B_ob])
            P.dma("sp", orT_s[b][hp * 128:(hp + 1) * 128, :], ob, [B_ob], [B_orTs[b]], B_ob)

    NKP = c.KVW // 128
    NQP = c.AW // 128
    SCALE = float(HD) ** -0.5

    def attn_phase(b):
        AR.reset()
        A = AR.alloc
        cosT, B_cos = A("cosT", [T]); sinT, B_sin = A("sinT", [T])
        P.dma("sp", cosT, ccos_d[:, :], [], [B_cos], B_cos)
        P.dma("sp", sinT, csin_d[:, :], [], [B_sin], B_sin)
        src, B_src = A("asrc", [T]); t1, B_t1 = A("at1", [T]); t2, B_t2 = A("at2", [T])
        za, B_za = A("za", [T])
        nb, B_nb = A("anb", [T], BF16)
        og, B_og = A("og", [T], BF16)
        qz, B_qz = [], []
        for p in range(2):
            v, bb = A("qz%d" % p, [T], BF16); qz.append(v); B_qz.append(bb)
        kd, B_kd = [], []
        for kvh in range(c.KVH):
            v, bb = A("kd%d" % kvh, [T], BF16); kd.append(v); B_kd.append(bb)
        Va, B_Va = [], []
        for kvh in range(c.KVH):
            row, brow = [], []
            for p in range(2):
                v, bb = A("Va%d_%d" % (kvh, p), [NT, 128], BF16); row.append(v); brow.append(bb)
            Va.append(row); B_Va.append(brow)
        pT, B_pT = [], []
        for i in range(3):
            v, bb = A("pT%d" % i, [TC], BF16); pT.append(v); B_pT.append(bb)
        rc, B_rc = A("rc", [TC]); on, B_on = A("on", [TC])

        def dv(fn, reads, writes):
            P.op("dve", fn, reads, writes)

        def qk_prep(ct, gcol):
            P.dma("sp", src, proj_s[ct], [B_proj[ct]], [B_src], B_src)
            dv(lambda e: e.tensor_tensor(out=t1, in0=src, in1=src, op=ALU.mult), [B_src], [B_t1])
            for tc in range(NTC):
                sl = slice(tc * TC, (tc + 1) * TC)
                P.op("pe", lambda e, sl=sl: e.matmul(psb[5][:, 0:TC], lhsT=bdones[:, :], rhs=t1[:, sl], start=True,
                                                     stop=True), [B_bd, B_t1], [B_ps[5]])
                P.op("act", lambda e, sl=sl: e.activation(out=t2[:, sl], in_=psb[5][:, 0:TC], func=AF.Sqrt,
                                                           scale=1.0 / HD, bias=NORM_EPS), [B_ps[5]], [B_t2])
            dv(lambda e: e.reciprocal(out=t2, in_=t2), [B_t2], [B_t2])
            dv(lambda e: e.scalar_tensor_tensor(out=t2, in0=src, scalar=pc[:, gcol:gcol + 1], in1=t2, op0=ALU.mult,
                                                op1=ALU.mult), [B_src, B_pc, B_t2], [B_t2])
            for tc in range(NTC):
                sl = slice(tc * TC, (tc + 1) * TC)
                P.op("pe", lambda e, sl=sl: e.matmul(psb[6][:, 0:TC], lhsT=rotm[:, :], rhs=t2[:, sl], start=True,
                                                     stop=True), [B_rot, B_t2], [B_ps[6]])
                dv(lambda e, sl=sl: e.tensor_tensor(out=t1[:, sl], in0=psb[6][:, 0:TC], in1=sinT[:, sl], op=ALU.mult),
                   [B_ps[6], B_sin], [B_t1])
            dv(lambda e: e.tensor_tensor(out=t2, in0=t2, in1=cosT, op=ALU.mult), [B_t2, B_cos], [B_t2])
            dv(lambda e: e.tensor_tensor(out=t2, in0=t2, in1=t1, op=ALU.add), [B_t2, B_t1], [B_t2])
            dv(lambda e: e.tensor_copy(out=nb, in_=t2), [B_t2], [B_nb])

        GQ = PCB + 9 * NHP
        for kp in range(NKP):
            qk_prep(c.O_AK // 128 + kp, GQ + 1)
            for h2 in range(2):
                kvh = kp * 2 + h2
                hsl = slice(h2 * 64, (h2 + 1) * 64)
                osl = slice((1 - h2) * 64, (2 - h2) * 64)
                dv(lambda e, kvh=kvh, hsl=hsl: e.tensor_copy(out=kd[kvh][hsl, :], in_=t2[hsl, :]), [B_t2], [B_kd[kvh]])
                dv(lambda e, hsl=hsl, osl=osl: e.tensor_copy(out=t1[osl, :], in_=t2[hsl, :]), [B_t2], [B_t1])
                dv(lambda e, kvh=kvh, osl=osl: e.tensor_copy(out=kd[kvh][osl, :], in_=t1[osl, :]), [B_t1], [B_kd[kvh]])
        if stop == "T1":
            return
        for kvh in range(c.KVH):
            for p in range(2):
                dv(lambda e, kvh=kvh, p=p: e.memset(Va[kvh][p].rearrange("p a b -> p (a b)"), 1.0), [], [B_Va[kvh][p]])
        if stop == "V1":
            return
        for kp in range(NKP):
            ct = c.O_AV // 128 + kp
            P.dma("sp", src, proj_s[ct], [B_proj[ct]], [B_src], B_src)
            if stop == "V2":
                return
            for it in range(NT):
                P.op("pe", lambda e, it=it: e.matmul(psb[7][:, 0:128], lhsT=src[:, it * 128:(it + 1) * 128],
                                                     rhs=ident_f[:, :], start=True, stop=True),
                     [B_src, B_identf], [B_ps[7]])
                if stop == "V3":
                    continue
                for h2 in range(2):
                    kvh = kp * 2 + h2
                    cs_ = slice(h2 * 64, (h2 + 1) * 64)
                    dv(lambda e, kvh=kvh, it=it, cs_=cs_: e.tensor_copy(out=Va[kvh][0][:, it, 0:64], in_=psb[7][:, cs_]),
                       [B_ps[7]], [B_Va[kvh][0]])
                    dv(lambda e, kvh=kvh, it=it, cs_=cs_: e.tensor_copy(out=Va[kvh][1][:, it, 64:128], in_=psb[7][:, cs_]),
                       [B_ps[7]], [B_Va[kvh][1]])
        if stop == "T2":
            return
        cnt = 0
        for qp in range(NQP):
            qk_prep(c.O_Q // 128 + qp, GQ)
            ctz = c.O_ZA // 128 + qp
            P.dma("sp", za, proj_s[ctz], [B_proj[ctz]], [B_za], B_za)
            for p in range(2):
                dv(lambda e, p=p: e.memset(qz[p], 0.0), [], [B_qz[p]])
                dv(lambda e, p=p: e.tensor_copy(out=qz[p][p * 64:(p + 1) * 64, :], in_=nb[p * 64:(p + 1) * 64, :]),
                   [B_nb], [B_qz[p]])
            for p in range(2):
                qh = qp * 2 + p
                kvh = qh // c.GROUP
                qsl = slice(p * 64, (p + 1) * 64)
                ssl = slice((1 - p) * 64, (2 - p) * 64)
                for qc in range(NTC):
                    qs = slice(qc * TC, (qc + 1) * TC)
                    acc = 3 + (cnt % 2)
                    def emit_s(kt):
                        sb_ = (cnt * NT + kt) % 3
                        P.op("pe", lambda e, sb_=sb_, kt=kt, qs=qs, kvh=kvh, p=p: e.matmul(
                            psb[sb_][:, 0:TC], lhsT=kd[kvh][:, kt * 128:(kt + 1) * 128], rhs=qz[p][:, qs], start=True,
                            stop=True), [B_kd[kvh], B_qz[p]], [B_ps[sb_]])

                    emit_s(0)
                    for kt in range(NT):
                        sb_ = (cnt * NT + kt) % 3
                        if kt + 1 < NT:
                            emit_s(kt + 1)
                        P.op("act", lambda e, sb_=sb_: e.activation(out=pT[sb_], in_=psb[sb_][:, 0:TC], func=AF.Exp,
                                                                    scale=SCALE), [B_ps[sb_]], [B_pT[sb_]])
                        P.op("pe", lambda e, sb_=sb_, kt=kt, acc=acc, kvh=kvh: e.matmul(
                            psb[acc][:, 0:TC], lhsT=Va[kvh][p][:, kt, :], rhs=pT[sb_], start=(kt == 0),
                            stop=(kt == NT - 1)), [B_Va[kvh][p], B_pT[sb_]], [B_ps[acc]], inc=(kt == NT - 1))
                    dv(lambda e, acc=acc: e.reciprocal(out=rc[ssl, :], in_=psb[acc][ssl, 0:TC]), [B_ps[acc]], [B_rc])
                    dv(lambda e: e.tensor_copy(out=rc[qsl, :], in_=rc[ssl, :]), [B_rc], [B_rc])
                    dv(lambda e, acc=acc: e.tensor_tensor(out=on[qsl, :], in0=psb[acc][qsl, 0:TC], in1=rc[qsl, :],
                                                          op=ALU.mult), [B_ps[acc], B_rc], [B_on])
                    dv(lambda e, qs=qs: e.tensor_tensor(out=og[qsl, qs], in0=on[qsl, :], in1=za[qsl, qs], op=ALU.mult),
                       [B_on, B_za], [B_og])
                    cnt += 1
            P.dma("sp", oaT_s[b][qp * 128:(qp + 1) * 128, :], og, [B_og], [B_oaTs[b]], B_og)

    CG = min(512, D)
    NCG = D // CG
    KR = RW // 128
    KA = c.AW // 128

    def phase_d_all():
        AR.reset()
        A = AR.alloc
        wout, B_wout = A("wout", [KC, D], BF16)
        fbc, B_fbc = A("fbc", [D])
        for k0 in range(0, KC, KG):
            k1 = min(KC, k0 + KG)
            P.dma("sp", wout[:, k0:k1, :], wq_out[k0 * 128:k1 * 128, :].rearrange("(kc kp) n -> kp kc n", kp=128),
                  [B_wq], [B_wout], B_wout)
        P.dma("sp", fbc, fing_d.partition_broadcast(128), [], [B_fbc], B_fbc)
        hTc, B_hTc = A("hTc", [KC, TC], BF16)
        orc, B_orc = A("orc", [KR, TC], BF16); oac, B_oac = A("oac", [KA, TC], BF16)
        mg, B_mg = A("mg", [KC, TC], BF16)
        wgr, B_wgr, wga, B_wga, wbr, B_wbr, wba, B_wba = ([] for _ in range(8))
        for i in range(2):
            v, bb = A("wgr%d" % i, [KC, 128], BF16); wgr.append(v); B_wgr.append(bb)
            v, bb = A("wga%d" % i, [KC, 128], BF16); wga.append(v); B_wga.append(bb)
            v, bb = A("wbr%d" % i, [KR, 128], BF16); wbr.append(v); B_wbr.append(bb)
            v, bb = A("wba%d" % i, [KA, 128], BF16); wba.append(v); B_wba.append(bb)
        s1, B_s1 = A("s1", [TC]); s2, B_s2 = A("s2", [TC]); u1, B_u1 = A("u1", [TC]); u2, B_u2 = A("u2", [TC])
        xt, B_xt = A("dxt", [D]); yp, B_yp = A("yp", [D]); junk, B_junk = A("djunk", [D], BF16)
        st2, B_st2 = A("st2", [4])

        def dv(fn, reads, writes):
            P.op("dve", fn, reads, writes)

        for b in range(c.BPC):
            for tcn in range(NTC):
                ts = slice(tcn * TC, (tcn + 1) * TC)
                for dst_, src_, nk, B_s, B_d in ((hTc, hT_s, KC, B_hTs[b], B_hTc), (orc, orT_s, KR, B_orTs[b], B_orc),
                                                 (oac, oaT_s, KA, B_oaTs[b], B_oac)):
                    for k0 in range(0, nk, KG):
                        k1 = min(nk, k0 + KG)
                        P.dma("sp", dst_[:, k0:k1, :],
                              src_[b][k0 * 128:k1 * 128, ts].rearrange("(kc kp) t -> kp kc t", kp=128),
                              [B_s], [B_d], B_d)
                for j in range(KC):
                    i = j % 2
                    P.dma("sp", wgr[i], wq_in[c.O_GR // 128 + j], [B_wqg[(c.O_GR // 128 + j) // WQG]], [B_wgr[i]], B_wgr[i])
                    P.dma("sp", wga[i], wq_in[c.O_GA // 128 + j], [B_wqg[(c.O_GA // 128 + j) // WQG]], [B_wga[i]], B_wga[i])
                    P.dma("sp", wbr[i], wq_br[:, j * 128:(j + 1) * 128].rearrange("(kc kp) n -> kp kc n", kp=128),
                          [B_wq], [B_wbr[i]], B_wbr[i])
                    P.dma("sp", wba[i], wq_ba[:, j * 128:(j + 1) * 128].rearrange("(kc kp) n -> kp kc n", kp=128),
                          [B_wq], [B_wba[i]], B_wba[i])
                    pb0 = 4 * i
                    for kc in range(KC):
                        P.op("pe", lambda e, kc=kc, i=i, pb0=pb0: e.matmul(
                            psb[pb0][:, 0:TC], lhsT=wgr[i][:, kc, :], rhs=hTc[:, kc, :], start=(kc == 0),
                            stop=(kc == KC - 1)), [B_wgr[i], B_hTc], [B_ps[pb0]], inc=(kc == KC - 1))
                    for kc in range(KC):
                        P.op("pe", lambda e, kc=kc, i=i, pb0=pb0: e.matmul(
                            psb[pb0 + 1][:, 0:TC], lhsT=wga[i][:, kc, :], rhs=hTc[:, kc, :], start=(kc == 0),
                            stop=(kc == KC - 1)), [B_wga[i], B_hTc], [B_ps[pb0 + 1]], inc=(kc == KC - 1))
                    for kc in range(KR):
                        P.op("pe", lambda e, kc=kc, i=i, pb0=pb0: e.matmul(
                            psb[pb0 + 2][:, 0:TC], lhsT=wbr[i][:, kc, :], rhs=orc[:, kc, :], start=(kc == 0),
                            stop=(kc == KR - 1)), [B_wbr[i], B_orc], [B_ps[pb0 + 2]], inc=(kc == KR - 1))
                    for kc in range(KA):
                        P.op("pe", lambda e, kc=kc, i=i, pb0=pb0: e.matmul(
                            psb[pb0 + 3][:, 0:TC], lhsT=wba[i][:, kc, :], rhs=oac[:, kc, :], start=(kc == 0),
                            stop=(kc == KA - 1)), [B_wba[i], B_oac], [B_ps[pb0 + 3]], inc=(kc == KA - 1))
                    P.op("act", lambda e, pb0=pb0: e.activation(out=s1, in_=psb[pb0][:, 0:TC], func=AF.Sigmoid),
                         [B_ps[pb0]], [B_s1])
                    P.op("act", lambda e, pb0=pb0: e.activation(out=s2, in_=psb[pb0 + 1][:, 0:TC], func=AF.Sigmoid),
                         [B_ps[pb0 + 1]], [B_s2])
                    dv(lambda e, pb0=pb0: e.tensor_tensor(out=u1, in0=psb[pb0 + 2][:, 0:TC], in1=s1, op=ALU.mult),
                       [B_ps[pb0 + 2], B_s1], [B_u1])
                    dv(lambda e, pb0=pb0: e.tensor_tensor(out=u2, in0=psb[pb0 + 3][:, 0:TC], in1=s2, op=ALU.mult),
                       [B_ps[pb0 + 3], B_s2], [B_u2])
                    dv(lambda e, j=j: e.tensor_tensor(out=mg[:, j, :], in0=u1, in1=u2, op=ALU.add), [B_u1, B_u2], [B_mg])
                for it in range(TC // 128):
                    t0 = tcn * TC + it * 128
                    P.dma("sp", xt, x_d[b, t0:t0 + 128, :], [], [B_xt], B_xt)
                    for cg in range(NCG):
                        pb = (it * NCG + cg) % 4
                        for kc in range(KC):
                            P.op("pe", lambda e, kc=kc, it=it, cg=cg, pb=pb: e.matmul(
                                psb[pb][:, 0:CG], lhsT=mg[:, kc, it * 128:(it + 1) * 128],
                                rhs=wout[:, kc, cg * CG:(cg + 1) * CG], start=(kc == 0), stop=(kc == KC - 1)),
                                [B_mg, B_wout], [B_ps[pb]], inc=(kc == KC - 1))
                        dv(lambda e, cg=cg, pb=pb: e.tensor_tensor(out=yp[:, cg * CG:(cg + 1) * CG], in0=psb[pb][:, 0:CG],
                                                                   in1=xt[:, cg * CG:(cg + 1) * CG], op=ALU.add),
                           [B_ps[pb], B_xt], [B_yp])
                    P.op("act", lambda e: e.activation(out=junk, in_=yp, func=AF.Square, accum_out=st2[:, 0:1]),
                         [B_yp], [B_junk, B_st2])
                    P.op("act", lambda e: e.activation(out=st2[:, 1:2], in_=st2[:, 0:1], func=AF.Sqrt, scale=1.0 / D,
                                                       bias=NORM_EPS), [B_st2], [B_st2])
                    dv(lambda e: e.reciprocal(out=st2[:, 2:3], in_=st2[:, 1:2]), [B_st2], [B_st2])
                    dv(lambda e: e.scalar_tensor_tensor(out=yp, in0=yp, scalar=st2[:, 2:3], in1=fbc, op0=ALU.mult,
                                                        op1=ALU.mult), [B_yp, B_st2, B_fbc], [B_yp])
                    P.dma("sp", y_d[b, t0:t0 + 128, :], yp, [B_yp], [B_y], B_yp)

    stop = getattr(c, "stop", None)
    for b in range(c.BPC):
        phase_a_proj(b)
        if stop == "A":
            continue
        rwkv_phase(b)
        if stop in ("R", "R1", "R2", "P1", "P2", "P3"):
            continue
        attn_phase(b)
    if stop is None or stop == "D":
        phase_d_all()

    P.finish()
    P.emit()
    return nc


def host_consts(cfg):
    c = cfg
    T = c.T
    ident = np.eye(128, dtype=np.float32)
    j64 = np.eye(64, dtype=np.float32)[::-1].copy()
    rot = np.zeros((128, 128), np.float32)
    for i in range(64):
        rot[2 * i + 1, 2 * i] = -1.0
        rot[2 * i, 2 * i + 1] = 1.0
    rows = T // c.GRID_W
    row = np.repeat(np.arange(rows, dtype=np.float32), c.GRID_W)
    col = np.tile(np.arange(c.GRID_W, dtype=np.float32), rows)
    axis_dim = HD // 2
    freqs = (10000.0 ** (-np.arange(0, axis_dim, 2, dtype=np.float32) / axis_dim)).astype(np.float32)
    ang = np.concatenate([row[:, None] * freqs, col[:, None] * freqs], axis=-1).astype(np.float32)
    cosT = np.repeat(np.cos(ang).T, 2, axis=0)
    sinT = np.repeat(np.sin(ang).T, 2, axis=0)
    cos2 = np.concatenate([cosT, cosT], 0).astype(np.float32)
    sin2 = np.concatenate([sinT, sinT], 0).astype(np.float32)
    bd = np.zeros((128, 128), np.float32)
    bd[:64, :64] = 1.0
    bd[64:, 64:] = 1.0
    m0 = np.ones((128, T), np.float32)
    m0[:, ::CL] = 0.0
    i64 = np.arange(64)
    strict_st = (i64[None, :] > i64[:, None]).astype(np.float32)
    strict_ts = (i64[None, :] < i64[:, None]).astype(np.float32)
    incl_st = (i64[None, :] >= i64[:, None]).astype(np.float32)
    masks = np.stack([np.tile(m, (1, 8)) for m in (strict_st, strict_ts, incl_st)], axis=1)
    return dict(c_ident=ident, c_j64=j64, c_rot=rot, c_cos=cos2, c_sin=sin2, c_bdones=bd, c_m0=m0,
                c_masks=np.ascontiguousarray(masks.astype(np.float32)))


def make_in_maps(cfg, inputs, n_cores):
    c = cfg
    consts = host_consts(c)
    f = lambda a: np.ascontiguousarray(np.asarray(a, dtype=np.float32))
    NST, NHP = c.SHIFT_W // 128, c.RW // 128
    smu = f(inputs["shift_mu"][0])
    cols = [smu.reshape(2, NST, 128).transpose(2, 1, 0).reshape(128, 2 * NST)]
    per = [f(inputs["w0"][0])[0], f(inputs["w0"][0])[1], f(inputs["a0"][0])[0], f(inputs["a0"][0])[1],
           f(inputs["k_k"][0]), f(inputs["k_a"][0]), f(inputs["r_k"][0]), f(inputs["gn_w"][0]), f(inputs["gn_b"][0])]
    per = np.stack([p.reshape(NHP, 128) for p in per], axis=-1)
    cols.append(per.transpose(1, 0, 2).reshape(128, 9 * NHP))
    qg = np.tile(f(inputs["q_norm_g"][0]), 2)[:, None]
    kg = np.tile(f(inputs["k_norm_g"][0]), 2)[:, None]
    pc = np.ascontiguousarray(np.concatenate(cols + [qg, kg], axis=1).astype(np.float32))
    shared = dict(
        w_in=f(inputs["w_in"][0]), w_br=f(inputs["w_branch_rwkv"][0]), w_ba=f(inputs["w_branch_attn"][0]),
        w_out=f(inputs["w_out"][0]), norm_g=f(inputs["norm_g"][0]), final_norm_g=f(inputs["final_norm_g"]),
        pc=pc, w_up=f(inputs["w_up"][0]).reshape(2 * LORA, c.RW), a_up=f(inputs["a_up"][0]).reshape(2 * LORA, c.RW),
        **consts)
    x = np.asarray(inputs["x"], dtype=np.float32)
    maps = []
    for i in range(n_cores):
        m = dict(shared)
        m["x"] = np.ascontiguousarray(x[i * c.BPC:(i + 1) * c.BPC])
        maps.append(m)
    return maps


def kernel(**inputs):
    cfg = Cfg()
    n = 8
    nc = build(cfg)
    in_maps = make_in_maps(cfg, inputs, n)
    res = run_bass_kernel_spmd(nc, in_maps, core_ids=list(range(n)))
    return np.concatenate([r["y"] for r in res.results], axis=0)
```
